# Optimizing a Trainium2 kernel written in Bass

```python
import jax, jax.numpy as jnp
from jax import lax
import numpy as np

D_MODEL = 1024
BATCH = 32
SEQ = 2048
DEPTH = 2
DEC_BATCH = 32
DEC_SEQ = 64
PAST_LEN = 4096

CHUNK = 64
PE_DIM = 256
HEAD_DIM = 64
ATTN_WIDTH = D_MODEL // 2
N_HEADS = ATTN_WIDTH // HEAD_DIM
N_KV_HEADS = max(1, N_HEADS // 4)
GQA_GROUP = N_HEADS // N_KV_HEADS
POOL_WIDTH = D_MODEL - ATTN_WIDTH
POOL_WINDOWS = (2, 4, 8, 16)
N_POOL_GROUPS = len(POOL_WINDOWS)
POOL_GROUP = POOL_WIDTH // N_POOL_GROUPS
POOL_HIST = max(POOL_WINDOWS) - 1
MIX_WIDTH = ATTN_WIDTH + POOL_WIDTH
WINDOW = 128
WIN_CHUNKS = WINDOW // CHUNK
ROPE_DIM = HEAD_DIM // 4
ROPE_THETA = 500000.0
D_FF = ((8 * D_MODEL // 3 + 255) // 256) * 256
EPS = 1e-6
NEG = -1e30
Q_COLS = N_HEADS * HEAD_DIM
KV_COLS = N_KV_HEADS * HEAD_DIM
IN_COLS = Q_COLS + 2 * KV_COLS + POOL_WIDTH

kernel_name = "hymba_macaron_swa_sink_pool_stream_step"


def rmsnorm(x, g):
    xf = x.astype(jnp.float32)
    y = xf * lax.rsqrt(jnp.mean(xf * xf, axis=-1, keepdims=True) + EPS)
    return (y * g.astype(jnp.float32)).astype(x.dtype)


def swiglu(h, w_in, w_out):
    gate, up = jnp.split(h @ w_in, 2, axis=-1)
    return (jax.nn.silu(gate) * up) @ w_out


def rope_partial(x, pos):
    half = ROPE_DIM // 2
    inv = jnp.power(jnp.float32(ROPE_THETA), -jnp.arange(half, dtype=jnp.float32) / half)
    ang = pos.astype(jnp.float32)[:, None] * inv[None, :]
    cos = jnp.cos(ang)[:, None, :]
    sin = jnp.sin(ang)[:, None, :]
    xf = x.astype(jnp.float32)
    x1, x2, rest = xf[..., :half], xf[..., half:ROPE_DIM], xf[..., ROPE_DIM:]
    out = jnp.concatenate([x1 * cos - x2 * sin, x2 * cos + x1 * sin, rest], axis=-1)
    return out.astype(x.dtype)


def mixer_inputs(h, w_in, q_norm, k_norm, pos):
    B, T, _ = h.shape
    z = h @ w_in
    q, k, v, u = jnp.split(z, [Q_COLS, Q_COLS + KV_COLS, Q_COLS + 2 * KV_COLS], axis=-1)
    q = rope_partial(rmsnorm(q.reshape(B, T, N_HEADS, HEAD_DIM), q_norm), pos)
    k = rope_partial(rmsnorm(k.reshape(B, T, N_KV_HEADS, HEAD_DIM), k_norm), pos)
    v = v.reshape(B, T, N_KV_HEADS, HEAD_DIM)
    return q, k, v, u


def attend(q, k, v, valid, sinks):
    B, N, Lq = q.shape[:3]
    qg = q.reshape(B, N, Lq, N_KV_HEADS, GQA_GROUP, HEAD_DIM)
    s = jnp.einsum('bnqhgd,bnshd->bnhgqs', qg, k).astype(jnp.float32) * (HEAD_DIM ** -0.5)
    s = jnp.where(valid[None, :, None, None, None, :], s, NEG)
    sink = jnp.broadcast_to(sinks.astype(jnp.float32).reshape(1, 1, N_KV_HEADS, GQA_GROUP, 1, 1),
                            s.shape[:-1] + (1,))
    pr = jax.nn.softmax(jnp.concatenate([s, sink], axis=-1), axis=-1)[..., :-1]
    o = jnp.einsum('bnhgqs,bnshd->bnqhgd', pr.astype(v.dtype), v)
    return o.reshape(B, N, Lq, ATTN_WIDTH)


def window_attn_prompt(q, k, v, sinks):
    B, T = q.shape[:2]
    NC = T // CHUNK
    pad = WIN_CHUNKS * CHUNK

    def band(x):
        xp = jnp.pad(x, ((0, 0), (pad, 0), (0, 0), (0, 0)))
        xc = xp.reshape(B, NC + WIN_CHUNKS, CHUNK, N_KV_HEADS, HEAD_DIM)
        return jnp.concatenate([xc[:, j:j + NC] for j in range(WIN_CHUNKS + 1)], axis=2)

    kb, vb = band(k), band(v)
    key_pos = (jnp.arange(NC)[:, None] * CHUNK - pad
               + jnp.arange((WIN_CHUNKS + 1) * CHUNK)[None, :])
    o = attend(q.reshape(B, NC, CHUNK, N_HEADS, HEAD_DIM), kb, vb, key_pos >= 0, sinks)
    return o.reshape(B, T, ATTN_WIDTH)


def window_attn_cached(q, k_all, v_all, sinks):
    B, T = q.shape[:2]
    Lk = k_all.shape[1]
    o = attend(q[:, None], k_all[:, None], v_all[:, None], jnp.ones((1, Lk), dtype=bool), sinks)
    return o.reshape(B, T, ATTN_WIDTH)


def pool_mix(u_ext, pos, w_pool, pool_scale):
    B = u_ext.shape[0]
    T = pos.shape[0]
    uf = u_ext.astype(jnp.float32)
    cs = jnp.concatenate([jnp.zeros_like(uf[:, :1]), lax.cumsum(uf, axis=1)], axis=1)
    end = cs[:, POOL_HIST + 1:]
    means = []
    for gi, w in enumerate(POOL_WINDOWS):
        sl = slice(gi * POOL_GROUP, (gi + 1) * POOL_GROUP)
        start = cs[:, POOL_HIST + 1 - w:POOL_HIST + 1 - w + T, sl]
        cnt = jnp.minimum(pos + 1, w).astype(jnp.float32)[:, None]
        means.append((end[..., sl] - start) / cnt)
    d = (jnp.concatenate(means, axis=-1) - uf[:, POOL_HIST:]).astype(u_ext.dtype)
    d = d.reshape(B, T, N_POOL_GROUPS, POOL_GROUP)
    y = jnp.einsum('btgc,gcd->btgd', d, w_pool).reshape(B, T, POOL_WIDTH)
    return y * pool_scale


def run_layer(x, pe, pos, k_hist, v_hist, u_hist, wts, i):
    x = x + 0.5 * swiglu(rmsnorm(x, wts['norm_ffa'][i]), wts['w_ffa_in'][i], wts['w_ffa_out'][i])
    h = rmsnorm(x, wts['norm_mix'][i])
    q, k, v, u = mixer_inputs(h, wts['w_in'][i], wts['q_norm'][i], wts['k_norm'][i], pos)
    if k_hist is None:
        a = window_attn_prompt(q, k, v, wts['sinks'][i])
        k_all, v_all = k, v
        u_ext = jnp.pad(u, ((0, 0), (POOL_HIST, 0), (0, 0)))
    else:
        k_all = jnp.concatenate([k_hist, k], axis=1)
        v_all = jnp.concatenate([v_hist, v], axis=1)
        a = window_attn_cached(q, k_all, v_all, wts['sinks'][i])
        u_ext = jnp.concatenate([u_hist, u], axis=1)
    m = pool_mix(u_ext, pos, wts['w_pool'][i], wts['pool_scale'][i])
    x = x + jnp.concatenate([a, m], axis=-1) @ wts['w_out'][i]
    x = x + 0.5 * swiglu(rmsnorm(x, wts['norm_ffb'][i]), wts['w_ffb_in'][i], wts['w_ffb_out'][i])
    gate = jax.nn.sigmoid(rmsnorm(x, wts['norm_pe'][i]) @ wts['w_pe_gate'][i])
    x = x + gate * (pe @ wts['w_pe_up'][i])
    return x, k_all[:, -WINDOW:], v_all[:, -WINDOW:], u_ext[:, -POOL_HIST:]


def setup_inputs(seed: int = 0) -> dict:
    key = jax.random.key(seed)
    ks = iter(jax.random.split(key, 32))

    def nrm(shape, scale):
        return jax.random.normal(next(ks), shape, jnp.float32) * scale

    win_rows = min(WINDOW, PAST_LEN)
    return {
        "x_prompt": nrm((BATCH, SEQ, D_MODEL), 1.0),
        "x_sample": nrm((DEC_BATCH, DEC_SEQ, D_MODEL), 1.0),
        "p_prompt": nrm((DEPTH, BATCH, SEQ, PE_DIM), 1.0),
        "p_sample": nrm((DEPTH, DEC_BATCH, DEC_SEQ, PE_DIM), 1.0),
        "cache_k": nrm((DEPTH, DEC_BATCH, win_rows, N_KV_HEADS, HEAD_DIM), 1.0),
        "cache_v": nrm((DEPTH, DEC_BATCH, win_rows, N_KV_HEADS, HEAD_DIM), 1.0),
        "state_pool": nrm((DEPTH, DEC_BATCH, POOL_HIST, POOL_WIDTH), 1.0),
        "norm_ffa": 1.0 + nrm((DEPTH, D_MODEL), 0.05),
        "w_ffa_in": nrm((DEPTH, D_MODEL, 2 * D_FF), D_MODEL ** -0.5),
        "w_ffa_out": nrm((DEPTH, D_FF, D_MODEL), D_FF ** -0.5),
        "norm_mix": 1.0 + nrm((DEPTH, D_MODEL), 0.05),
        "w_in": nrm((DEPTH, D_MODEL, IN_COLS), D_MODEL ** -0.5),
        "q_norm": 1.0 + nrm((DEPTH, HEAD_DIM), 0.05),
        "k_norm": 1.0 + nrm((DEPTH, HEAD_DIM), 0.05),
        "sinks": nrm((DEPTH, N_HEADS), 1.0),
        "w_pool": nrm((DEPTH, N_POOL_GROUPS, POOL_GROUP, POOL_GROUP), POOL_GROUP ** -0.5),
        "pool_scale": 1.0 + nrm((DEPTH, POOL_WIDTH), 0.05),
        "w_out": nrm((DEPTH, MIX_WIDTH, D_MODEL), MIX_WIDTH ** -0.5),
        "norm_ffb": 1.0 + nrm((DEPTH, D_MODEL), 0.05),
        "w_ffb_in": nrm((DEPTH, D_MODEL, 2 * D_FF), D_MODEL ** -0.5),
        "w_ffb_out": nrm((DEPTH, D_FF, D_MODEL), D_FF ** -0.5),
        "norm_pe": 1.0 + nrm((DEPTH, D_MODEL), 0.05),
        "w_pe_gate": nrm((DEPTH, D_MODEL, D_MODEL), D_MODEL ** -0.5),
        "w_pe_up": nrm((DEPTH, PE_DIM, D_MODEL), PE_DIM ** -0.5),
    }


def reference(x_prompt, x_sample, p_prompt, p_sample, cache_k, cache_v, state_pool,
              norm_ffa, w_ffa_in, w_ffa_out, norm_mix, w_in, q_norm, k_norm, sinks,
              w_pool, pool_scale, w_out, norm_ffb, w_ffb_in, w_ffb_out,
              norm_pe, w_pe_gate, w_pe_up):
    wts = dict(norm_ffa=norm_ffa, w_ffa_in=w_ffa_in, w_ffa_out=w_ffa_out,
               norm_mix=norm_mix, w_in=w_in, q_norm=q_norm, k_norm=k_norm, sinks=sinks,
               w_pool=w_pool, pool_scale=pool_scale, w_out=w_out,
               norm_ffb=norm_ffb, w_ffb_in=w_ffb_in, w_ffb_out=w_ffb_out,
               norm_pe=norm_pe, w_pe_gate=w_pe_gate, w_pe_up=w_pe_up)
    pos_p = jnp.arange(x_prompt.shape[1], dtype=jnp.int32)
    pos_s = PAST_LEN + jnp.arange(x_sample.shape[1], dtype=jnp.int32)
    xp, xs = x_prompt, x_sample
    kp, vp, up, ksm, vsm, usm = [], [], [], [], [], []
    for i in range(DEPTH):
        xp, k1, v1, u1 = run_layer(xp, p_prompt[i], pos_p, None, None, None, wts, i)
        xs, k2, v2, u2 = run_layer(xs, p_sample[i], pos_s, cache_k[i], cache_v[i], state_pool[i], wts, i)
        kp.append(k1); vp.append(v1); up.append(u1)
        ksm.append(k2); vsm.append(v2); usm.append(u2)
    return (xp, xs, jnp.stack(kp), jnp.stack(vp), jnp.stack(up),
            jnp.stack(ksm), jnp.stack(vsm), jnp.stack(usm))
```

```python
import math
from contextlib import ExitStack

import numpy as np
import concourse.bass as bass
import concourse.mybir as mybir
from concourse.bass_utils import run_bass_kernel_spmd

F32 = mybir.dt.float32
BF16 = mybir.dt.bfloat16
AF = mybir.ActivationFunctionType
ALU = mybir.AluOpType

L = 2
D = 1024
KC = 8
FF = 2816
FC = 22
FH = 11
SEQ = 2048
TP = 1024
DSEQ = 64
PE_DIM = 256
EPS = 1e-6
PAST = 4096
NCORES = 8

import os
DBG_STOP = os.environ.get("KSTOP", "")


class _Stop(Exception):
    pass


def ckpt(name):
    if DBG_STOP and DBG_STOP == name:
        raise _Stop()


ENGS = ("pe", "act", "dve", "pool", "sp")
N_DMA_SLOTS = 24


class Prog:
    def __init__(self, nc):
        self.nc = nc
        self.ops = []
        self.last_w = {}
        self.readers = {}

    def op(self, eng, fn, reads=(), writes=(), dma=False):
        i = len(self.ops)
        deps = set()
        for r in reads:
            w = self.last_w.get(r)
            if w is not None:
                deps.add(w)
            if isinstance(r, tuple) and r[0] == "bank":
                for x in self.readers.get(r, ()):
                    if self.ops[x]["eng"] != eng:
                        deps.add(x)
        for r in writes:
            w = self.last_w.get(r)
            if w is not None:
                deps.add(w)
            for x in self.readers.get(r, ()):
                deps.add(x)
        self.ops.append(dict(eng=eng, fn=fn, deps=deps, dma=dma, signal=dma, sig=None, waits=None))
        for r in reads:
            self.readers.setdefault(r, []).append(i)
        for r in writes:
            self.last_w[r] = i
            self.readers[r] = []
        return i

    def finish(self, eng="pool"):
        deps = set(self.last_w.values())
        self.ops.append(dict(eng=eng, fn=None, deps=deps, dma=False, signal=False, sig=None, waits=None))

    def build(self, stack):
        nc = self.nc
        ops = self.ops

        def pe_pe(o, od):
            return od["eng"] == "pe" and o["eng"] == "pe" and not od["dma"] and not o["dma"]

        slot_last = {}
        nslot = {"sp": 0, "pool": 0, "act": 0}
        slot_rng = {"sp": (0, 14), "pool": (14, 8), "act": (22, 2)}
        for i, o in enumerate(ops):
            if o["dma"]:
                base, cntq = slot_rng[o["eng"]]
                s = base + nslot[o["eng"]] % cntq
                nslot[o["eng"]] += 1
                o["slot"] = s
                if s in slot_last:
                    o["deps"].add(slot_last[s])
                slot_last[s] = i
        for o in ops:
            for d in o["deps"]:
                if not pe_pe(o, ops[d]):
                    ops[d]["signal"] = True
        cnt = {e: 0 for e in ENGS}
        dcnt = {}
        for o in ops:
            if o["dma"]:
                s = o["slot"]
                dcnt[s] = dcnt.get(s, 0) + 1
                o["sig"] = (("dma", s), 16 * dcnt[s])
            elif o["signal"]:
                cnt[o["eng"]] += 1
                o["sig"] = (o["eng"], cnt[o["eng"]])
        known = {e: {} for e in ENGS}
        clocks = [None] * len(ops)
        for i, o in enumerate(ops):
            e = o["eng"]
            kn = known[e]
            waits = {}
            for d in sorted(o["deps"], reverse=True):
                od = ops[d]
                if pe_pe(o, od):
                    continue
                key, val = od["sig"]
                if kn.get(key, 0) >= val:
                    continue
                waits[key] = max(waits.get(key, 0), val)
                for k2, v2 in clocks[d].items():
                    if kn.get(k2, 0) < v2:
                        kn[k2] = v2
            o["waits"] = list(waits.items())
            ck = dict(kn)
            if o["sig"] is not None:
                k, v = o["sig"]
                if ck.get(k, 0) < v:
                    ck[k] = v
            clocks[i] = ck
            o["deps"] = None
        del clocks
        sems = {}
        for e in ENGS:
            sems[e] = stack.enter_context(nc.semaphore("sem_" + e))
        for s in range(N_DMA_SLOTS):
            sems[("dma", s)] = stack.enter_context(nc.semaphore("sem_dma%d" % s))
        per_eng = {e: [o for o in ops if o["eng"] == e] for e in ENGS}
        self.stats = {e: len(per_eng[e]) for e in ENGS}

        def run(ename, eh):
            for o in per_eng[ename]:
                for k, v in o["waits"]:
                    eh.wait_ge(sems[k], v)
                if o["fn"] is None:
                    continue
                ins = o["fn"](eh)
                if o["sig"] is not None:
                    k, v = o["sig"]
                    ins.then_inc(sems[k], 16 if o["dma"] else 1)

        block = stack.enter_context(nc.Block())

        @block.sync
        def _(e):
            run("sp", e)

        @block.scalar
        def _(e):
            run("act", e)

        @block.vector
        def _(e):
            run("dve", e)

        @block.gpsimd
        def _(e):
            run("pool", e)

        @block.tensor
        def _(e):
            run("pe", e)


def _rope_tables(pos):
    pos = np.asarray(pos, np.float64)
    half = 8
    inv = np.power(500000.0, -np.arange(half, dtype=np.float64) / half)
    ang = (pos.astype(np.float32)[None, :] * inv.astype(np.float32)[:, None]).astype(np.float32).astype(np.float64)
    C = np.ones((128, len(pos)), np.float64)
    S = np.zeros((128, len(pos)), np.float64)
    for p in range(128):
        d = p % 64
        if d < 16:
            C[p] = np.cos(ang[d % 8])
            S[p] = np.sin(ang[d % 8])
    return C.astype(np.float32), S.astype(np.float32)


def _consts():
    ident = np.eye(128, dtype=np.float32)
    R = np.zeros((128, 128), np.float32)
    for m in range(128):
        d = m % 64
        if d < 8:
            R[m + 8, m] = -1.0
        elif d < 16:
            R[m - 8, m] = 1.0
    invcnt = np.zeros((128, 4, 15), np.float32)
    for g, w in enumerate((2, 4, 8, 16)):
        for t in range(15):
            invcnt[:, g, t] = 1.0 / min(t + 1, w)
    negc = np.zeros((128, 3), np.float32)
    negc[0:64, 1] = -30000.0
    negc[64:128, 2] = -30000.0
    c32 = np.concatenate([ident, R, invcnt.reshape(128, 60), negc], axis=1)
    onesD = np.full((128, 128), 1.0 / 1024.0, np.float32)
    blk = np.zeros((128, 128), np.float32)
    blk[0:64, 0:64] = 1.0 / 64.0
    blk[64:128, 64:128] = 1.0 / 64.0
    onesA = np.zeros((128, 128), np.float32)
    onesA[:, 0:64] = 1.0
    onesB = np.zeros((128, 128), np.float32)
    onesB[:, 64:128] = 1.0
    mP = np.ones((128, 128), np.float32)
    mP[0:64, 64:128] = 0.0
    mO = np.ones((128, 128), np.float32)
    mO[64:128, 0:64] = 0.0
    cb = np.concatenate([onesD, blk, onesA, onesB, np.tile(mP, (1, 4)), np.tile(mO, (1, 4))], axis=1)
    cp, sp_ = _rope_tables(np.arange(SEQ))
    cs, ss = _rope_tables(PAST + np.arange(DSEQ))
    rope_p = np.stack([cp, sp_], axis=0)
    rope_s = np.stack([np.tile(cs, (1, 4)), np.tile(ss, (1, 4))], axis=0)
    return c32, cb, rope_p, rope_s


def build_program(nb_p=4, nb_s=4):
    nc = bass.Bass("TRN2", target_bir_lowering=False)

    def din(name, shape):
        return nc.dram_tensor(name, list(shape), F32, kind="ExternalInput").ap()

    def dout(name, shape):
        return nc.dram_tensor(name, list(shape), F32, kind="ExternalOutput").ap()

    NBP = max(nb_p, 1)
    NBS = max(nb_s, 1)
    xp = din("xp", [NBP, SEQ, D])
    xs = din("xs", [NBS * DSEQ, D])
    pp = din("pp", [L, NBP, SEQ, PE_DIM])
    psm = din("ps", [L, NBS * DSEQ, PE_DIM])
    ck = din("ck", [L, NBS, 128, 128])
    cv = din("cv", [L, NBS, 128, 128])
    spool = din("spool", [L, NBS, 15, 512])
    W = {}
    W["ffa_in"] = din("w_ffa_in", [L, D, 2 * FF])
    W["ffa_out"] = din("w_ffa_out", [L, FF, D])
    W["ffb_in"] = din("w_ffb_in", [L, D, 2 * FF])
    W["ffb_out"] = din("w_ffb_out", [L, FF, D])
    W["w_in"] = din("w_in", [L, D, 1280])
    W["w_out"] = din("w_out", [L, D, D])
    W["pe_gate"] = din("w_pe_gate", [L, D, D])
    W["pe_up"] = din("w_pe_up", [L, PE_DIM, D])
    W["w_pool"] = din("w_pool", [L, 4, 128, 128])
    small_d = din("small", [128, 84])
    c32_d = din("c32", [128, 319])
    cb_d = din("cb", [128, 1536])
    rope_p_d = din("rope_p", [2, 128, SEQ])
    rope_s_d = din("rope_s", [2, 128, 256])

    yp = dout("yp", [NBP, SEQ, D])
    ys = dout("ys", [NBS * DSEQ, D])
    nkp = dout("nkp", [L, NBP, 128, 128])
    nvp = dout("nvp", [L, NBP, 128, 128])
    npp = dout("npp", [L, NBP, 15, 512])
    nks = dout("nks", [L, NBS, 128, 128])
    nvs = dout("nvs", [L, NBS, 128, 128])
    nps = dout("nps", [L, NBS, 15, 512])

    tiles = {}
    tile_list = []

    def add_tile(tid, elems):
        tiles[tid] = (len(tile_list), elems)
        tile_list.append(tid)

    for l in range(L):
        for which in ("a", "b"):
            for n in range(2 * FC):
                add_tile(("ffn_in", l, which, n), KC * 128)
            for h in range(2):
                for d in range(KC):
                    add_tile(("ffn_out", l, which, h, d), FH * 128)
        for n in range(11):
            add_tile(("w_in", l, n), KC * 128)
        for d in range(KC):
            add_tile(("w_out", l, d), KC * 128)
        for d in range(KC):
            add_tile(("pe_gate", l, d), KC * 128)
        for d in range(KC):
            add_tile(("pe_up", l, d), 2 * 128)
        add_tile(("w_pool", l), 4 * 128)
    WMAX = FH * 128
    scr = nc.dram_tensor("scr", [len(tile_list), 128, WMAX], BF16).ap()

    st = ExitStack()
    with st:
        def sb(name, shape, dt):
            return st.enter_context(nc.sbuf_tensor("sb_" + name, list(shape), dt))

        T = TP
        xT = sb("xT", [128, KC, T], F32)
        hT = sb("hT", [128, KC, T], BF16)
        act = sb("act", [128, 12, T], BF16)
        kdup = sb("kdup", [128, 2, 1152], BF16)
        krot32 = sb("krot32", [128, 2, 256], F32)
        uext = sb("uext", [128, 4, 1040], F32)
        dT = sb("dT", [128, 4, T], BF16)
        vpad = sb("vpad", [128, 9, 512], BF16)
        tmpA = sb("tmpA", [128, 1040], F32)
        tmpB = sb("tmpB", [128, 1040], F32)
        peT = sb("peT", [128, 2, T], BF16)
        ropeC = sb("ropeC", [128, T], F32)
        ropeS = sb("ropeS", [128, T], F32)
        NSLOT = int(os.environ.get("KNSLOT", "10"))
        wring = sb("wring", [128, NSLOT, WMAX], BF16)
        NTMP = int(os.environ.get("KNTMP", "10"))
        tmps = [sb("tmp%d" % i, [128, 512], F32) for i in range(NTMP)]
        pbufs = [sb("pbuf%d" % i, [128, 2, 512], BF16) for i in range(3)]
        c32 = sb("c32", [128, 319], F32)
        cb = sb("cb", [128, 1536], BF16)
        small = sb("small", [128, 84], F32)
        esf = sb("esf", [128, L, 4, 128], F32)
        neghalf = sb("neghalf", [128, 8], F32)
        epscol = neghalf
        kcarry = sb("kcarry", [128, L, 2, 128], BF16)
        vcarry = sb("vcarry", [128, L, 512], BF16)
        ucarry = sb("ucarry", [128, L, 4, 15], F32)
        kvstage = sb("kvstage", [128, 2, 256], F32)
        pstage = sb("pstage", [128, 2, 512], F32)
        ostage = sb("ostage", [128, 4, 128], F32)
        pin = sb("pin", [128, 2, 256], F32)

        psum_all = st.enter_context(nc.psum_tensor("psum_all", [128, 4096], F32))
        banks = [psum_all[:, i * 512:(i + 1) * 512] for i in range(8)]

        ident = c32[:, 0:128]
        Rm = c32[:, 128:256]
        invcnt = c32[:, 256:316]
        onesD = cb[:, 0:128]
        blk1 = cb[:, 128:256]
        onesA = cb[:, 256:384]
        onesB = cb[:, 384:512]
        maskP = cb[:, 512:1024]
        maskO = cb[:, 1024:1536]

        def gcol(l, n, c):
            return small[:, (l * 4 + n) * 8 + c:(l * 4 + n) * 8 + c + 1]

        def qkg(l, which):
            return small[:, 64 + l * 2 + which:64 + l * 2 + which + 1]

        def pscale(l, g):
            return small[:, 76 + l * 4 + g:76 + l * 4 + g + 1]

        P = Prog(nc)
        state = dict(bank=0, tmp=0, pbuf=0, dmaq=0)

        NROT_OUT = int(os.environ.get("KNROT", "6"))
        state["nrot"] = NROT_OUT

        def bank():
            i = state["bank"] % state["nrot"]
            state["bank"] = (i + 1) % state["nrot"]
            return banks[i], ("bank", i)

        def bank_pair():
            i = state["bank"] % state["nrot"]
            if i % 2:
                i = (i + 1) % state["nrot"]
            state["bank"] = (i + 2) % state["nrot"]
            return i, [("bank", i), ("bank", i + 1)]

        def tmp():
            i = state["tmp"]
            state["tmp"] = (i + 1) % NTMP
            state["tmpcount"] = state.get("tmpcount", 0) + 1
            return tmps[i], ("tmp", i)

        def flags(i, n):
            return dict(start=(i == 0), stop=(i == n - 1))

        P.op("sp", lambda e: e.dma_start(out=c32[:], in_=c32_d), writes=["c32"], dma=True)
        P.op("sp", lambda e: e.dma_start(out=small[:], in_=small_d), writes=["small"], dma=True)
        P.op("pool", lambda e: e.dma_start(out=cb[:], in_=cb_d), writes=["cb"], dma=True)
        P.op("pool", lambda e: e.memset(neghalf[:], EPS), writes=["neghalf"])
        P.op("dve", lambda e: e.memset(vpad[:], 0.0), writes=[("vpad", i) for i in range(9)])
        zt, zk = tmp()
        P.op("dve", lambda e: e.memset(zt[:], 0.0), writes=[zk])
        et, ek = tmp()
        P.op("act", lambda e: e.activation(et[:, 0:8], small[:, 68:76], AF.Exp), reads=["small"], writes=[ek])
        for l in range(L):
            for j in range(4):
                P.op("dve", lambda e, l=l, j=j: e.tensor_scalar_add(esf[:, l, j, :], zt[:, 0:128],
                                                                    et[:, l * 4 + j:l * 4 + j + 1]),
                     reads=[zk, ek], writes=["esf"])

        def src_ap(tid):
            kind = tid[0]
            if kind == "ffn_in":
                _, l, which, n = tid
                w = W["ffa_in" if which == "a" else "ffb_in"]
                return [(w[l, :, n * 128:(n + 1) * 128].rearrange("(k p) c -> p k c", p=128), 0, KC, 0, 128)]
            if kind == "ffn_out":
                _, l, which, h, d = tid
                w = W["ffa_out" if which == "a" else "ffb_out"]
                return [(w[l, h * FH * 128:(h + 1) * FH * 128, d * 128:(d + 1) * 128]
                         .rearrange("(k p) c -> p k c", p=128), 0, FH, 0, 128)]
            if kind == "w_in":
                _, l, n = tid
                w = W["w_in"]
                if n < 4:
                    cols = [(n * 128, 128, 0)]
                elif n < 6:
                    c0 = 512 + (n - 4) * 64
                    cols = [(c0, 64, 0), (c0, 64, 64)]
                elif n == 6:
                    cols = [(640, 128, 0)]
                else:
                    cols = [(768 + (n - 7) * 128, 128, 0)]
                return [(w[l, :, c0:c0 + cn].rearrange("(k p) c -> p k c", p=128), 0, KC, o0, cn)
                        for (c0, cn, o0) in cols]
            if kind in ("w_out", "pe_gate"):
                _, l, d = tid
                w = W[kind]
                return [(w[l, :, d * 128:(d + 1) * 128].rearrange("(k p) c -> p k c", p=128), 0, KC, 0, 128)]
            if kind == "pe_up":
                _, l, d = tid
                w = W["pe_up"]
                return [(w[l, :, d * 128:(d + 1) * 128].rearrange("(k p) c -> p k c", p=128), 0, 2, 0, 128)]
            if kind == "w_pool":
                _, l = tid
                w = W["w_pool"]
                return [(w[l].rearrange("g p c -> p g c"), 0, 4, 0, 128)]
            raise ValueError(tid)

        def wsrc(name, l, r0, nr, c0, ncol):
            return W[name][l, r0:r0 + nr, c0:c0 + ncol].rearrange("(k p) c -> p k c", p=128)

        groups = []
        for l in range(L):
            for which in ("a", "b"):
                for n0 in range(0, 2 * FC, 4):
                    groups.append((("ffn_in", l, which, n0), 4, KC,
                                   [(wsrc("ff%s_in" % which, l, 0, D, n0 * 128, 512), 0, 512)]))
                for h in range(2):
                    for d0 in range(0, KC, 2):
                        groups.append((("ffn_out", l, which, h, d0), 2, FH,
                                       [(wsrc("ff%s_out" % which, l, h * FH * 128, FH * 128, d0 * 128, 256), 0, 256)]))
            groups.append((("w_in", l, 0), 4, KC, [(wsrc("w_in", l, 0, D, 0, 512), 0, 512)]))
            for g in range(2):
                c0 = 512 + g * 64
                groups.append((("w_in", l, 4 + g), 1, KC, [(wsrc("w_in", l, 0, D, c0, 64), 0, 64),
                                                           (wsrc("w_in", l, 0, D, c0, 64), 64, 64)]))
            groups.append((("w_in", l, 6), 1, KC, [(wsrc("w_in", l, 0, D, 640, 128), 0, 128)]))
            groups.append((("w_in", l, 7), 4, KC, [(wsrc("w_in", l, 0, D, 768, 512), 0, 512)]))
            for name in ("w_out", "pe_gate"):
                for d0 in range(0, KC, 4):
                    groups.append(((name, l, d0), 4, KC, [(wsrc(name, l, 0, D, d0 * 128, 512), 0, 512)]))
            groups.append((("pe_up", l, 0), 8, 2, [(wsrc("pe_up", l, 0, PE_DIM, 0, 1024), 0, 1024)]))
            groups.append((("w_pool", l), 1, 4, [(W["w_pool"][l].rearrange("g p c -> p g c"), 0, 128)]))
        assert sum(g_[1] for g_ in groups) == len(tile_list)
        s32 = [(xT[:, 0:4, :].rearrange("p a t -> p (a t)"), [("xT", c, tb) for c in range(0, 4) for tb in range(2)]),
               (xT[:, 4:8, :].rearrange("p a t -> p (a t)"), [("xT", c, tb) for c in range(4, 8) for tb in range(2)]),
               (uext[:, :, :].rearrange("p a t -> p (a t)"), [("uext", g) for g in range(4)])]
        s16 = [(hT[:, 0:4, :].rearrange("p a t -> p (a t)"), [("hT", c, tb) for c in range(0, 4) for tb in range(2)]),
               (hT[:, 4:8, :].rearrange("p a t -> p (a t)"), [("hT", c, tb) for c in range(4, 8) for tb in range(2)]),
               (act[:, 0:4, :].rearrange("p a t -> p (a t)"), [("act", c, tb) for c in range(0, 4) for tb in range(2)])]
        cast_engs = ("act", "act", "act")
        for gi, (tid0, ng, kct, srcs) in enumerate(groups if DBG_STOP != 'consts' else []):
            idx0 = tiles[tid0][0]
            st32f, k32 = s32[gi % 3]
            st16f, k16 = s16[gi % 3]
            gw = ng * 128
            ne = kct * gw
            v32 = st32f[:, 0:ne].rearrange("p (k c) -> p k c", c=gw)
            for (sap, o0, cn) in srcs:
                P.op("sp", lambda e, dst=v32[:, :, o0:o0 + cn], sap=sap: e.dma_start(out=dst, in_=sap),
                     writes=k32, dma=True)
            if ng == 1:
                cin = st32f[:, 0:ne]
                cout = st16f[:, 0:ne]
            else:
                cin = st32f[:, 0:ne].rearrange("p (k n c) -> p k n c", k=kct, n=ng)
                cout = st16f[:, 0:ne].rearrange("p (n k c) -> p k n c", n=ng, k=kct)
            ce = cast_engs[gi % 3]
            if ce == "act":
                P.op("act", lambda e, a=cout, b=cin: e.copy(a, b), reads=k32, writes=k16)
            else:
                P.op(ce, lambda e, a=cout, b=cin: e.tensor_copy(a, b), reads=k32, writes=k16)
            te = kct * 128
            dst = scr[idx0:idx0 + ng, :, 0:te].rearrange("n p e -> p n e")
            P.op("sp", lambda e, dst=dst, a=st16f[:, 0:ne].rearrange("p (n e) -> p n e", n=ng): e.dma_start(out=dst, in_=a),
                 reads=k16, writes=[("scr", tile_list[idx0 + j]) for j in range(ng)], dma=True)

        class WStream:
            def __init__(self):
                self.seq = []
                self.pos = 0
                self.loaded = 0

            def load_next(self):
                if self.loaded >= len(self.seq):
                    return
                tid = self.seq[self.loaded]
                s = self.loaded % NSLOT
                self.loaded += 1
                idx, elems = tiles[tid]
                P.op("sp", lambda e, s=s, idx=idx, elems=elems: e.dma_start(out=wring[:, s, 0:elems],
                                                                             in_=scr[idx, :, 0:elems]),
                     reads=[("scr", tid)], writes=[("w", s)], dma=True)

            def start(self):
                for _ in range(NSLOT):
                    self.load_next()

            def get(self, tid):
                assert self.seq[self.pos] == tid, (self.seq[self.pos], tid)
                s = self.pos % NSLOT
                self.pos += 1
                return s, ("w", s)

            def release(self):
                self.load_next()

        WS = WStream()

        def layer_tiles(l):
            out = []
            for which in ("a", "b"):
                ff = []
                for h in range(2):
                    for fi in range(FH):
                        f = h * FH + fi
                        ff.append(("ffn_in", l, which, f))
                        ff.append(("ffn_in", l, which, FC + f))
                    for d in range(KC):
                        ff.append(("ffn_out", l, which, h, d))
                if which == "a":
                    out += ff
                    out += [("w_in", l, n) for n in range(11)]
                    out.append(("w_pool", l))
                    out += [("w_out", l, d) for d in range(KC)]
                else:
                    out += ff
                    for d in range(KC):
                        out.append(("pe_gate", l, d))
                        out.append(("pe_up", l, d))
            return out

        passes = []
        for b in range(nb_p):
            for half in range(2):
                passes.append(dict(kind="p", b=b, half=half, T=TP, tbs=[(0, 512), (512, 512)]))
        if nb_s > 0:
            passes.append(dict(kind="s", T=nb_s * DSEQ, tbs=[(0, nb_s * DSEQ)]))
        for _ in passes:
            for l in range(L):
                WS.seq += layer_tiles(l)
        if DBG_STOP not in ('consts', 'prologue'):
            WS.start()

        def xk(c, tb):
            return ("xT", c, tb)

        def hk(c, tb):
            return ("hT", c, tb)

        def ak(c, tb):
            return ("act", c, tb)

        def norm_sq(l, n, ps_, tbi):
            off, nn = ps_["tbs"][tbi]
            sq = act[:, 0:8, off:off + nn]
            P.op("act", lambda e, sq=sq, off=off, nn=nn: e.activation(sq, xT[:, :, off:off + nn], AF.Square),
                 reads=[xk(c, tbi) for c in range(KC)], writes=[ak(c, tbi) for c in range(8)])

        def norm_rest(l, n, ps_, tbi):
            off, nn = ps_["tbs"][tbi]
            bk, bkk = bank()
            def mm(e, bk=bk, off=off, nn=nn):
                for c in range(KC):
                    ins = e.matmul(bk[:, 0:nn], onesD, act[:, c, off:off + nn], **flags(c, KC))
                return ins
            P.op("pe", mm, reads=[ak(c, tbi) for c in range(8)] + ["cb"], writes=[bkk])
            t1, t1k = tmp()
            P.op("act", lambda e, t1=t1, bk=bk, nn=nn: e.activation(t1[:, 0:nn], bk[:, 0:nn], AF.Ln,
                                                                    bias=epscol[:, 0:1]),
                 reads=[bkk, "neghalf"], writes=[t1k])
            t2, t2k = tmp()
            P.op("act", lambda e, t1=t1, t2=t2, nn=nn: e.activation(t2[:, 0:nn], t1[:, 0:nn], AF.Exp, scale=-0.5),
                 reads=[t1k], writes=[t2k])
            for c in range(KC):
                P.op("dve", lambda e, c=c, t2=t2, off=off, nn=nn: e.scalar_tensor_tensor(
                    hT[:, c, off:off + nn], xT[:, c, off:off + nn], gcol(l, n, c), t2[:, 0:nn], ALU.mult, ALU.mult),
                    reads=[xk(c, tbi), t2k, "small"], writes=[hk(c, tbi)])

        def norm(l, n, ps_):
            for tbi in range(len(ps_["tbs"])):
                norm_sq(l, n, ps_, tbi)
                norm_rest(l, n, ps_, tbi)

        def ffn(l, which, ps_):
            for h in range(2):
                for fi in range(FH):
                    f = h * FH + fi
                    sg_, sgk = WS.get(("ffn_in", l, which, f))
                    su_, suk = WS.get(("ffn_in", l, which, FC + f))
                    for tbi, (off, nn) in enumerate(ps_["tbs"]):
                        bg, bgk = bank()
                        bu, buk = bank()
                        def mm(e, s=sg_, bk=bg, off=off, nn=nn):
                            for k in range(KC):
                                ins = e.matmul(bk[:, 0:nn], wring[:, s, k * 128:(k + 1) * 128], hT[:, k, off:off + nn],
                                               **flags(k, KC))
                            return ins
                        P.op("pe", mm, reads=[sgk] + [hk(c, tbi) for c in range(KC)], writes=[bgk])
                        def mm2(e, s=su_, bk=bu, off=off, nn=nn):
                            for k in range(KC):
                                ins = e.matmul(bk[:, 0:nn], wring[:, s, k * 128:(k + 1) * 128], hT[:, k, off:off + nn],
                                               **flags(k, KC))
                            return ins
                        P.op("pe", mm2, reads=[suk] + [hk(c, tbi) for c in range(KC)], writes=[buk])
                        t1, t1k = tmp()
                        P.op("act", lambda e, t1=t1, bg=bg, nn=nn: e.activation(t1[:, 0:nn], bg[:, 0:nn], AF.Silu),
                             reads=[bgk], writes=[t1k])
                        P.op("dve", lambda e, t1=t1, bu=bu, fi=fi, off=off, nn=nn: e.tensor_tensor(
                            act[:, fi, off:off + nn], t1[:, 0:nn], bu[:, 0:nn], ALU.mult),
                            reads=[t1k, buk], writes=[ak(fi, tbi)])
                    WS.release()
                    WS.release()
                for d in range(KC):
                    so_, sok = WS.get(("ffn_out", l, which, h, d))
                    for tbi, (off, nn) in enumerate(ps_["tbs"]):
                        by, byk = bank()
                        def mm(e, s=so_, bk=by, off=off, nn=nn):
                            for fi in range(FH):
                                ins = e.matmul(bk[:, 0:nn], wring[:, s, fi * 128:(fi + 1) * 128],
                                               act[:, fi, off:off + nn], **flags(fi, FH))
                            return ins
                        P.op("pe", mm, reads=[sok] + [ak(fi, tbi) for fi in range(FH)], writes=[byk])
                        P.op("dve", lambda e, by=by, d=d, off=off, nn=nn: e.scalar_tensor_tensor(
                            xT[:, d, off:off + nn], by[:, 0:nn], 0.5, xT[:, d, off:off + nn], ALU.mult, ALU.add),
                            reads=[byk, xk(d, tbi)], writes=[xk(d, tbi)])
                    WS.release()

        def ffn_pipe(l, which, ps_, hook_mid, it_hook=None):
            tbs = ps_["tbs"]

            def p1_block(fi, tbi, sg_, sgk, su_, suk):
                off, nn = tbs[tbi]
                bg, bgk = bank()
                bu, buk = bank()
                def mm(e, s=sg_, bk=bg, off=off, nn=nn):
                    for k in range(KC):
                        ins = e.matmul(bk[:, 0:nn], wring[:, s, k * 128:(k + 1) * 128], hT[:, k, off:off + nn],
                                       **flags(k, KC))
                    return ins
                P.op("pe", mm, reads=[sgk] + [hk(c, tbi) for c in range(KC)], writes=[bgk])
                def mm2(e, s=su_, bk=bu, off=off, nn=nn):
                    for k in range(KC):
                        ins = e.matmul(bk[:, 0:nn], wring[:, s, k * 128:(k + 1) * 128], hT[:, k, off:off + nn],
                                       **flags(k, KC))
                    return ins
                P.op("pe", mm2, reads=[suk] + [hk(c, tbi) for c in range(KC)], writes=[buk])
                t1, t1k = tmp()
                P.op("act", lambda e, t1=t1, bg=bg, nn=nn: e.activation(t1[:, 0:nn], bg[:, 0:nn], AF.Silu),
                     reads=[bgk], writes=[t1k])
                P.op("dve", lambda e, t1=t1, bu=bu, fi=fi, off=off, nn=nn: e.tensor_tensor(
                    act[:, fi, off:off + nn], t1[:, 0:nn], bu[:, 0:nn], ALU.mult),
                    reads=[t1k, buk], writes=[ak(fi, tbi)])

            def p2_block(d, tbi, so_, sok):
                off, nn = tbs[tbi]
                by, byk = bank()
                def mm(e, s=so_, bk=by, off=off, nn=nn):
                    for fi in range(FH):
                        ins = e.matmul(bk[:, 0:nn], wring[:, s, fi * 128:(fi + 1) * 128],
                                       act[:, fi, off:off + nn], **flags(fi, FH))
                    return ins
                P.op("pe", mm, reads=[sok] + [ak(fi, tbi) for fi in range(FH)], writes=[byk])
                P.op("dve", lambda e, by=by, d=d, off=off, nn=nn: e.scalar_tensor_tensor(
                    xT[:, d, off:off + nn], by[:, 0:nn], 0.5, xT[:, d, off:off + nn], ALU.mult, ALU.add),
                    reads=[byk, xk(d, tbi)], writes=[xk(d, tbi)])

            NH = 4
            itc = [0]
            held = []
            for fi in range(NH):
                g_ = WS.get(("ffn_in", l, which, fi))
                u_ = WS.get(("ffn_in", l, which, FC + fi))
                held.append(g_ + u_)
            for fi in range(NH):
                p1_block(fi, 0, *held[fi])
            hook_mid()
            for fi in range(NH):
                p1_block(fi, 1, *held[fi])
                WS.release()
                WS.release()
            for h in range(2):
                for fi in range(NH if h == 0 else 0, FH):
                    f = h * FH + fi
                    g_ = WS.get(("ffn_in", l, which, f))
                    u_ = WS.get(("ffn_in", l, which, FC + f))
                    for tbi in range(2):
                        p1_block(fi, tbi, *(g_ + u_))
                    WS.release()
                    WS.release()
                    if it_hook is not None:
                        it_hook(itc[0])
                        itc[0] += 1
                if h == 0:
                    for d in range(KC):
                        o_ = WS.get(("ffn_out", l, which, 0, d))
                        for tbi in range(2):
                            p2_block(d, tbi, *o_)
                        WS.release()
            slots = [WS.get(("ffn_out", l, which, 1, d)) for d in range(KC)]
            tail0 = [(lambda d=d: p2_block(d, 0, *slots[d])) for d in range(KC)]
            def t1f(d):
                p2_block(d, 1, *slots[d])
                WS.release()
            tail1 = [(lambda d=d: t1f(d)) for d in range(KC)]
            return tail0, tail1

        def gate_pipe(l, ps_, hook_mid):
            tbs = ps_["tbs"]

            def block(d, tbi, sg_, sgk, su_, suk):
                off, nn = tbs[tbi]
                bg, bgk = bank()
                bu, buk = bank()
                def mm(e, s=sg_, bk=bg, off=off, nn=nn):
                    for k in range(KC):
                        ins = e.matmul(bk[:, 0:nn], wring[:, s, k * 128:(k + 1) * 128], hT[:, k, off:off + nn],
                                       **flags(k, KC))
                    return ins
                P.op("pe", mm, reads=[sgk] + [hk(c, tbi) for c in range(KC)], writes=[bgk])
                def mm2(e, s=su_, bk=bu, off=off, nn=nn):
                    for k in range(2):
                        ins = e.matmul(bk[:, 0:nn], wring[:, s, k * 128:(k + 1) * 128], peT[:, k, off:off + nn],
                                       **flags(k, 2))
                    return ins
                tts = list(range(off // 128, (off + nn + 127) // 128))
                P.op("pe", mm2, reads=[suk] + [("peT", tt) for tt in tts], writes=[buk])
                t1, t1k = tmp()
                P.op("act", lambda e, t1=t1, bg=bg, nn=nn: e.activation(t1[:, 0:nn], bg[:, 0:nn], AF.Tanh, scale=0.5),
                     reads=[bgk], writes=[t1k])
                t2, t2k = tmp()
                P.op("dve", lambda e, t1=t1, t2=t2, bu=bu, nn=nn: e.scalar_tensor_tensor(
                    t2[:, 0:nn], t1[:, 0:nn], 1.0, bu[:, 0:nn], ALU.add, ALU.mult), reads=[t1k, buk], writes=[t2k])
                P.op("dve", lambda e, t2=t2, d=d, off=off, nn=nn: e.scalar_tensor_tensor(
                    xT[:, d, off:off + nn], t2[:, 0:nn], 0.5, xT[:, d, off:off + nn], ALU.mult, ALU.add),
                    reads=[t2k, xk(d, tbi)], writes=[xk(d, tbi)])

            held = []
            for d in range(4):
                g_ = WS.get(("pe_gate", l, d))
                u_ = WS.get(("pe_up", l, d))
                held.append(g_ + u_)
            for d in range(4):
                block(d, 0, *held[d])
            hook_mid()
            for d in range(4):
                block(d, 1, *held[d])
                WS.release()
                WS.release()
            held2 = []
            for d in range(4, 8):
                g_ = WS.get(("pe_gate", l, d))
                u_ = WS.get(("pe_up", l, d))
                held2.append(g_ + u_)
            tail0 = [(lambda d=d: block(d, 0, *held2[d - 4])) for d in range(4, 8)]
            def t1f(d):
                block(d, 1, *held2[d - 4])
                WS.release()
                WS.release()
            tail1 = [(lambda d=d: t1f(d)) for d in range(4, 8)]
            return tail0, tail1

        def load_x(ps_):
            for tt in range(ps_["T"] // 128):
                load_tile(ps_, tt)

        def load_tile(ps_, tt):
            load_dma(ps_, tt)
            load_tr(ps_, tt)

        def load_dma(ps_, tt):
            sl = tt % 2
            xin = uext[:, sl, 0:1024]
            if ps_["kind"] == "p":
                src = xp[ps_["b"], ps_["half"] * TP + tt * 128: ps_["half"] * TP + (tt + 1) * 128, :]
            else:
                src = xs[tt * 128:(tt + 1) * 128, :]
            P.op("pool", lambda e, xin=xin, src=src: e.dma_start(out=xin, in_=src), writes=[("uext", sl)], dma=True)

        def load_tr(ps_, tt):
            if True:
                sl = tt % 2
                xin = uext[:, sl, 0:1024]
                tbi = (tt * 128) // 512 if ps_["kind"] == "p" else 0
                for hb in range(2):
                    bk, bkk = bank()
                    def mm(e, bk=bk, xin=xin, hb=hb):
                        for c4 in range(4):
                            c = hb * 4 + c4
                            ins = e.transpose(bk[:, c4 * 128:(c4 + 1) * 128], xin[:, c * 128:(c + 1) * 128], ident)
                        return ins
                    P.op("pe", mm, reads=[("uext", sl), "c32"], writes=[bkk])
                    P.op("act", lambda e, bk=bk, hb=hb, tt=tt: e.copy(
                        xT[:, hb * 4:hb * 4 + 4, tt * 128:(tt + 1) * 128], bk[:].rearrange("p (c t) -> p c t", t=128)),
                        reads=[bkk], writes=[xk(hb * 4 + c4, tbi) for c4 in range(4)])

        def store_y(ps_):
            for tt in range(ps_["T"] // 128):
                store_tile(ps_, tt)

        def store_tile(ps_, tt):
            if True:
                sl = 2 + tt % 2
                yst = uext[:, sl, 0:1024]
                tbi = (tt * 128) // 512 if ps_["kind"] == "p" else 0
                for hb in range(2):
                    bk, bkk = bank()
                    def mm(e, bk=bk, hb=hb, tt=tt):
                        for c4 in range(4):
                            c = hb * 4 + c4
                            ins = e.transpose(bk[:, c4 * 128:(c4 + 1) * 128], xT[:, c, tt * 128:(tt + 1) * 128], ident)
                        return ins
                    P.op("pe", mm, reads=[xk(hb * 4 + c4, tbi) for c4 in range(4)] + ["c32"], writes=[bkk])
                    eng = "act" if hb == 0 else "dve"
                    if eng == "act":
                        P.op("act", lambda e, bk=bk, hb=hb, yst=yst: e.copy(yst[:, hb * 512:(hb + 1) * 512], bk[:]),
                             reads=[bkk], writes=[("uext", sl)])
                    else:
                        P.op("dve", lambda e, bk=bk, hb=hb, yst=yst: e.tensor_copy(yst[:, hb * 512:(hb + 1) * 512], bk[:]),
                             reads=[bkk, ("uext", sl)], writes=[("uext", sl)])
                if ps_["kind"] == "p":
                    dst = yp[ps_["b"], ps_["half"] * TP + tt * 128: ps_["half"] * TP + (tt + 1) * 128, :]
                else:
                    dst = ys[tt * 128:(tt + 1) * 128, :]
                P.op("pool", lambda e, yst=yst, dst=dst: e.dma_start(out=dst, in_=yst), reads=[("uext", sl)],
                     writes=[("out", "y", id(ps_), tt)], dma=True)

        def pe_dma(l, ps_, tt):
            sl = tt % 2
            if ps_["kind"] == "p":
                src = pp[l, ps_["b"], ps_["half"] * TP + tt * 128: ps_["half"] * TP + (tt + 1) * 128, :]
            else:
                src = psm[l, tt * 128:(tt + 1) * 128, :]
            P.op("pool", lambda e, sl=sl, src=src: e.dma_start(out=pin[:, sl, :], in_=src), writes=[("pin", sl)],
                 dma=True)

        def pe_tr(l, ps_, tt):
            sl = tt % 2
            bk, bkk = bank()
            def mm(e, bk=bk, sl=sl):
                for j in range(2):
                    ins = e.transpose(bk[:, j * 128:(j + 1) * 128], pin[:, sl, j * 128:(j + 1) * 128], ident)
                return ins
            P.op("pe", mm, reads=[("pin", sl), "c32"], writes=[bkk])
            P.op("act", lambda e, bk=bk, tt=tt: e.copy(peT[:, :, tt * 128:(tt + 1) * 128],
                                                       bk[:, 0:256].rearrange("p (c t) -> p c t", t=128)),
                 reads=[bkk], writes=[("peT", tt)])

        def load_pe(l, ps_):
            for tt in range(ps_["T"] // 128):
                pe_dma(l, ps_, tt)
                pe_tr(l, ps_, tt)

        def segs_of(ps_):
            if ps_["kind"] == "p":
                return [dict(b=ps_["b"], t0=0, Ls=TP, kbase=0, vbase=0, ubase=0,
                             hist=("none" if ps_["half"] == 0 else "carry"))]
            return [dict(b=s, t0=s * DSEQ, Ls=DSEQ, kbase=s * 192, vbase=2 * s, ubase=s * 79, hist="cache")
                    for s in range(nb_s)]

        def kd_keys_own(ps_, g, tbi):
            if ps_["kind"] == "p":
                return [("kdup", g, 1 + tbi * 4 + i) for i in range(4)]
            return [("kdup", g, 2 * s + 1) for s in range(nb_s)]

        def mixer(l, ps_, hook_mid=None, defer_wout=False):
            T_ = ps_["T"]
            isp = ps_["kind"] == "p"
            segs = segs_of(ps_)
            S = len(segs)
            Ls = segs[0]["Ls"]
            UW = 15 + Ls
            for s, sg in enumerate(segs):
                ub = sg["ubase"]
                if sg["hist"] == "none":
                    P.op("pool", lambda e, ub=ub: e.memset(uext[:, :, ub:ub + 15], 0.0),
                         writes=[("uext", g) for g in range(4)])
                elif sg["hist"] == "carry":
                    P.op("pool", lambda e: e.tensor_copy(kdup[:, :, 0:128], kcarry[:, l, :, :]),
                         reads=[("kcarry", l)], writes=[("kdup", 0, 0), ("kdup", 1, 0)])
                    P.op("pool", lambda e: e.tensor_copy(vpad[:, 0, :], vcarry[:, l, :]),
                         reads=[("vcarry", l)], writes=[("vpad", 0)])
                    P.op("pool", lambda e, ub=ub: e.tensor_copy(uext[:, :, ub:ub + 15], ucarry[:, l, :, :]),
                         reads=[("ucarry", l)], writes=[("uext", g) for g in range(4)])
                else:
                    b = sg["b"]
                    kb = sg["kbase"]
                    vb = sg["vbase"]
                    srck = ck[l, b].rearrange("r (g d) -> r g d", g=2)
                    P.op("pool", lambda e, srck=srck: e.dma_start(
                        out=kvstage[:, 0, :].rearrange("r (g c) -> r g c", g=2)[:, :, 0:64], in_=srck),
                        writes=["kvs0"], dma=True)
                    P.op("pool", lambda e, srck=srck: e.dma_start(
                        out=kvstage[:, 0, :].rearrange("r (g c) -> r g c", g=2)[:, :, 64:128], in_=srck),
                        writes=["kvs0b"], dma=True)
                    bk, bkk = bank()
                    def mm(e, bk=bk):
                        for g in range(2):
                            ins = e.transpose(bk[:, g * 128:(g + 1) * 128], kvstage[:, 0, g * 128:(g + 1) * 128], ident)
                        return ins
                    P.op("pe", mm, reads=["kvs0", "kvs0b", "c32"], writes=[bkk])
                    P.op("act", lambda e, bk=bk, kb=kb: e.copy(kdup[:, :, kb:kb + 128],
                                                               bk[:, 0:256].rearrange("p (g t) -> p g t", g=2)),
                         reads=[bkk], writes=[("kdup", 0, 2 * s), ("kdup", 1, 2 * s)])
                    P.op("pool", lambda e, b=b: e.dma_start(out=kvstage[:, 1, 0:128], in_=cv[l, b]),
                         writes=["kvs1"], dma=True)
                    vin = kvstage[:, 1, 0:128].rearrange("r (g d) -> r g d", g=2)
                    P.op("act", lambda e, vb=vb, vin=vin: e.copy(
                        vpad[:, vb, :].rearrange("r (g c) -> r g c", g=2)[:, :, 0:64], vin),
                        reads=["kvs1"], writes=[("vpad", vb)])
                    P.op("act", lambda e, vb=vb, vin=vin: e.copy(
                        vpad[:, vb, :].rearrange("r (g c) -> r g c", g=2)[:, :, 192:256], vin),
                        reads=["kvs1", ("vpad", vb)], writes=[("vpad", vb)])
                    P.op("pool", lambda e, b=b: e.dma_start(out=pstage[0:15, 0, :], in_=spool[l, b]),
                         writes=["pst0"], dma=True)
                    bk2, bk2k = bank()
                    def mm2(e, bk2=bk2):
                        for g in range(4):
                            ins = e.transpose(bk2[:, g * 16:g * 16 + 15], pstage[0:15, 0, g * 128:(g + 1) * 128],
                                              ident[0:15, 0:15])
                        return ins
                    P.op("pe", mm2, reads=["pst0", "c32"], writes=[bk2k])
                    P.op("act", lambda e, bk2=bk2, ub=ub: e.copy(
                        uext[:, :, ub:ub + 15], bk2[:, 0:64].rearrange("p (g t) -> p g t", g=4)[:, :, 0:15]),
                        reads=[bk2k], writes=[("uext", g) for g in range(4)])
                    P.op("pool", lambda e, b=b: e.dma_start(out=nks[l, b, 0:64, :], in_=ck[l, b, 64:128, :]),
                         writes=[("out", "nks0", l, b)], dma=True)
                    P.op("pool", lambda e, b=b: e.dma_start(out=nvs[l, b, 0:64, :], in_=cv[l, b, 64:128, :]),
                         writes=[("out", "nvs0", l, b)], dma=True)

            ckpt('m_hist')
            if hook_mid is None:
                items = [(n, tbi, off, nn) for n in range(6) for tbi, (off, nn) in enumerate(ps_["tbs"])]
            else:
                items = [(n, tbi, off, nn) for tbi, (off, nn) in enumerate(ps_["tbs"]) for n in range(6)]
            qk_state = {}
            qk_slots = {}

            def qk_S1(i):
                n, tbi, off, nn = items[i]
                if tbi == 0:
                    qk_slots[n] = WS.get(("w_in", l, n))
                sw, swk = qk_slots[n]
                bz, bzk = bank()
                def mm(e, s=sw, bk=bz, off=off, nn=nn):
                    for k in range(KC):
                        ins = e.matmul(bk[:, 0:nn], wring[:, s, k * 128:(k + 1) * 128], hT[:, k, off:off + nn],
                                       **flags(k, KC))
                    return ins
                P.op("pe", mm, reads=[swk] + [hk(c, tbi) for c in range(KC)], writes=[bzk])
                zg, zgk = tmp()
                wh = 0 if n < 4 else 1
                P.op("dve", lambda e, zg=zg, bz=bz, nn=nn, wh=wh: e.tensor_scalar(zg[:, 0:nn], bz[:, 0:nn], qkg(l, wh),
                                                                                  None, ALU.mult),
                     reads=[bzk, "small"], writes=[zgk])
                sl = i % 2
                sqh = dT[:, sl, 0:nn]
                P.op("act", lambda e, sqh=sqh, bz=bz, nn=nn: e.activation(sqh, bz[:, 0:nn], AF.Square),
                     reads=[bzk], writes=[("dT", sl)])
                if tbi == len(ps_["tbs"]) - 1:
                    WS.release()
                qk_state[i] = (zg, zgk, sl, state["tmpcount"])

            def qk_S2(i):
                n, tbi, off, nn = items[i]
                zg, zgk, sl, cnt0 = qk_state.pop(i)
                assert state["tmpcount"] - cnt0 < NTMP, (state["tmpcount"], cnt0)
                sqh = dT[:, sl, 0:nn]
                a1, a1k = tmp()
                P.op("pool", lambda e, a1=a1, zg=zg, off=off, nn=nn: e.tensor_tensor(
                    a1[:, 0:nn], zg[:, 0:nn], ropeC[:, off:off + nn], ALU.mult), reads=[zgk, "ropeC"], writes=[a1k])
                br, brk = bank()
                P.op("pe", lambda e, br=br, zg=zg, nn=nn: e.matmul(br[:, 0:nn], Rm, zg[:, 0:nn], start=True, stop=True),
                     reads=[zgk, "c32"], writes=[brk])
                bs, bsk = bank()
                P.op("pe", lambda e, bs=bs, sqh=sqh, nn=nn: e.matmul(bs[:, 0:nn], blk1, sqh, start=True, stop=True),
                     reads=[("dT", sl), "cb"], writes=[bsk])
                t1, t1k = tmp()
                P.op("act", lambda e, t1=t1, bs=bs, nn=nn: e.activation(t1[:, 0:nn], bs[:, 0:nn], AF.Ln,
                                                                        bias=epscol[:, 0:1]),
                     reads=[bsk, "neghalf"], writes=[t1k])
                a2, a2k = tmp()
                P.op("dve", lambda e, a2=a2, br=br, off=off, nn=nn: e.tensor_tensor(
                    a2[:, 0:nn], br[:, 0:nn], ropeS[:, off:off + nn], ALU.mult), reads=[brk, "ropeS"], writes=[a2k])
                P.op("dve", lambda e, a1=a1, a2=a2, nn=nn: e.tensor_tensor(
                    a2[:, 0:nn], a1[:, 0:nn], a2[:, 0:nn], ALU.add), reads=[a1k, a2k], writes=[a2k])
                t2, t2k = tmp()
                P.op("act", lambda e, t1=t1, t2=t2, nn=nn: e.activation(t2[:, 0:nn], t1[:, 0:nn], AF.Exp, scale=-0.5),
                     reads=[t1k], writes=[t2k])
                if n < 4:
                    P.op("dve", lambda e, a2=a2, t2=t2, n=n, off=off, nn=nn: e.tensor_tensor(
                        act[:, 8 + n, off:off + nn], a2[:, 0:nn], t2[:, 0:nn], ALU.mult),
                        reads=[a2k, t2k], writes=[ak(8 + n, tbi)])
                else:
                    g = n - 4
                    a3, a3k = tmp()
                    P.op("dve", lambda e, a2=a2, t2=t2, a3=a3, nn=nn: e.tensor_tensor(
                        a3[:, 0:nn], a2[:, 0:nn], t2[:, 0:nn], ALU.mult), reads=[a2k, t2k], writes=[a3k])
                    if isp:
                        P.op("act", lambda e, a3=a3, g=g, off=off, nn=nn: e.copy(
                            kdup[:, g, 128 + off:128 + off + nn], a3[:, 0:nn]),
                            reads=[a3k], writes=kd_keys_own(ps_, g, tbi))
                        if ps_["half"] == 1 and tbi == 1:
                            P.op("act", lambda e, a3=a3, g=g: e.copy(krot32[:, g, 0:128], a3[:, 384:512]),
                                 reads=[a3k], writes=[("krot32", g)])
                    else:
                        P.op("act", lambda e, a3=a3, g=g, nn=nn: e.copy(
                            kdup[:, g, 0:S * 192].rearrange("p (s c) -> p s c", c=192)[:, :, 128:192],
                            a3[:, 0:nn].rearrange("p (s c) -> p s c", c=64)),
                            reads=[a3k], writes=kd_keys_own(ps_, g, tbi))
                        P.op("act", lambda e, a3=a3, g=g, nn=nn: e.copy(krot32[:, g, 0:nn], a3[:, 0:nn]),
                             reads=[a3k], writes=[("krot32", g)])

            for i in range(len(items) + 1):
                if hook_mid is not None and i == 6:
                    hook_mid()
                if i < len(items):
                    qk_S1(i)
                if i >= 1:
                    qk_S2(i - 1)

            ckpt('m_qk_done')
            ckpt('m_qk')
            sw, swk = WS.get(("w_in", l, 6))
            ktiles = []
            for s, sg in enumerate(segs):
                if isp:
                    for i in range(Ls // 128):
                        ktiles.append((s, i * 128, 128, 1 + i))
                else:
                    ktiles.append((s, sg["t0"], DSEQ, sg["vbase"] + 1))
            for q0 in range(0, len(ktiles), 4):
                grp = ktiles[q0:q0 + 4]
                bv, bvk = bank()
                def mm(e, s=sw, bv=bv, grp=grp):
                    for gi, (_, toff, ksz, _) in enumerate(grp):
                        for k in range(KC):
                            ins = e.matmul(bv[0:ksz, gi * 128:(gi + 1) * 128], hT[:, k, toff:toff + ksz],
                                           wring[:, s, k * 128:(k + 1) * 128], **flags(k, KC))
                    return ins
                tbset = sorted(set(((toff // 512) if isp else 0) for (_, toff, _, _) in grp))
                P.op("pe", mm, reads=[swk] + [hk(c, tbi) for c in range(KC) for tbi in tbset], writes=[bvk])
                for gi, (s, toff, ksz, vidx) in enumerate(grp if not os.environ.get('KNO_VCOPY') else []):
                    vin = bv[0:ksz, gi * 128:(gi + 1) * 128].rearrange("r (g d) -> r g d", g=2)
                    P.op("act", lambda e, vin=vin, ksz=ksz, vidx=vidx: e.copy(
                        vpad[0:ksz, vidx, :].rearrange("r (g c) -> r g c", g=2)[:, :, 0:64], vin),
                        reads=[bvk], writes=[("vpad", vidx)])
                    P.op("act", lambda e, vin=vin, ksz=ksz, vidx=vidx: e.copy(
                        vpad[0:ksz, vidx, :].rearrange("r (g c) -> r g c", g=2)[:, :, 192:256], vin),
                        reads=[bvk, ("vpad", vidx)], writes=[("vpad", vidx)])
                    sg = segs[s]
                    last = (toff + ksz == sg["t0"] + Ls) and (not isp or ps_["half"] == 1) and not os.environ.get('KNO_VOUT')
                    if last:
                        P.op("dve", lambda e, bv=bv, gi=gi, ksz=ksz: e.tensor_copy(
                            ostage[0:ksz, 1, :], bv[0:ksz, gi * 128:(gi + 1) * 128]), reads=[bvk], writes=["ost1"])
                        if isp:
                            dst = nvp[l, sg["b"], :, :]
                        else:
                            dst = nvs[l, sg["b"], 64:128, :]
                        P.op("pool", lambda e, dst=dst, ksz=ksz: e.dma_start(out=dst, in_=ostage[0:ksz, 1, :]),
                             reads=["ost1"], writes=[("out", "nv", l, sg["b"], isp)], dma=True)
            WS.release()

            ckpt('m_v')
            for g in range(4):
                sw, swk = WS.get(("w_in", l, 7 + g))
                for tbi, (off, nn) in enumerate(ps_["tbs"]):
                    bz, bzk = bank()
                    def mm(e, s=sw, bk=bz, off=off, nn=nn):
                        for k in range(KC):
                            ins = e.matmul(bk[:, 0:nn], wring[:, s, k * 128:(k + 1) * 128], hT[:, k, off:off + nn],
                                           **flags(k, KC))
                        return ins
                    P.op("pe", mm, reads=[swk] + [hk(c, tbi) for c in range(KC)], writes=[bzk])
                    if isp:
                        P.op("dve", lambda e, bz=bz, g=g, off=off, nn=nn: e.tensor_copy(
                            uext[:, g, 15 + off:15 + off + nn], bz[:, 0:nn]),
                            reads=[bzk], writes=[("uext", g)])
                    else:
                        P.op("act", lambda e, bz=bz, g=g, nn=nn: e.copy(
                            uext[:, g, 0:S * 79].rearrange("p (s c) -> p s c", c=79)[:, :, 15:79],
                            bz[:, 0:nn].rearrange("p (s c) -> p s c", c=64)),
                            reads=[bzk], writes=[("uext", g)])
                WS.release()

            U3 = [uext[:, g, 0:S * UW].rearrange("p (s c) -> p s c", c=UW) for g in range(4)]
            A3 = tmpA[:, 0:S * UW].rearrange("p (s c) -> p s c", c=UW)
            B3 = tmpB[:, 0:S * UW].rearrange("p (s c) -> p s c", c=UW)
            def pool_group(g, w):
                cur, curk = U3[g], ("uext", g)
                width = 1
                bufs = [(A3, "tmpA"), (B3, "tmpB")]
                bi = 0
                while width < w:
                    nxt, nxtk = bufs[bi]
                    bi ^= 1
                    lo = 2 * width - 1
                    P.op("dve", lambda e, nxt=nxt, cur=cur, lo=lo, width=width: e.tensor_tensor(
                        nxt[:, :, lo:UW], cur[:, :, lo:UW], cur[:, :, lo - width:UW - width], ALU.add),
                        reads=[curk], writes=[nxtk])
                    cur, curk = nxt, nxtk
                    width *= 2
                dslot = g
                dview = dT[:, dslot, 0:T_].rearrange("p (s c) -> p s c", c=Ls)
                if isp and ps_["half"] == 0:
                    t1, t1k = tmp()
                    P.op("dve", lambda e, t1=t1, cur=cur, g=g: e.tensor_tensor(
                        t1[:, 0:15], cur[:, 0, 15:30], invcnt[:, g * 15:(g + 1) * 15], ALU.mult),
                        reads=[curk, "c32"], writes=[t1k])
                P.op("act", lambda e, cur=cur, w=w: e.activation(
                    cur[:, :, 15:UW], cur[:, :, 15:UW], AF.Copy, scale=1.0 / w), reads=[curk], writes=[curk])
                P.op("dve", lambda e, dview=dview, cur=cur, g=g: e.tensor_tensor(
                    dview, cur[:, :, 15:UW], U3[g][:, :, 15:UW], ALU.subtract),
                    reads=[curk, ("uext", g)], writes=[("dT", dslot)])
                if isp and ps_["half"] == 0:
                    P.op("dve", lambda e, t1=t1, g=g, dslot=dslot: e.tensor_tensor(
                        dT[:, dslot, 0:15], t1[:, 0:15], uext[:, g, 15:30], ALU.subtract),
                        reads=[t1k, ("uext", g), ("dT", dslot)], writes=[("dT", dslot)])

            pool_pending = [(g, w) for g, w in enumerate((2, 4, 8, 16))]
            ckpt('m_u')
            qtiles = []
            for s, sg in enumerate(segs):
                if isp:
                    for m in range(Ls // 128):
                        prev = None
                        if m > 0:
                            prev = (128 + (m - 1) * 128, m, 128, m, "P")
                        elif sg["hist"] == "carry":
                            prev = (0, 0, 128, 0, "P")
                        own = (128 + m * 128, 1 + m, 128, 1 + m, "O")
                        qtiles.append(dict(qoff=m * 128, qsz=128, prev=prev, own=own, mask=True))
                else:
                    kb = sg["kbase"]
                    qtiles.append(dict(qoff=sg["t0"], qsz=DSEQ, prev=(kb, sg["vbase"], 128, 2 * s, "P"),
                                       own=(kb + 128, sg["vbase"] + 1, DSEQ, 2 * s + 1, "O"), mask=False))
            units = [(qi, g) for qi in range(len(qtiles)) for g in range(2)]
            ustate = {}

            def att_A(ui):
                qi, g = units[ui]
                qt = qtiles[qi]
                qoff, qsz = qt["qoff"], qt["qsz"]
                tbq = (qoff // 512) if isp else 0
                kts = [k_ for k_ in (qt["prev"], qt["own"]) if k_ is not None]
                pbi = ui % 3
                pb = pbufs[pbi]
                pbk = ("pbuf", pbi)
                for ki, (kcol, vidx, ksz, kidx, kind) in enumerate(kts):
                    bi, bSk = bank_pair()
                    def mm(e, bi=bi, g=g, kcol=kcol, ksz=ksz, qoff=qoff, qsz=qsz):
                        for hh in range(4):
                            head = g * 4 + hh
                            j = head // 2
                            par = head % 2
                            p0 = 64 * par
                            c0 = (bi + par) * 512 + (hh // 2) * qsz
                            ins = e.matmul(psum_all[0:ksz, c0:c0 + qsz],
                                           kdup[p0:p0 + 64, g, kcol:kcol + ksz],
                                           act[p0:p0 + 64, 8 + j, qoff:qoff + qsz], start=True, stop=True)
                        return ins
                    P.op("pe", mm, reads=[("kdup", g, kidx), ak(8 + 2 * g, tbq), ak(9 + 2 * g, tbq)], writes=bSk)
                    src3 = psum_all[0:ksz, bi * 512:(bi + 2) * 512].rearrange("p (a c) -> p a c", a=2)[:, :, 0:2 * qsz]
                    dst3 = pb[0:ksz, ki, 0:4 * qsz].rearrange("p (a c) -> p a c", a=2)
                    if not qt["mask"]:
                        P.op("act", lambda e, src3=src3, dst3=dst3: e.activation(dst3, src3, AF.Exp, scale=0.125),
                             reads=bSk, writes=[pbk])
                    else:
                        for qh in range(2):
                            if kind == "P":
                                bcol = 316 if qh == 0 else 317
                            else:
                                bcol = 318 if qh == 0 else 316
                            s4 = src3.rearrange("p a (h q) -> p a h q", h=2)[:, :, :, qh * 64:(qh + 1) * 64]
                            d4 = dst3.rearrange("p a (h q) -> p a h q", h=2)[:, :, :, qh * 64:(qh + 1) * 64]
                            P.op("act", lambda e, s4=s4, d4=d4, bcol=bcol: e.activation(
                                d4, s4, AF.Exp, scale=0.125, bias=c32[:, bcol:bcol + 1]),
                                reads=bSk + ["c32"] + ([pbk] if qh else []), writes=[pbk])
                ustate[ui] = (pb, pbk, kts)

            def att_B(ui):
                qi, g = units[ui]
                qt = qtiles[qi]
                qoff, qsz = qt["qoff"], qt["qsz"]
                tbq = (qoff // 512) if isp else 0
                pb, pbk, kts = ustate.pop(ui)
                acc = 4 + 2 * (qi % 2)
                bN, bNk = banks[acc], ("bank", acc)
                bD, bDk = banks[acc + 1], ("bank", acc + 1)
                for jj in range(2):
                    j = g * 2 + jj
                    def mmN(e, bN=bN, pb=pb, g=g, jj=jj, j=j, kts=kts, qsz=qsz):
                        nmm = 2 * len(kts)
                        i = 0
                        for ki, (kcol, vidx, ksz, kidx, kind) in enumerate(kts):
                            for ab in range(2):
                                hh = 2 * jj + ab
                                pc = (hh % 2) * 2 * qsz + (hh // 2) * qsz
                                ins = e.matmul(bN[:, j * qsz:(j + 1) * qsz],
                                               vpad[0:ksz, vidx, g * 256 + ab * 128:g * 256 + (ab + 1) * 128],
                                               pb[0:ksz, ki, pc:pc + qsz], **flags(i, nmm))
                                i += 1
                        return ins
                    P.op("pe", mmN, reads=[pbk] + [("vpad", k_[1]) for k_ in kts], writes=[bNk])
                    def mmD(e, bD=bD, pb=pb, jj=jj, j=j, kts=kts, qsz=qsz):
                        nmm = 2 * len(kts)
                        i = 0
                        for ki, (kcol, vidx, ksz, kidx, kind) in enumerate(kts):
                            for ab in range(2):
                                hh = 2 * jj + ab
                                pc = (hh % 2) * 2 * qsz + (hh // 2) * qsz
                                ins = e.matmul(bD[:, j * qsz:(j + 1) * qsz],
                                               (onesA if ab == 0 else onesB)[0:ksz, :],
                                               pb[0:ksz, ki, pc:pc + qsz], **flags(i, nmm))
                                i += 1
                        return ins
                    P.op("pe", mmD, reads=[pbk, "cb"], writes=[bDk])
                if g == 1:
                    t1, t1k = tmp()
                    nq = 4 * qsz
                    P.op("dve", lambda e, t1=t1, bD=bD, qsz=qsz, nq=nq: e.tensor_tensor(
                        t1[:, 0:nq].rearrange("p (j q) -> p j q", j=4), bD[:, 0:nq].rearrange("p (j q) -> p j q", j=4),
                        esf[:, l, :, 0:qsz], ALU.add), reads=[bDk, "esf"], writes=[t1k])
                    t3, t3k = tmp()
                    P.op("act", lambda e, t1=t1, t3=t3, nq=nq: e.activation(t3[:, 0:nq], t1[:, 0:nq], AF.Ln),
                         reads=[t1k], writes=[t3k])
                    t2, t2k = tmp()
                    P.op("act", lambda e, t3=t3, t2=t2, nq=nq: e.activation(t2[:, 0:nq], t3[:, 0:nq], AF.Exp, scale=-1.0),
                         reads=[t3k], writes=[t2k])
                    P.op("dve", lambda e, t2=t2, bN=bN, qoff=qoff, qsz=qsz, nq=nq: e.tensor_tensor(
                        act[:, 0:4, qoff:qoff + qsz], bN[:, 0:nq].rearrange("p (j q) -> p j q", j=4),
                        t2[:, 0:nq].rearrange("p (j q) -> p j q", j=4), ALU.mult),
                        reads=[bNk, t2k], writes=[ak(c, tbq) for c in range(4)])

            SK = 2
            state["nrot"] = 4
            for i in range(len(units) + SK):
                if i < len(units):
                    att_A(i)
                if i >= SK:
                    att_B(i - SK)
                if i >= 1 and i % 3 == 1 and pool_pending:
                    pool_group(*pool_pending.pop(0))
            while pool_pending:
                pool_group(*pool_pending.pop(0))
            state["nrot"] = NROT_OUT

            ckpt('m_attn')
            if isp and ps_["half"] == 0:
                P.op("pool", lambda e: e.tensor_copy(kcarry[:, l, :, :], kdup[:, :, 128 + 896:128 + 1024]),
                     reads=[("kdup", 0, 8), ("kdup", 1, 8)], writes=[("kcarry", l)])
                P.op("pool", lambda e: e.tensor_copy(vcarry[:, l, :], vpad[:, 8, :]),
                     reads=[("vpad", 8)], writes=[("vcarry", l)])
            else:
                for s, sg in enumerate(segs):
                    nrow = 128 if isp else DSEQ
                    c0 = 0 if isp else sg["t0"]
                    bk, bkk = bank()
                    def mm(e, bk=bk, c0=c0, nrow=nrow):
                        for g in range(2):
                            ins = e.transpose(bk[0:nrow, g * 128:(g + 1) * 128], krot32[:, g, c0:c0 + nrow], ident)
                        return ins
                    P.op("pe", mm, reads=[("krot32", 0), ("krot32", 1), "c32"], writes=[bkk])
                    P.op("dve", lambda e, bk=bk, nrow=nrow: e.tensor_copy(
                        ostage[0:nrow, 0, :].rearrange("r (g d) -> r g d", g=2),
                        bk[0:nrow, 0:256].rearrange("r (g c) -> r g c", g=2)[:, :, 0:64]),
                        reads=[bkk], writes=["ost0"])
                    dst = nkp[l, sg["b"], :, :] if isp else nks[l, sg["b"], 64:128, :]
                    P.op("pool", lambda e, dst=dst, nrow=nrow: e.dma_start(out=dst, in_=ostage[0:nrow, 0, :]),
                         reads=["ost0"], writes=[("out", "nk", l, sg["b"], isp)], dma=True)

            ckpt('m_kout')
            swp, swpk = WS.get(("w_pool", l))
            for g in range(4):
                dslot = g
                for tbi, (off, nn) in enumerate(ps_["tbs"]):
                    bz, bzk = bank()
                    P.op("pe", lambda e, bz=bz, g=g, dslot=dslot, off=off, nn=nn, s=swp: e.matmul(
                        bz[:, 0:nn], wring[:, s, g * 128:(g + 1) * 128], dT[:, dslot, off:off + nn], start=True, stop=True),
                        reads=[swpk, ("dT", dslot)], writes=[bzk])
                    P.op("dve", lambda e, bz=bz, g=g, off=off, nn=nn: e.tensor_scalar(
                        act[:, 4 + g, off:off + nn], bz[:, 0:nn], pscale(l, g), None, ALU.mult),
                        reads=[bzk, "small"], writes=[ak(4 + g, tbi)])
            WS.release()
            if isp and ps_["half"] == 0:
                P.op("pool", lambda e: e.tensor_copy(ucarry[:, l, :, :], uext[:, :, TP:TP + 15]),
                     reads=[("uext", g) for g in range(4)], writes=[("ucarry", l)])
            else:
                for s, sg in enumerate(segs):
                    ub = sg["ubase"]
                    bk, bkk = bank()
                    def mm(e, bk=bk, ub=ub):
                        for g in range(4):
                            ins = e.transpose(bk[0:15, g * 128:(g + 1) * 128], uext[:, g, ub + Ls:ub + Ls + 15], ident)
                        return ins
                    P.op("pe", mm, reads=[("uext", g) for g in range(4)] + ["c32"], writes=[bkk])
                    P.op("dve", lambda e, bk=bk: e.tensor_copy(pstage[0:15, 1, :], bk[0:15, :]), reads=[bkk], writes=["pst1"])
                    dst = npp[l, sg["b"]] if isp else nps[l, sg["b"]]
                    P.op("pool", lambda e, dst=dst: e.dma_start(out=dst, in_=pstage[0:15, 1, :]), reads=["pst1"],
                         writes=[("out", "np", l, sg["b"], isp)], dma=True)

            ckpt('m_pool')
            def wo_block(d, tbi, sw, swk):
                off, nn = ps_["tbs"][tbi]
                bz, bzk = bank()
                def mm(e, s=sw, bk=bz, off=off, nn=nn):
                    for k in range(KC):
                        ins = e.matmul(bk[:, 0:nn], wring[:, s, k * 128:(k + 1) * 128], act[:, k, off:off + nn],
                                       **flags(k, KC))
                    return ins
                P.op("pe", mm, reads=[swk] + [ak(c, tbi) for c in range(KC)], writes=[bzk])
                P.op("dve", lambda e, bz=bz, d=d, off=off, nn=nn: e.tensor_tensor(
                    xT[:, d, off:off + nn], bz[:, 0:nn], xT[:, d, off:off + nn], ALU.add),
                    reads=[bzk, xk(d, tbi)], writes=[xk(d, tbi)])

            if not defer_wout:
                for d in range(KC):
                    sl_ = WS.get(("w_out", l, d))
                    for tbi in range(len(ps_["tbs"])):
                        wo_block(d, tbi, *sl_)
                    WS.release()
                return None
            wslots = [WS.get(("w_out", l, d)) for d in range(KC)]
            tail0 = [(lambda d=d: wo_block(d, 0, *wslots[d])) for d in range(KC)]
            def wt1(d):
                wo_block(d, 1, *wslots[d])
                WS.release()
            tail1 = [(lambda d=d: wt1(d)) for d in range(KC)]
            return tail0, tail1

        def pe_gate(l, ps_):
            for d in range(KC):
                sg_, sgk = WS.get(("pe_gate", l, d))
                su_, suk = WS.get(("pe_up", l, d))
                for tbi, (off, nn) in enumerate(ps_["tbs"]):
                    bg, bgk = bank()
                    bu, buk = bank()
                    def mm(e, s=sg_, bk=bg, off=off, nn=nn):
                        for k in range(KC):
                            ins = e.matmul(bk[:, 0:nn], wring[:, s, k * 128:(k + 1) * 128], hT[:, k, off:off + nn],
                                           **flags(k, KC))
                        return ins
                    P.op("pe", mm, reads=[sgk] + [hk(c, tbi) for c in range(KC)], writes=[bgk])
                    def mm2(e, s=su_, bk=bu, off=off, nn=nn):
                        for k in range(2):
                            ins = e.matmul(bk[:, 0:nn], wring[:, s, k * 128:(k + 1) * 128], peT[:, k, off:off + nn],
                                           **flags(k, 2))
                        return ins
                    tts = list(range(off // 128, (off + nn + 127) // 128))
                    P.op("pe", mm2, reads=[suk] + [("peT", tt) for tt in tts], writes=[buk])
                    t1, t1k = tmp()
                    P.op("act", lambda e, t1=t1, bg=bg, nn=nn: e.activation(t1[:, 0:nn], bg[:, 0:nn], AF.Tanh, scale=0.5),
                         reads=[bgk], writes=[t1k])
                    t2, t2k = tmp()
                    P.op("dve", lambda e, t1=t1, t2=t2, bu=bu, nn=nn: e.scalar_tensor_tensor(
                        t2[:, 0:nn], t1[:, 0:nn], 1.0, bu[:, 0:nn], ALU.add, ALU.mult), reads=[t1k, buk], writes=[t2k])
                    P.op("dve", lambda e, t2=t2, d=d, off=off, nn=nn: e.scalar_tensor_tensor(
                        xT[:, d, off:off + nn], t2[:, 0:nn], 0.5, xT[:, d, off:off + nn], ALU.mult, ALU.add),
                        reads=[t2k, xk(d, tbi)], writes=[xk(d, tbi)])
                WS.release()
                WS.release()

        def load_rope(ps_):
            T_ = ps_["T"]
            if ps_["kind"] == "p":
                c0 = ps_["half"] * TP
                P.op("pool", lambda e, c0=c0: e.dma_start(out=ropeC[:, 0:TP], in_=rope_p_d[0, :, c0:c0 + TP]),
                     writes=["ropeC"], dma=True)
                P.op("pool", lambda e, c0=c0: e.dma_start(out=ropeS[:, 0:TP], in_=rope_p_d[1, :, c0:c0 + TP]),
                     writes=["ropeS"], dma=True)
            else:
                P.op("pool", lambda e, T_=T_: e.dma_start(out=ropeC[:, 0:T_], in_=rope_s_d[0, :, 0:T_]),
                     writes=["ropeC"], dma=True)
                P.op("pool", lambda e, T_=T_: e.dma_start(out=ropeS[:, 0:T_], in_=rope_s_d[1, :, 0:T_]),
                     writes=["ropeS"], dma=True)

        pipelined_entry = False
        for pidx, ps_ in enumerate(passes if DBG_STOP not in ("consts", "prologue") else []):
            nxt = passes[pidx + 1] if pidx + 1 < len(passes) else None
            if nxt is not None and (nxt["kind"] != "p" or DBG_STOP or os.environ.get("KNOPIPE") or os.environ.get("KNOXPIPE")):
                nxt = None
            T_ = ps_["T"]
            if not pipelined_entry:
                load_rope(ps_)
            try:
                if ps_["kind"] == "s" or DBG_STOP or os.environ.get("KNOPIPE"):
                    load_x(ps_)
                    ckpt("load_x")
                    for l in range(L):
                        load_pe(l, ps_)
                        ckpt("load_pe")
                        norm(l, 0, ps_)
                        ckpt("norm0")
                        ffn(l, "a", ps_)
                        ckpt("ffna")
                        norm(l, 1, ps_)
                        mixer(l, ps_)
                        ckpt("mixer")
                        norm(l, 2, ps_)
                        ffn(l, "b", ps_)
                        norm(l, 3, ps_)
                        pe_gate(l, ps_)
                        ckpt("layer")
                    store_y(ps_)
                else:
                    phases = []
                    for l in range(L):
                        phases += [("ffn", l, "a", 0), ("mix", l, None, 1), ("ffn", l, "b", 2), ("gate", l, None, 3)]
                    if not pipelined_entry:
                        load_x(ps_)
                        norm(0, 0, ps_)
                    for pi, (kind, l, which, n) in enumerate(phases):
                        if pi > 0 or pipelined_entry:
                            hook = (lambda l=l, n=n: norm_rest(l, n, ps_, 1))
                        else:
                            hook = (lambda: None)
                        if kind == "ffn":
                            ith = None
                            if which == "b":
                                def ith(it, l=l):
                                    if it < 8:
                                        pe_dma(l, ps_, it)
                                    if 1 <= it <= 8:
                                        pe_tr(l, ps_, it - 1)
                            t0, t1 = ffn_pipe(l, which, ps_, hook, ith)
                        elif kind == "mix":
                            t0, t1 = mixer(l, ps_, hook_mid=hook, defer_wout=True)
                        else:
                            t0, t1 = gate_pipe(l, ps_, hook)
                        if pi + 1 == len(phases) and nxt is not None:
                            load_rope(nxt)
                            load_dma(nxt, 0)
                            load_dma(nxt, 1)
                        for b_ in t0:
                            b_()
                        if pi + 1 < len(phases):
                            _, l2, _, n2 = phases[pi + 1]
                            norm_sq(l2, n2, ps_, 0)
                            for b_ in t1[:2]:
                                b_()
                            norm_rest(l2, n2, ps_, 0)
                            for b_ in t1[2:]:
                                b_()
                            norm_sq(l2, n2, ps_, 1)
                        elif nxt is None:
                            for b_ in t1:
                                b_()
                        else:
                            for tt in range(4):
                                store_tile(ps_, tt)
                                load_tr(nxt, tt)
                                load_dma(nxt, tt + 2)
                            norm_sq(0, 0, nxt, 0)
                            for b_ in t1[:2]:
                                b_()
                            norm_rest(0, 0, nxt, 0)
                            for b_ in t1[2:]:
                                b_()
                            for tt in range(4, 8):
                                store_tile(ps_, tt)
                                load_tr(nxt, tt)
                                if tt + 2 < 8:
                                    load_dma(nxt, tt + 2)
                            norm_sq(0, 0, nxt, 1)
                    if nxt is None:
                        store_y(ps_)
                        pipelined_entry = False
                    else:
                        pipelined_entry = True
            except _Stop:
                break
        if not DBG_STOP:
            assert WS.pos == len(WS.seq), (WS.pos, len(WS.seq))
        P.finish("pool")
        P.build(st)
        stats = P.stats
    return nc, stats


def _small_table(norm_ffa, norm_mix, norm_ffb, norm_pe, q_norm, k_norm, sinks, pool_scale):
    sm = np.zeros((128, 84), np.float32)
    norms = [norm_ffa, norm_mix, norm_ffb, norm_pe]
    p = np.arange(128)
    for l in range(L):
        for n in range(4):
            sm[:, (l * 4 + n) * 8:(l * 4 + n) * 8 + 8] = np.asarray(norms[n][l], np.float32).reshape(8, 128).T
        sm[:, 64 + l * 2 + 0] = np.asarray(q_norm[l], np.float32)[p % 64]
        sm[:, 64 + l * 2 + 1] = np.asarray(k_norm[l], np.float32)[p % 64]
        for j in range(4):
            sm[:, 68 + l * 4 + j] = np.asarray(sinks[l], np.float32)[2 * j + p // 64]
        sm[:, 76 + l * 4:76 + l * 4 + 4] = np.asarray(pool_scale[l], np.float32).reshape(4, 128).T
    return sm


_CACHE = {}


def _get_prog(nb_p, nb_s):
    key = (nb_p, nb_s)
    if key not in _CACHE:
        _CACHE[key] = build_program(nb_p, nb_s)
    return _CACHE[key]


def make_in_maps(inputs, ncores, nb_p, nb_s):
    f32 = lambda a: np.ascontiguousarray(np.asarray(a, dtype=np.float32))
    c32, cb, rope_p, rope_s = _consts()
    sm = _small_table(inputs["norm_ffa"], inputs["norm_mix"], inputs["norm_ffb"], inputs["norm_pe"],
                      inputs["q_norm"], inputs["k_norm"], inputs["sinks"], inputs["pool_scale"])
    shared = dict(
        w_ffa_in=f32(inputs["w_ffa_in"]), w_ffa_out=f32(inputs["w_ffa_out"]),
        w_ffb_in=f32(inputs["w_ffb_in"]), w_ffb_out=f32(inputs["w_ffb_out"]),
        w_in=f32(inputs["w_in"]), w_out=f32(inputs["w_out"]), w_pe_gate=f32(inputs["w_pe_gate"]),
        w_pe_up=f32(inputs["w_pe_up"]), w_pool=f32(inputs["w_pool"]),
        small=sm, c32=c32, cb=cb, rope_p=rope_p, rope_s=rope_s)
    xp = f32(inputs["x_prompt"]); xs = f32(inputs["x_sample"])
    pp = f32(inputs["p_prompt"]); ps = f32(inputs["p_sample"])
    ck = f32(inputs["cache_k"]); cv = f32(inputs["cache_v"]); sp = f32(inputs["state_pool"])
    maps = []
    for i in range(ncores):
        bp = slice(i * nb_p, (i + 1) * nb_p) if nb_p else slice(0, 1)
        bs = slice(i * nb_s, (i + 1) * nb_s) if nb_s else slice(0, 1)
        nbs = max(nb_s, 1)
        m = dict(shared)
        m["xp"] = np.ascontiguousarray(xp[bp])
        m["xs"] = np.ascontiguousarray(xs[bs]).reshape(nbs * DSEQ, D)
        m["pp"] = np.ascontiguousarray(pp[:, bp])
        m["ps"] = np.ascontiguousarray(ps[:, bs]).reshape(L, nbs * DSEQ, PE_DIM)
        m["ck"] = np.ascontiguousarray(ck[:, bs]).reshape(L, nbs, 128, 128)
        m["cv"] = np.ascontiguousarray(cv[:, bs]).reshape(L, nbs, 128, 128)
        m["spool"] = np.ascontiguousarray(sp[:, bs])
        maps.append(m)
    return maps


def gather(results, nb_p, nb_s):
    nbp = max(nb_p, 1)
    nbs = max(nb_s, 1)
    yp = np.concatenate([r["yp"] for r in results], axis=0)
    ys = np.concatenate([r["ys"].reshape(nbs, DSEQ, D) for r in results], axis=0)
    nkp = np.concatenate([r["nkp"].reshape(L, nbp, 128, 2, 64) for r in results], axis=1)
    nvp = np.concatenate([r["nvp"].reshape(L, nbp, 128, 2, 64) for r in results], axis=1)
    npp = np.concatenate([r["npp"] for r in results], axis=1)
    nks = np.concatenate([r["nks"].reshape(L, nbs, 128, 2, 64) for r in results], axis=1)
    nvs = np.concatenate([r["nvs"].reshape(L, nbs, 128, 2, 64) for r in results], axis=1)
    nps = np.concatenate([r["nps"] for r in results], axis=1)
    return tuple(np.ascontiguousarray(a, dtype=np.float32) for a in (yp, ys, nkp, nvp, npp, nks, nvs, nps))


def kernel(**inputs):
    nb_p = inputs["x_prompt"].shape[0] // NCORES
    nb_s = inputs["x_sample"].shape[0] // NCORES
    nc, _ = _get_prog(nb_p, nb_s)
    maps = make_in_maps(inputs, NCORES, nb_p, nb_s)
    res = run_bass_kernel_spmd(nc, maps, core_ids=list(range(NCORES)))
    return gather(res.results, nb_p, nb_s)
```

```python
import math
from contextlib import ExitStack

import numpy as np
import concourse.bass as bass
import concourse.mybir as mybir
from concourse.bass_utils import run_bass_kernel_spmd

F32 = mybir.dt.float32
BF16 = mybir.dt.bfloat16
AF = mybir.ActivationFunctionType
ALU = mybir.AluOpType

L = 2
D = 1024
KC = 8
FF = 2816
FC = 22
FH = 11
SEQ = 2048
TP = 1024
DSEQ = 64
PE_DIM = 256
EPS = 1e-6
PAST = 4096
NCORES = 8

import os
DBG_STOP = os.environ.get("KSTOP", "")


class _Stop(Exception):
    pass


def ckpt(name):
    if DBG_STOP and DBG_STOP == name:
        raise _Stop()


ENGS = ("pe", "act", "dve", "pool", "sp")
N_DMA_SLOTS = 24


class Prog:
    def __init__(self, nc):
        self.nc = nc
        self.ops = []
        self.last_w = {}
        self.readers = {}

    def op(self, eng, fn, reads=(), writes=(), dma=False):
        i = len(self.ops)
        deps = set()
        for r in reads:
            w = self.last_w.get(r)
            if w is not None:
                deps.add(w)
            if isinstance(r, tuple) and r[0] == "bank":
                for x in self.readers.get(r, ()):
                    if self.ops[x]["eng"] != eng:
                        deps.add(x)
        for r in writes:
            w = self.last_w.get(r)
            if w is not None:
                deps.add(w)
            for x in self.readers.get(r, ()):
                deps.add(x)
        self.ops.append(dict(eng=eng, fn=fn, deps=deps, dma=dma, signal=dma, sig=None, waits=None))
        for r in reads:
            self.readers.setdefault(r, []).append(i)
        for r in writes:
            self.last_w[r] = i
            self.readers[r] = []
        return i

    def finish(self, eng="pool"):
        deps = set(self.last_w.values())
        self.ops.append(dict(eng=eng, fn=None, deps=deps, dma=False, signal=False, sig=None, waits=None))

    def build(self, stack):
        nc = self.nc
        ops = self.ops

        def pe_pe(o, od):
            return od["eng"] == "pe" and o["eng"] == "pe" and not od["dma"] and not o["dma"]

        slot_last = {}
        nslot = {"sp": 0, "pool": 0, "act": 0}
        slot_rng = {"sp": (0, 14), "pool": (14, 8), "act": (22, 2)}
        for i, o in enumerate(ops):
            if o["dma"]:
                base, cntq = slot_rng[o["eng"]]
                s = base + nslot[o["eng"]] % cntq
                nslot[o["eng"]] += 1
                o["slot"] = s
                if s in slot_last:
                    o["deps"].add(slot_last[s])
                slot_last[s] = i
        for o in ops:
            for d in o["deps"]:
                if not pe_pe(o, ops[d]):
                    ops[d]["signal"] = True
        cnt = {e: 0 for e in ENGS}
        dcnt = {}
        for o in ops:
            if o["dma"]:
                s = o["slot"]
                dcnt[s] = dcnt.get(s, 0) + 1
                o["sig"] = (("dma", s), 16 * dcnt[s])
            elif o["signal"]:
                cnt[o["eng"]] += 1
                o["sig"] = (o["eng"], cnt[o["eng"]])
        known = {e: {} for e in ENGS}
        clocks = [None] * len(ops)
        for i, o in enumerate(ops):
            e = o["eng"]
            kn = known[e]
            waits = {}
            for d in sorted(o["deps"], reverse=True):
                od = ops[d]
                if pe_pe(o, od):
                    continue
                key, val = od["sig"]
                if kn.get(key, 0) >= val:
                    continue
                waits[key] = max(waits.get(key, 0), val)
                for k2, v2 in clocks[d].items():
                    if kn.get(k2, 0) < v2:
                        kn[k2] = v2
            o["waits"] = list(waits.items())
            ck = dict(kn)
            if o["sig"] is not None:
                k, v = o["sig"]
                if ck.get(k, 0) < v:
                    ck[k] = v
            clocks[i] = ck
            o["deps"] = None
        del clocks
        sems = {}
        for e in ENGS:
            sems[e] = stack.enter_context(nc.semaphore("sem_" + e))
        for s in range(N_DMA_SLOTS):
            sems[("dma", s)] = stack.enter_context(nc.semaphore("sem_dma%d" % s))
        per_eng = {e: [o for o in ops if o["eng"] == e] for e in ENGS}
        self.stats = {e: len(per_eng[e]) for e in ENGS}

        def run(ename, eh):
            for o in per_eng[ename]:
                for k, v in o["waits"]:
                    eh.wait_ge(sems[k], v)
                if o["fn"] is None:
                    continue
                ins = o["fn"](eh)
                if o["sig"] is not None:
                    k, v = o["sig"]
                    ins.then_inc(sems[k], 16 if o["dma"] else 1)

        block = stack.enter_context(nc.Block())

        @block.sync
        def _(e):
            run("sp", e)

        @block.scalar
        def _(e):
            run("act", e)

        @block.vector
        def _(e):
            run("dve", e)

        @block.gpsimd
        def _(e):
            run("pool", e)

        @block.tensor
        def _(e):
            run("pe", e)


def _rope_tables(pos):
    pos = np.asarray(pos, np.float64)
    half = 8
    inv = np.power(500000.0, -np.arange(half, dtype=np.float64) / half)
    ang = (pos.astype(np.float32)[None, :] * inv.astype(np.float32)[:, None]).astype(np.float32).astype(np.float64)
    C = np.ones((128, len(pos)), np.float64)
    S = np.zeros((128, len(pos)), np.float64)
    for p in range(128):
        d = p % 64
        if d < 16:
            C[p] = np.cos(ang[d % 8])
            S[p] = np.sin(ang[d % 8])
    return C.astype(np.float32), S.astype(np.float32)


def _consts():
    ident = np.eye(128, dtype=np.float32)
    R = np.zeros((128, 128), np.float32)
    for m in range(128):
        d = m % 64
        if d < 8:
            R[m + 8, m] = -1.0
        elif d < 16:
            R[m - 8, m] = 1.0
    invcnt = np.zeros((128, 4, 15), np.float32)
    for g, w in enumerate((2, 4, 8, 16)):
        for t in range(15):
            invcnt[:, g, t] = 1.0 / min(t + 1, w)
    negc = np.zeros((128, 3), np.float32)
    negc[0:64, 1] = -30000.0
    negc[64:128, 2] = -30000.0
    c32 = np.concatenate([ident, R, invcnt.reshape(128, 60), negc], axis=1)
    onesD = np.full((128, 128), 1.0 / 1024.0, np.float32)
    blk = np.zeros((128, 128), np.float32)
    blk[0:64, 0:64] = 1.0 / 64.0
    blk[64:128, 64:128] = 1.0 / 64.0
    onesA = np.zeros((128, 128), np.float32)
    onesA[:, 0:64] = 1.0
    onesB = np.zeros((128, 128), np.float32)
    onesB[:, 64:128] = 1.0
    mP = np.ones((128, 128), np.float32)
    mP[0:64, 64:128] = 0.0
    mO = np.ones((128, 128), np.float32)
    mO[64:128, 0:64] = 0.0
    cb = np.concatenate([onesD, blk, onesA, onesB, np.tile(mP, (1, 4)), np.tile(mO, (1, 4))], axis=1)
    cp, sp_ = _rope_tables(np.arange(SEQ))
    cs, ss = _rope_tables(PAST + np.arange(DSEQ))
    rope_p = np.stack([cp, sp_], axis=0)
    rope_s = np.stack([np.tile(cs, (1, 4)), np.tile(ss, (1, 4))], axis=0)
    return c32, cb, rope_p, rope_s


def build_program(nb_p=4, nb_s=4):
    nc = bass.Bass("TRN2", target_bir_lowering=False)

    def din(name, shape):
        return nc.dram_tensor(name, list(shape), F32, kind="ExternalInput").ap()

    def dout(name, shape):
        return nc.dram_tensor(name, list(shape), F32, kind="ExternalOutput").ap()

    NBP = max(nb_p, 1)
    NBS = max(nb_s, 1)
    xp = din("xp", [NBP, SEQ, D])
    xs = din("xs", [NBS * DSEQ, D])
    pp = din("pp", [L, NBP, SEQ, PE_DIM])
    psm = din("ps", [L, NBS * DSEQ, PE_DIM])
    ck = din("ck", [L, NBS, 128, 128])
    cv = din("cv", [L, NBS, 128, 128])
    spool = din("spool", [L, NBS, 15, 512])
    W = {}
    W["ffa_in"] = din("w_ffa_in", [L, D, 2 * FF])
    W["ffa_out"] = din("w_ffa_out", [L, FF, D])
    W["ffb_in"] = din("w_ffb_in", [L, D, 2 * FF])
    W["ffb_out"] = din("w_ffb_out", [L, FF, D])
    W["w_in"] = din("w_in", [L, D, 1280])
    W["w_out"] = din("w_out", [L, D, D])
    W["pe_gate"] = din("w_pe_gate", [L, D, D])
    W["pe_up"] = din("w_pe_up", [L, PE_DIM, D])
    W["w_pool"] = din("w_pool", [L, 4, 128, 128])
    small_d = din("small", [128, 84])
    c32_d = din("c32", [128, 319])
    cb_d = din("cb", [128, 1536])
    rope_p_d = din("rope_p", [2, 128, SEQ])
    rope_s_d = din("rope_s", [2, 128, 256])

    yp = dout("yp", [NBP, SEQ, D])
    ys = dout("ys", [NBS * DSEQ, D])
    nkp = dout("nkp", [L, NBP, 128, 128])
    nvp = dout("nvp", [L, NBP, 128, 128])
    npp = dout("npp", [L, NBP, 15, 512])
    nks = dout("nks", [L, NBS, 128, 128])
    nvs = dout("nvs", [L, NBS, 128, 128])
    nps = dout("nps", [L, NBS, 15, 512])

    tiles = {}
    tile_list = []

    def add_tile(tid, elems):
        tiles[tid] = (len(tile_list), elems)
        tile_list.append(tid)

    for l in range(L):
        for which in ("a", "b"):
            for n in range(2 * FC):
                add_tile(("ffn_in", l, which, n), KC * 128)
            for h in range(2):
                for d in range(KC):
                    add_tile(("ffn_out", l, which, h, d), FH * 128)
        for n in range(11):
            add_tile(("w_in", l, n), KC * 128)
        for d in range(KC):
            add_tile(("w_out", l, d), KC * 128)
        for d in range(KC):
            add_tile(("pe_gate", l, d), KC * 128)
        for d in range(KC):
            add_tile(("pe_up", l, d), 2 * 128)
        add_tile(("w_pool", l), 4 * 128)
    WMAX = FH * 128
    scr = nc.dram_tensor("scr", [len(tile_list), 128, WMAX], BF16).ap()

    st = ExitStack()
    with st:
        def sb(name, shape, dt):
            return st.enter_context(nc.sbuf_tensor("sb_" + name, list(shape), dt))

        T = TP
        xT = sb("xT", [128, KC, T], F32)
        hT = sb("hT", [128, KC, T], BF16)
        act = sb("act", [128, 12, T], BF16)
        kdup = sb("kdup", [128, 2, 1152], BF16)
        krot32 = sb("krot32", [128, 2, 256], F32)
        uext = sb("uext", [128, 4, 1040], F32)
        dT = sb("dT", [128, 4, T], BF16)
        vpad = sb("vpad", [128, 9, 512], BF16)
        tmpA = sb("tmpA", [128, 1040], F32)
        tmpB = sb("tmpB", [128, 1040], F32)
        peT = sb("peT", [128, 2, T], BF16)
        ropeC = sb("ropeC", [128, T], F32)
        ropeS = sb("ropeS", [128, T], F32)
        NSLOT = int(os.environ.get("KNSLOT", "10"))
        wring = sb("wring", [128, NSLOT, WMAX], BF16)
        NTMP = int(os.environ.get("KNTMP", "10"))
        tmps = [sb("tmp%d" % i, [128, 512], F32) for i in range(NTMP)]
        pbufs = [sb("pbuf%d" % i, [128, 2, 512], BF16) for i in range(3)]
        c32 = sb("c32", [128, 319], F32)
        cb = sb("cb", [128, 1536], BF16)
        small = sb("small", [128, 84], F32)
        esf = sb("esf", [128, L, 4, 128], F32)
        neghalf = sb("neghalf", [128, 8], F32)
        epscol = neghalf
        kcarry = sb("kcarry", [128, L, 2, 128], BF16)
        vcarry = sb("vcarry", [128, L, 512], BF16)
        ucarry = sb("ucarry", [128, L, 4, 15], F32)
        kvstage = sb("kvstage", [128, 2, 256], F32)
        pstage = sb("pstage", [128, 2, 512], F32)
        ostage = sb("ostage", [128, 4, 128], F32)
        pin = sb("pin", [128, 2, 256], F32)

        psum_all = st.enter_context(nc.psum_tensor("psum_all", [128, 4096], F32))
        banks = [psum_all[:, i * 512:(i + 1) * 512] for i in range(8)]

        ident = c32[:, 0:128]
        Rm = c32[:, 128:256]
        invcnt = c32[:, 256:316]
        onesD = cb[:, 0:128]
        blk1 = cb[:, 128:256]
        onesA = cb[:, 256:384]
        onesB = cb[:, 384:512]
        maskP = cb[:, 512:1024]
        maskO = cb[:, 1024:1536]

        def gcol(l, n, c):
            return small[:, (l * 4 + n) * 8 + c:(l * 4 + n) * 8 + c + 1]

        def qkg(l, which):
            return small[:, 64 + l * 2 + which:64 + l * 2 + which + 1]

        def pscale(l, g):
            return small[:, 76 + l * 4 + g:76 + l * 4 + g + 1]

        P = Prog(nc)
        state = dict(bank=0, tmp=0, pbuf=0, dmaq=0)

        NROT_OUT = int(os.environ.get("KNROT", "6"))
        state["nrot"] = NROT_OUT

        def bank():
            i = state["bank"] % state["nrot"]
            state["bank"] = (i + 1) % state["nrot"]
            return banks[i], ("bank", i)

        def bank_pair():
            i = state["bank"] % state["nrot"]
            if i % 2:
                i = (i + 1) % state["nrot"]
            state["bank"] = (i + 2) % state["nrot"]
            return i, [("bank", i), ("bank", i + 1)]

        def tmp():
            i = state["tmp"]
            state["tmp"] = (i + 1) % NTMP
            state["tmpcount"] = state.get("tmpcount", 0) + 1
            return tmps[i], ("tmp", i)

        def flags(i, n):
            return dict(start=(i == 0), stop=(i == n - 1))

        P.op("sp", lambda e: e.dma_start(out=c32[:], in_=c32_d), writes=["c32"], dma=True)
        P.op("sp", lambda e: e.dma_start(out=small[:], in_=small_d), writes=["small"], dma=True)
        P.op("pool", lambda e: e.dma_start(out=cb[:], in_=cb_d), writes=["cb"], dma=True)
        P.op("pool", lambda e: e.memset(neghalf[:], EPS), writes=["neghalf"])
        P.op("dve", lambda e: e.memset(vpad[:], 0.0), writes=[("vpad", i) for i in range(9)])
        zt, zk = tmp()
        P.op("dve", lambda e: e.memset(zt[:], 0.0), writes=[zk])
        et, ek = tmp()
        P.op("act", lambda e: e.activation(et[:, 0:8], small[:, 68:76], AF.Exp), reads=["small"], writes=[ek])
        for l in range(L):
            for j in range(4):
                P.op("dve", lambda e, l=l, j=j: e.tensor_scalar_add(esf[:, l, j, :], zt[:, 0:128],
                                                                    et[:, l * 4 + j:l * 4 + j + 1]),
                     reads=[zk, ek], writes=["esf"])

        def src_ap(tid):
            kind = tid[0]
            if kind == "ffn_in":
                _, l, which, n = tid
                w = W["ffa_in" if which == "a" else "ffb_in"]
                return [(w[l, :, n * 128:(n + 1) * 128].rearrange("(k p) c -> p k c", p=128), 0, KC, 0, 128)]
            if kind == "ffn_out":
                _, l, which, h, d = tid
                w = W["ffa_out" if which == "a" else "ffb_out"]
                return [(w[l, h * FH * 128:(h + 1) * FH * 128, d * 128:(d + 1) * 128]
                         .rearrange("(k p) c -> p k c", p=128), 0, FH, 0, 128)]
            if kind == "w_in":
                _, l, n = tid
                w = W["w_in"]
                if n < 4:
                    cols = [(n * 128, 128, 0)]
                elif n < 6:
                    c0 = 512 + (n - 4) * 64
                    cols = [(c0, 64, 0), (c0, 64, 64)]
                elif n == 6:
                    cols = [(640, 128, 0)]
                else:
                    cols = [(768 + (n - 7) * 128, 128, 0)]
                return [(w[l, :, c0:c0 + cn].rearrange("(k p) c -> p k c", p=128), 0, KC, o0, cn)
                        for (c0, cn, o0) in cols]
            if kind in ("w_out", "pe_gate"):
                _, l, d = tid
                w = W[kind]
                return [(w[l, :, d * 128:(d + 1) * 128].rearrange("(k p) c -> p k c", p=128), 0, KC, 0, 128)]
            if kind == "pe_up":
                _, l, d = tid
                w = W["pe_up"]
                return [(w[l, :, d * 128:(d + 1) * 128].rearrange("(k p) c -> p k c", p=128), 0, 2, 0, 128)]
            if kind == "w_pool":
                _, l = tid
                w = W["w_pool"]
                return [(w[l].rearrange("g p c -> p g c"), 0, 4, 0, 128)]
            raise ValueError(tid)

        def wsrc(name, l, r0, nr, c0, ncol):
            return W[name][l, r0:r0 + nr, c0:c0 + ncol].rearrange("(k p) c -> p k c", p=128)

        groups = []
        for l in range(L):
            for which in ("a", "b"):
                for n0 in range(0, 2 * FC, 4):
                    groups.append((("ffn_in", l, which, n0), 4, KC,
                                   [(wsrc("ff%s_in" % which, l, 0, D, n0 * 128, 512), 0, 512)]))
                for h in range(2):
                    for d0 in range(0, KC, 2):
                        groups.append((("ffn_out", l, which, h, d0), 2, FH,
                                       [(wsrc("ff%s_out" % which, l, h * FH * 128, FH * 128, d0 * 128, 256), 0, 256)]))
            groups.append((("w_in", l, 0), 4, KC, [(wsrc("w_in", l, 0, D, 0, 512), 0, 512)]))
            for g in range(2):
                c0 = 512 + g * 64
                groups.append((("w_in", l, 4 + g), 1, KC, [(wsrc("w_in", l, 0, D, c0, 64), 0, 64),
                                                           (wsrc("w_in", l, 0, D, c0, 64), 64, 64)]))
            groups.append((("w_in", l, 6), 1, KC, [(wsrc("w_in", l, 0, D, 640, 128), 0, 128)]))
            groups.append((("w_in", l, 7), 4, KC, [(wsrc("w_in", l, 0, D, 768, 512), 0, 512)]))
            for name in ("w_out", "pe_gate"):
                for d0 in range(0, KC, 4):
                    groups.append(((name, l, d0), 4, KC, [(wsrc(name, l, 0, D, d0 * 128, 512), 0, 512)]))
            groups.append((("pe_up", l, 0), 8, 2, [(wsrc("pe_up", l, 0, PE_DIM, 0, 1024), 0, 1024)]))
            groups.append((("w_pool", l), 1, 4, [(W["w_pool"][l].rearrange("g p c -> p g c"), 0, 128)]))
        assert sum(g_[1] for g_ in groups) == len(tile_list)
        s32 = [(xT[:, 0:4, :].rearrange("p a t -> p (a t)"), [("xT", c, tb) for c in range(0, 4) for tb in range(2)]),
               (xT[:, 4:8, :].rearrange("p a t -> p (a t)"), [("xT", c, tb) for c in range(4, 8) for tb in range(2)]),
               (uext[:, :, :].rearrange("p a t -> p (a t)"), [("uext", g) for g in range(4)])]
        s16 = [(hT[:, 0:4, :].rearrange("p a t -> p (a t)"), [("hT", c, tb) for c in range(0, 4) for tb in range(2)]),
               (hT[:, 4:8, :].rearrange("p a t -> p (a t)"), [("hT", c, tb) for c in range(4, 8) for tb in range(2)]),
               (act[:, 0:4, :].rearrange("p a t -> p (a t)"), [("act", c, tb) for c in range(0, 4) for tb in range(2)])]
        cast_engs = ("act", "act", "act")
        for gi, (tid0, ng, kct, srcs) in enumerate(groups if DBG_STOP != 'consts' else []):
            idx0 = tiles[tid0][0]
            st32f, k32 = s32[gi % 3]
            st16f, k16 = s16[gi % 3]
            gw = ng * 128
            ne = kct * gw
            v32 = st32f[:, 0:ne].rearrange("p (k c) -> p k c", c=gw)
            for (sap, o0, cn) in srcs:
                P.op("sp", lambda e, dst=v32[:, :, o0:o0 + cn], sap=sap: e.dma_start(out=dst, in_=sap),
                     writes=k32, dma=True)
            if ng == 1:
                cin = st32f[:, 0:ne]
                cout = st16f[:, 0:ne]
            else:
                cin = st32f[:, 0:ne].rearrange("p (k n c) -> p k n c", k=kct, n=ng)
                cout = st16f[:, 0:ne].rearrange("p (n k c) -> p k n c", n=ng, k=kct)
            ce = cast_engs[gi % 3]
            if ce == "act":
                P.op("act", lambda e, a=cout, b=cin: e.copy(a, b), reads=k32, writes=k16)
            else:
                P.op(ce, lambda e, a=cout, b=cin: e.tensor_copy(a, b), reads=k32, writes=k16)
            te = kct * 128
            dst = scr[idx0:idx0 + ng, :, 0:te].rearrange("n p e -> p n e")
            P.op("sp", lambda e, dst=dst, a=st16f[:, 0:ne].rearrange("p (n e) -> p n e", n=ng): e.dma_start(out=dst, in_=a),
                 reads=k16, writes=[("scr", tile_list[idx0 + j]) for j in range(ng)], dma=True)

        class WStream:
            def __init__(self):
                self.seq = []
                self.pos = 0
                self.loaded = 0

            def load_next(self):
                if self.loaded >= len(self.seq):
                    return
                tid = self.seq[self.loaded]
                s = self.loaded % NSLOT
                self.loaded += 1
                idx, elems = tiles[tid]
                P.op("sp", lambda e, s=s, idx=idx, elems=elems: e.dma_start(out=wring[:, s, 0:elems],
                                                                             in_=scr[idx, :, 0:elems]),
                     reads=[("scr", tid)], writes=[("w", s)], dma=True)

            def start(self):
                for _ in range(NSLOT):
                    self.load_next()

            def get(self, tid):
                assert self.seq[self.pos] == tid, (self.seq[self.pos], tid)
                s = self.pos % NSLOT
                self.pos += 1
                return s, ("w", s)

            def release(self):
                self.load_next()

        WS = WStream()

        def layer_tiles(l):
            out = []
            for which in ("a", "b"):
                ff = []
                for h in range(2):
                    for fi in range(FH):
                        f = h * FH + fi
                        ff.append(("ffn_in", l, which, f))
                        ff.append(("ffn_in", l, which, FC + f))
                    for d in range(KC):
                        ff.append(("ffn_out", l, which, h, d))
                if which == "a":
                    out += ff
                    out += [("w_in", l, n) for n in range(11)]
                    out.append(("w_pool", l))
                    out += [("w_out", l, d) for d in range(KC)]
                else:
                    out += ff
                    for d in range(KC):
                        out.append(("pe_gate", l, d))
                        out.append(("pe_up", l, d))
            return out

        passes = []
        for b in range(nb_p):
            for half in range(2):
                passes.append(dict(kind="p", b=b, half=half, T=TP, tbs=[(0, 512), (512, 512)]))
        if nb_s > 0:
            passes.append(dict(kind="s", T=nb_s * DSEQ, tbs=[(0, nb_s * DSEQ)]))
        for _ in passes:
            for l in range(L):
                WS.seq += layer_tiles(l)
        if DBG_STOP not in ('consts', 'prologue'):
            WS.start()

        def xk(c, tb):
            return ("xT", c, tb)

        def hk(c, tb):
            return ("hT", c, tb)

        def ak(c, tb):
            return ("act", c, tb)

        def norm_sq(l, n, ps_, tbi):
            off, nn = ps_["tbs"][tbi]
            sq = act[:, 0:8, off:off + nn]
            P.op("act", lambda e, sq=sq, off=off, nn=nn: e.activation(sq, xT[:, :, off:off + nn], AF.Square),
                 reads=[xk(c, tbi) for c in range(KC)], writes=[ak(c, tbi) for c in range(8)])

        def norm_rest(l, n, ps_, tbi):
            off, nn = ps_["tbs"][tbi]
            bk, bkk = bank()
            def mm(e, bk=bk, off=off, nn=nn):
                for c in range(KC):
                    ins = e.matmul(bk[:, 0:nn], onesD, act[:, c, off:off + nn], **flags(c, KC))
                return ins
            P.op("pe", mm, reads=[ak(c, tbi) for c in range(8)] + ["cb"], writes=[bkk])
            t1, t1k = tmp()
            P.op("act", lambda e, t1=t1, bk=bk, nn=nn: e.activation(t1[:, 0:nn], bk[:, 0:nn], AF.Ln,
                                                                    bias=epscol[:, 0:1]),
                 reads=[bkk, "neghalf"], writes=[t1k])
            t2, t2k = tmp()
            P.op("act", lambda e, t1=t1, t2=t2, nn=nn: e.activation(t2[:, 0:nn], t1[:, 0:nn], AF.Exp, scale=-0.5),
                 reads=[t1k], writes=[t2k])
            for c in range(KC):
                P.op("dve", lambda e, c=c, t2=t2, off=off, nn=nn: e.scalar_tensor_tensor(
                    hT[:, c, off:off + nn], xT[:, c, off:off + nn], gcol(l, n, c), t2[:, 0:nn], ALU.mult, ALU.mult),
                    reads=[xk(c, tbi), t2k, "small"], writes=[hk(c, tbi)])

        def norm(l, n, ps_):
            for tbi in range(len(ps_["tbs"])):
                norm_sq(l, n, ps_, tbi)
                norm_rest(l, n, ps_, tbi)

        def ffn(l, which, ps_):
            for h in range(2):
                for fi in range(FH):
                    f = h * FH + fi
                    sg_, sgk = WS.get(("ffn_in", l, which, f))
                    su_, suk = WS.get(("ffn_in", l, which, FC + f))
                    for tbi, (off, nn) in enumerate(ps_["tbs"]):
                        bg, bgk = bank()
                        bu, buk = bank()
                        def mm(e, s=sg_, bk=bg, off=off, nn=nn):
                            for k in range(KC):
                                ins = e.matmul(bk[:, 0:nn], wring[:, s, k * 128:(k + 1) * 128], hT[:, k, off:off + nn],
                                               **flags(k, KC))
                            return ins
                        P.op("pe", mm, reads=[sgk] + [hk(c, tbi) for c in range(KC)], writes=[bgk])
                        def mm2(e, s=su_, bk=bu, off=off, nn=nn):
                            for k in range(KC):
                                ins = e.matmul(bk[:, 0:nn], wring[:, s, k * 128:(k + 1) * 128], hT[:, k, off:off + nn],
                                               **flags(k, KC))
                            return ins
                        P.op("pe", mm2, reads=[suk] + [hk(c, tbi) for c in range(KC)], writes=[buk])
                        t1, t1k = tmp()
                        P.op("act", lambda e, t1=t1, bg=bg, nn=nn: e.activation(t1[:, 0:nn], bg[:, 0:nn], AF.Silu),
                             reads=[bgk], writes=[t1k])
                        P.op("dve", lambda e, t1=t1, bu=bu, fi=fi, off=off, nn=nn: e.tensor_tensor(
                            act[:, fi, off:off + nn], t1[:, 0:nn], bu[:, 0:nn], ALU.mult),
                            reads=[t1k, buk], writes=[ak(fi, tbi)])
                    WS.release()
                    WS.release()
                for d in range(KC):
                    so_, sok = WS.get(("ffn_out", l, which, h, d))
                    for tbi, (off, nn) in enumerate(ps_["tbs"]):
                        by, byk = bank()
                        def mm(e, s=so_, bk=by, off=off, nn=nn):
                            for fi in range(FH):
                                ins = e.matmul(bk[:, 0:nn], wring[:, s, fi * 128:(fi + 1) * 128],
                                               act[:, fi, off:off + nn], **flags(fi, FH))
                            return ins
                        P.op("pe", mm, reads=[sok] + [ak(fi, tbi) for fi in range(FH)], writes=[byk])
                        P.op("dve", lambda e, by=by, d=d, off=off, nn=nn: e.scalar_tensor_tensor(
                            xT[:, d, off:off + nn], by[:, 0:nn], 0.5, xT[:, d, off:off + nn], ALU.mult, ALU.add),
                            reads=[byk, xk(d, tbi)], writes=[xk(d, tbi)])
                    WS.release()

        def ffn_pipe(l, which, ps_, hook_mid, it_hook=None):
            tbs = ps_["tbs"]

            def p1_block(fi, tbi, sg_, sgk, su_, suk):
                off, nn = tbs[tbi]
                bg, bgk = bank()
                bu, buk = bank()
                def mm(e, s=sg_, bk=bg, off=off, nn=nn):
                    for k in range(KC):
                        ins = e.matmul(bk[:, 0:nn], wring[:, s, k * 128:(k + 1) * 128], hT[:, k, off:off + nn],
                                       **flags(k, KC))
                    return ins
                P.op("pe", mm, reads=[sgk] + [hk(c, tbi) for c in range(KC)], writes=[bgk])
                def mm2(e, s=su_, bk=bu, off=off, nn=nn):
                    for k in range(KC):
                        ins = e.matmul(bk[:, 0:nn], wring[:, s, k * 128:(k + 1) * 128], hT[:, k, off:off + nn],
                                       **flags(k, KC))
                    return ins
                P.op("pe", mm2, reads=[suk] + [hk(c, tbi) for c in range(KC)], writes=[buk])
                t1, t1k = tmp()
                P.op("act", lambda e, t1=t1, bg=bg, nn=nn: e.activation(t1[:, 0:nn], bg[:, 0:nn], AF.Silu),
                     reads=[bgk], writes=[t1k])
                P.op("dve", lambda e, t1=t1, bu=bu, fi=fi, off=off, nn=nn: e.tensor_tensor(
                    act[:, fi, off:off + nn], t1[:, 0:nn], bu[:, 0:nn], ALU.mult),
                    reads=[t1k, buk], writes=[ak(fi, tbi)])

            def p2_block(d, tbi, so_, sok):
                off, nn = tbs[tbi]
                by, byk = bank()
                def mm(e, s=so_, bk=by, off=off, nn=nn):
                    for fi in range(FH):
                        ins = e.matmul(bk[:, 0:nn], wring[:, s, fi * 128:(fi + 1) * 128],
                                       act[:, fi, off:off + nn], **flags(fi, FH))
                    return ins
                P.op("pe", mm, reads=[sok] + [ak(fi, tbi) for fi in range(FH)], writes=[byk])
                P.op("dve", lambda e, by=by, d=d, off=off, nn=nn: e.scalar_tensor_tensor(
                    xT[:, d, off:off + nn], by[:, 0:nn], 0.5, xT[:, d, off:off + nn], ALU.mult, ALU.add),
                    reads=[byk, xk(d, tbi)], writes=[xk(d, tbi)])

            NH = 4
            itc = [0]
            held = []
            for fi in range(NH):
                g_ = WS.get(("ffn_in", l, which, fi))
                u_ = WS.get(("ffn_in", l, which, FC + fi))
                held.append(g_ + u_)
            for fi in range(NH):
                p1_block(fi, 0, *held[fi])
                if fi == 1:
                    hook_mid()
            for fi in range(NH):
                p1_block(fi, 1, *held[fi])
                WS.release()
                WS.release()
            for h in range(2):
                for fi in range(NH if h == 0 else 0, FH):
                    f = h * FH + fi
                    g_ = WS.get(("ffn_in", l, which, f))
                    u_ = WS.get(("ffn_in", l, which, FC + f))
                    for tbi in range(2):
                        p1_block(fi, tbi, *(g_ + u_))
                    WS.release()
                    WS.release()
                    if it_hook is not None:
                        it_hook(itc[0])
                        itc[0] += 1
                if h == 0:
                    for d in range(KC):
                        o_ = WS.get(("ffn_out", l, which, 0, d))
                        for tbi in range(2):
                            p2_block(d, tbi, *o_)
                        WS.release()
            slots = [WS.get(("ffn_out", l, which, 1, d)) for d in range(KC)]
            tail0 = [(lambda d=d: p2_block(d, 0, *slots[d])) for d in range(KC)]
            def t1f(d):
                p2_block(d, 1, *slots[d])
                WS.release()
            tail1 = [(lambda d=d: t1f(d)) for d in range(KC)]
            return tail0, tail1

        def gate_pipe(l, ps_, hook_mid):
            tbs = ps_["tbs"]

            def block(d, tbi, sg_, sgk, su_, suk):
                off, nn = tbs[tbi]
                bg, bgk = bank()
                bu, buk = bank()
                def mm(e, s=sg_, bk=bg, off=off, nn=nn):
                    for k in range(KC):
                        ins = e.matmul(bk[:, 0:nn], wring[:, s, k * 128:(k + 1) * 128], hT[:, k, off:off + nn],
                                       **flags(k, KC))
                    return ins
                P.op("pe", mm, reads=[sgk] + [hk(c, tbi) for c in range(KC)], writes=[bgk])
                def mm2(e, s=su_, bk=bu, off=off, nn=nn):
                    for k in range(2):
                        ins = e.matmul(bk[:, 0:nn], wring[:, s, k * 128:(k + 1) * 128], peT[:, k, off:off + nn],
                                       **flags(k, 2))
                    return ins
                tts = list(range(off // 128, (off + nn + 127) // 128))
                P.op("pe", mm2, reads=[suk] + [("peT", tt) for tt in tts], writes=[buk])
                t1, t1k = tmp()
                P.op("act", lambda e, t1=t1, bg=bg, nn=nn: e.activation(t1[:, 0:nn], bg[:, 0:nn], AF.Tanh, scale=0.5),
                     reads=[bgk], writes=[t1k])
                t2, t2k = tmp()
                P.op("dve", lambda e, t1=t1, t2=t2, bu=bu, nn=nn: e.scalar_tensor_tensor(
                    t2[:, 0:nn], t1[:, 0:nn], 1.0, bu[:, 0:nn], ALU.add, ALU.mult), reads=[t1k, buk], writes=[t2k])
                P.op("dve", lambda e, t2=t2, d=d, off=off, nn=nn: e.scalar_tensor_tensor(
                    xT[:, d, off:off + nn], t2[:, 0:nn], 0.5, xT[:, d, off:off + nn], ALU.mult, ALU.add),
                    reads=[t2k, xk(d, tbi)], writes=[xk(d, tbi)])

            held = []
            for d in range(4):
                g_ = WS.get(("pe_gate", l, d))
                u_ = WS.get(("pe_up", l, d))
                held.append(g_ + u_)
            for d in range(4):
                block(d, 0, *held[d])
                if d == 1:
                    hook_mid()
            for d in range(4):
                block(d, 1, *held[d])
                WS.release()
                WS.release()
            held2 = []
            for d in range(4, 8):
                g_ = WS.get(("pe_gate", l, d))
                u_ = WS.get(("pe_up", l, d))
                held2.append(g_ + u_)
            tail0 = [(lambda d=d: block(d, 0, *held2[d - 4])) for d in range(4, 8)]
            def t1f(d):
                block(d, 1, *held2[d - 4])
                WS.release()
                WS.release()
            tail1 = [(lambda d=d: t1f(d)) for d in range(4, 8)]
            return tail0, tail1

        def load_x(ps_):
            for tt in range(ps_["T"] // 128):
                load_tile(ps_, tt)

        def load_tile(ps_, tt):
            load_dma(ps_, tt)
            load_tr(ps_, tt)

        def load_dma(ps_, tt):
            sl = tt % 2
            xin = uext[:, sl, 0:1024]
            if ps_["kind"] == "p":
                src = xp[ps_["b"], ps_["half"] * TP + tt * 128: ps_["half"] * TP + (tt + 1) * 128, :]
            else:
                src = xs[tt * 128:(tt + 1) * 128, :]
            P.op("pool", lambda e, xin=xin, src=src: e.dma_start(out=xin, in_=src), writes=[("uext", sl)], dma=True)

        def load_tr(ps_, tt):
            if True:
                sl = tt % 2
                xin = uext[:, sl, 0:1024]
                tbi = (tt * 128) // 512 if ps_["kind"] == "p" else 0
                for hb in range(2):
                    bk, bkk = bank()
                    def mm(e, bk=bk, xin=xin, hb=hb):
                        for c4 in range(4):
                            c = hb * 4 + c4
                            ins = e.transpose(bk[:, c4 * 128:(c4 + 1) * 128], xin[:, c * 128:(c + 1) * 128], ident)
                        return ins
                    P.op("pe", mm, reads=[("uext", sl), "c32"], writes=[bkk])
                    P.op("act", lambda e, bk=bk, hb=hb, tt=tt: e.copy(
                        xT[:, hb * 4:hb * 4 + 4, tt * 128:(tt + 1) * 128], bk[:].rearrange("p (c t) -> p c t", t=128)),
                        reads=[bkk], writes=[xk(hb * 4 + c4, tbi) for c4 in range(4)])

        def store_y(ps_):
            for tt in range(ps_["T"] // 128):
                store_tile(ps_, tt)

        def store_tile(ps_, tt):
            if True:
                sl = 2 + tt % 2
                yst = uext[:, sl, 0:1024]
                tbi = (tt * 128) // 512 if ps_["kind"] == "p" else 0
                for hb in range(2):
                    bk, bkk = bank()
                    def mm(e, bk=bk, hb=hb, tt=tt):
                        for c4 in range(4):
                            c = hb * 4 + c4
                            ins = e.transpose(bk[:, c4 * 128:(c4 + 1) * 128], xT[:, c, tt * 128:(tt + 1) * 128], ident)
                        return ins
                    P.op("pe", mm, reads=[xk(hb * 4 + c4, tbi) for c4 in range(4)] + ["c32"], writes=[bkk])
                    eng = "act" if hb == 0 else "dve"
                    if eng == "act":
                        P.op("act", lambda e, bk=bk, hb=hb, yst=yst: e.copy(yst[:, hb * 512:(hb + 1) * 512], bk[:]),
                             reads=[bkk], writes=[("uext", sl)])
                    else:
                        P.op("dve", lambda e, bk=bk, hb=hb, yst=yst: e.tensor_copy(yst[:, hb * 512:(hb + 1) * 512], bk[:]),
                             reads=[bkk, ("uext", sl)], writes=[("uext", sl)])
                if ps_["kind"] == "p":
                    dst = yp[ps_["b"], ps_["half"] * TP + tt * 128: ps_["half"] * TP + (tt + 1) * 128, :]
                else:
                    dst = ys[tt * 128:(tt + 1) * 128, :]
                P.op("pool", lambda e, yst=yst, dst=dst: e.dma_start(out=dst, in_=yst), reads=[("uext", sl)],
                     writes=[("out", "y", id(ps_), tt)], dma=True)

        def pe_dma(l, ps_, tt):
            sl = tt % 2
            if ps_["kind"] == "p":
                src = pp[l, ps_["b"], ps_["half"] * TP + tt * 128: ps_["half"] * TP + (tt + 1) * 128, :]
            else:
                src = psm[l, tt * 128:(tt + 1) * 128, :]
            P.op("pool", lambda e, sl=sl, src=src: e.dma_start(out=pin[:, sl, :], in_=src), writes=[("pin", sl)],
                 dma=True)

        def pe_tr(l, ps_, tt):
            sl = tt % 2
            bk, bkk = bank()
            def mm(e, bk=bk, sl=sl):
                for j in range(2):
                    ins = e.transpose(bk[:, j * 128:(j + 1) * 128], pin[:, sl, j * 128:(j + 1) * 128], ident)
                return ins
            P.op("pe", mm, reads=[("pin", sl), "c32"], writes=[bkk])
            P.op("act", lambda e, bk=bk, tt=tt: e.copy(peT[:, :, tt * 128:(tt + 1) * 128],
                                                       bk[:, 0:256].rearrange("p (c t) -> p c t", t=128)),
                 reads=[bkk], writes=[("peT", tt)])

        def load_pe(l, ps_):
            for tt in range(ps_["T"] // 128):
                pe_dma(l, ps_, tt)
                pe_tr(l, ps_, tt)

        def segs_of(ps_):
            if ps_["kind"] == "p":
                return [dict(b=ps_["b"], t0=0, Ls=TP, kbase=0, vbase=0, ubase=0,
                             hist=("none" if ps_["half"] == 0 else "carry"))]
            return [dict(b=s, t0=s * DSEQ, Ls=DSEQ, kbase=s * 192, vbase=2 * s, ubase=s * 79, hist="cache")
                    for s in range(nb_s)]

        def kd_keys_own(ps_, g, tbi):
            if ps_["kind"] == "p":
                return [("kdup", g, 1 + tbi * 4 + i) for i in range(4)]
            return [("kdup", g, 2 * s + 1) for s in range(nb_s)]

        def mixer(l, ps_, hook_mid=None, defer_wout=False):
            T_ = ps_["T"]
            isp = ps_["kind"] == "p"
            segs = segs_of(ps_)
            S = len(segs)
            Ls = segs[0]["Ls"]
            UW = 15 + Ls
            for s, sg in enumerate(segs):
                ub = sg["ubase"]
                if sg["hist"] == "none":
                    P.op("pool", lambda e, ub=ub: e.memset(uext[:, :, ub:ub + 15], 0.0),
                         writes=[("uext", g) for g in range(4)])
                elif sg["hist"] == "carry":
                    P.op("pool", lambda e: e.tensor_copy(kdup[:, :, 0:128], kcarry[:, l, :, :]),
                         reads=[("kcarry", l)], writes=[("kdup", 0, 0), ("kdup", 1, 0)])
                    P.op("pool", lambda e: e.tensor_copy(vpad[:, 0, :], vcarry[:, l, :]),
                         reads=[("vcarry", l)], writes=[("vpad", 0)])
                    P.op("pool", lambda e, ub=ub: e.tensor_copy(uext[:, :, ub:ub + 15], ucarry[:, l, :, :]),
                         reads=[("ucarry", l)], writes=[("uext", g) for g in range(4)])
                else:
                    b = sg["b"]
                    kb = sg["kbase"]
                    vb = sg["vbase"]
                    srck = ck[l, b].rearrange("r (g d) -> r g d", g=2)
                    P.op("pool", lambda e, srck=srck: e.dma_start(
                        out=kvstage[:, 0, :].rearrange("r (g c) -> r g c", g=2)[:, :, 0:64], in_=srck),
                        writes=["kvs0"], dma=True)
                    P.op("pool", lambda e, srck=srck: e.dma_start(
                        out=kvstage[:, 0, :].rearrange("r (g c) -> r g c", g=2)[:, :, 64:128], in_=srck),
                        writes=["kvs0b"], dma=True)
                    bk, bkk = bank()
                    def mm(e, bk=bk):
                        for g in range(2):
                            ins = e.transpose(bk[:, g * 128:(g + 1) * 128], kvstage[:, 0, g * 128:(g + 1) * 128], ident)
                        return ins
                    P.op("pe", mm, reads=["kvs0", "kvs0b", "c32"], writes=[bkk])
                    P.op("act", lambda e, bk=bk, kb=kb: e.copy(kdup[:, :, kb:kb + 128],
                                                               bk[:, 0:256].rearrange("p (g t) -> p g t", g=2)),
                         reads=[bkk], writes=[("kdup", 0, 2 * s), ("kdup", 1, 2 * s)])
                    P.op("pool", lambda e, b=b: e.dma_start(out=kvstage[:, 1, 0:128], in_=cv[l, b]),
                         writes=["kvs1"], dma=True)
                    vin = kvstage[:, 1, 0:128].rearrange("r (g d) -> r g d", g=2)
                    P.op("act", lambda e, vb=vb, vin=vin: e.copy(
                        vpad[:, vb, :].rearrange("r (g c) -> r g c", g=2)[:, :, 0:64], vin),
                        reads=["kvs1"], writes=[("vpad", vb)])
                    P.op("act", lambda e, vb=vb, vin=vin: e.copy(
                        vpad[:, vb, :].rearrange("r (g c) -> r g c", g=2)[:, :, 192:256], vin),
                        reads=["kvs1", ("vpad", vb)], writes=[("vpad", vb)])
                    P.op("pool", lambda e, b=b: e.dma_start(out=pstage[0:15, 0, :], in_=spool[l, b]),
                         writes=["pst0"], dma=True)
                    bk2, bk2k = bank()
                    def mm2(e, bk2=bk2):
                        for g in range(4):
                            ins = e.transpose(bk2[:, g * 16:g * 16 + 15], pstage[0:15, 0, g * 128:(g + 1) * 128],
                                              ident[0:15, 0:15])
                        return ins
                    P.op("pe", mm2, reads=["pst0", "c32"], writes=[bk2k])
                    P.op("act", lambda e, bk2=bk2, ub=ub: e.copy(
                        uext[:, :, ub:ub + 15], bk2[:, 0:64].rearrange("p (g t) -> p g t", g=4)[:, :, 0:15]),
                        reads=[bk2k], writes=[("uext", g) for g in range(4)])
                    P.op("pool", lambda e, b=b: e.dma_start(out=nks[l, b, 0:64, :], in_=ck[l, b, 64:128, :]),
                         writes=[("out", "nks0", l, b)], dma=True)
                    P.op("pool", lambda e, b=b: e.dma_start(out=nvs[l, b, 0:64, :], in_=cv[l, b, 64:128, :]),
                         writes=[("out", "nvs0", l, b)], dma=True)

            ckpt('m_hist')
            if hook_mid is None:
                items = [(n, tbi, off, nn) for n in range(6) for tbi, (off, nn) in enumerate(ps_["tbs"])]
            else:
                items = [(n, tbi, off, nn) for tbi, (off, nn) in enumerate(ps_["tbs"]) for n in range(6)]
            qk_state = {}
            qk_slots = {}

            def qk_S1(i):
                n, tbi, off, nn = items[i]
                if tbi == 0:
                    qk_slots[n] = WS.get(("w_in", l, n))
                sw, swk = qk_slots[n]
                bz, bzk = bank()
                def mm(e, s=sw, bk=bz, off=off, nn=nn):
                    for k in range(KC):
                        ins = e.matmul(bk[:, 0:nn], wring[:, s, k * 128:(k + 1) * 128], hT[:, k, off:off + nn],
                                       **flags(k, KC))
                    return ins
                P.op("pe", mm, reads=[swk] + [hk(c, tbi) for c in range(KC)], writes=[bzk])
                zg, zgk = tmp()
                wh = 0 if n < 4 else 1
                P.op("dve", lambda e, zg=zg, bz=bz, nn=nn, wh=wh: e.tensor_scalar(zg[:, 0:nn], bz[:, 0:nn], qkg(l, wh),
                                                                                  None, ALU.mult),
                     reads=[bzk, "small"], writes=[zgk])
                sl = i % 2
                sqh = dT[:, sl, 0:nn]
                P.op("act", lambda e, sqh=sqh, bz=bz, nn=nn: e.activation(sqh, bz[:, 0:nn], AF.Square),
                     reads=[bzk], writes=[("dT", sl)])
                if tbi == len(ps_["tbs"]) - 1:
                    WS.release()
                qk_state[i] = (zg, zgk, sl, state["tmpcount"])

            def qk_S2(i):
                n, tbi, off, nn = items[i]
                zg, zgk, sl, cnt0 = qk_state.pop(i)
                assert state["tmpcount"] - cnt0 < NTMP, (state["tmpcount"], cnt0)
                sqh = dT[:, sl, 0:nn]
                a1, a1k = tmp()
                P.op("pool", lambda e, a1=a1, zg=zg, off=off, nn=nn: e.tensor_tensor(
                    a1[:, 0:nn], zg[:, 0:nn], ropeC[:, off:off + nn], ALU.mult), reads=[zgk, "ropeC"], writes=[a1k])
                br, brk = bank()
                P.op("pe", lambda e, br=br, zg=zg, nn=nn: e.matmul(br[:, 0:nn], Rm, zg[:, 0:nn], start=True, stop=True),
                     reads=[zgk, "c32"], writes=[brk])
                bs, bsk = bank()
                P.op("pe", lambda e, bs=bs, sqh=sqh, nn=nn: e.matmul(bs[:, 0:nn], blk1, sqh, start=True, stop=True),
                     reads=[("dT", sl), "cb"], writes=[bsk])
                t1, t1k = tmp()
                P.op("act", lambda e, t1=t1, bs=bs, nn=nn: e.activation(t1[:, 0:nn], bs[:, 0:nn], AF.Ln,
                                                                        bias=epscol[:, 0:1]),
                     reads=[bsk, "neghalf"], writes=[t1k])
                a2, a2k = tmp()
                P.op("dve", lambda e, a2=a2, br=br, off=off, nn=nn: e.tensor_tensor(
                    a2[:, 0:nn], br[:, 0:nn], ropeS[:, off:off + nn], ALU.mult), reads=[brk, "ropeS"], writes=[a2k])
                P.op("dve", lambda e, a1=a1, a2=a2, nn=nn: e.tensor_tensor(
                    a2[:, 0:nn], a1[:, 0:nn], a2[:, 0:nn], ALU.add), reads=[a1k, a2k], writes=[a2k])
                t2, t2k = tmp()
                P.op("act", lambda e, t1=t1, t2=t2, nn=nn: e.activation(t2[:, 0:nn], t1[:, 0:nn], AF.Exp, scale=-0.5),
                     reads=[t1k], writes=[t2k])
                if n < 4:
                    P.op("dve", lambda e, a2=a2, t2=t2, n=n, off=off, nn=nn: e.tensor_tensor(
                        act[:, 8 + n, off:off + nn], a2[:, 0:nn], t2[:, 0:nn], ALU.mult),
                        reads=[a2k, t2k], writes=[ak(8 + n, tbi)])
                else:
                    g = n - 4
                    a3, a3k = tmp()
                    P.op("dve", lambda e, a2=a2, t2=t2, a3=a3, nn=nn: e.tensor_tensor(
                        a3[:, 0:nn], a2[:, 0:nn], t2[:, 0:nn], ALU.mult), reads=[a2k, t2k], writes=[a3k])
                    if isp:
                        P.op("act", lambda e, a3=a3, g=g, off=off, nn=nn: e.copy(
                            kdup[:, g, 128 + off:128 + off + nn], a3[:, 0:nn]),
                            reads=[a3k], writes=kd_keys_own(ps_, g, tbi))
                        if ps_["half"] == 1 and tbi == 1:
                            P.op("act", lambda e, a3=a3, g=g: e.copy(krot32[:, g, 0:128], a3[:, 384:512]),
                                 reads=[a3k], writes=[("krot32", g)])
                    else:
                        P.op("act", lambda e, a3=a3, g=g, nn=nn: e.copy(
                            kdup[:, g, 0:S * 192].rearrange("p (s c) -> p s c", c=192)[:, :, 128:192],
                            a3[:, 0:nn].rearrange("p (s c) -> p s c", c=64)),
                            reads=[a3k], writes=kd_keys_own(ps_, g, tbi))
                        P.op("act", lambda e, a3=a3, g=g, nn=nn: e.copy(krot32[:, g, 0:nn], a3[:, 0:nn]),
                             reads=[a3k], writes=[("krot32", g)])

            for i in range(len(items) + 1):
                if hook_mid is not None and i == 3:
                    hook_mid()
                if i < len(items):
                    qk_S1(i)
                if i >= 1:
                    qk_S2(i - 1)

            ckpt('m_qk_done')
            ckpt('m_qk')
            sw, swk = WS.get(("w_in", l, 6))
            ktiles = []
            for s, sg in enumerate(segs):
                if isp:
                    for i in range(Ls // 128):
                        ktiles.append((s, i * 128, 128, 1 + i))
                else:
                    ktiles.append((s, sg["t0"], DSEQ, sg["vbase"] + 1))
            for q0 in range(0, len(ktiles), 4):
                grp = ktiles[q0:q0 + 4]
                bv, bvk = bank()
                def mm(e, s=sw, bv=bv, grp=grp):
                    for gi, (_, toff, ksz, _) in enumerate(grp):
                        for k in range(KC):
                            ins = e.matmul(bv[0:ksz, gi * 128:(gi + 1) * 128], hT[:, k, toff:toff + ksz],
                                           wring[:, s, k * 128:(k + 1) * 128], **flags(k, KC))
                    return ins
                tbset = sorted(set(((toff // 512) if isp else 0) for (_, toff, _, _) in grp))
                P.op("pe", mm, reads=[swk] + [hk(c, tbi) for c in range(KC) for tbi in tbset], writes=[bvk])
                for gi, (s, toff, ksz, vidx) in enumerate(grp if not os.environ.get('KNO_VCOPY') else []):
                    vin = bv[0:ksz, gi * 128:(gi + 1) * 128].rearrange("r (g d) -> r g d", g=2)
                    P.op("act", lambda e, vin=vin, ksz=ksz, vidx=vidx: e.copy(
                        vpad[0:ksz, vidx, :].rearrange("r (g c) -> r g c", g=2)[:, :, 0:64], vin),
                        reads=[bvk], writes=[("vpad", vidx)])
                    P.op("act", lambda e, vin=vin, ksz=ksz, vidx=vidx: e.copy(
                        vpad[0:ksz, vidx, :].rearrange("r (g c) -> r g c", g=2)[:, :, 192:256], vin),
                        reads=[bvk, ("vpad", vidx)], writes=[("vpad", vidx)])
                    sg = segs[s]
                    last = (toff + ksz == sg["t0"] + Ls) and (not isp or ps_["half"] == 1) and not os.environ.get('KNO_VOUT')
                    if last:
                        P.op("dve", lambda e, bv=bv, gi=gi, ksz=ksz: e.tensor_copy(
                            ostage[0:ksz, 1, :], bv[0:ksz, gi * 128:(gi + 1) * 128]), reads=[bvk], writes=["ost1"])
                        if isp:
                            dst = nvp[l, sg["b"], :, :]
                        else:
                            dst = nvs[l, sg["b"], 64:128, :]
                        P.op("pool", lambda e, dst=dst, ksz=ksz: e.dma_start(out=dst, in_=ostage[0:ksz, 1, :]),
                             reads=["ost1"], writes=[("out", "nv", l, sg["b"], isp)], dma=True)
            WS.release()

            ckpt('m_v')
            for g in range(4):
                sw, swk = WS.get(("w_in", l, 7 + g))
                for tbi, (off, nn) in enumerate(ps_["tbs"]):
                    bz, bzk = bank()
                    def mm(e, s=sw, bk=bz, off=off, nn=nn):
                        for k in range(KC):
                            ins = e.matmul(bk[:, 0:nn], wring[:, s, k * 128:(k + 1) * 128], hT[:, k, off:off + nn],
                                           **flags(k, KC))
                        return ins
                    P.op("pe", mm, reads=[swk] + [hk(c, tbi) for c in range(KC)], writes=[bzk])
                    if isp:
                        P.op("dve", lambda e, bz=bz, g=g, off=off, nn=nn: e.tensor_copy(
                            uext[:, g, 15 + off:15 + off + nn], bz[:, 0:nn]),
                            reads=[bzk], writes=[("uext", g)])
                    else:
                        P.op("act", lambda e, bz=bz, g=g, nn=nn: e.copy(
                            uext[:, g, 0:S * 79].rearrange("p (s c) -> p s c", c=79)[:, :, 15:79],
                            bz[:, 0:nn].rearrange("p (s c) -> p s c", c=64)),
                            reads=[bzk], writes=[("uext", g)])
                WS.release()

            U3 = [uext[:, g, 0:S * UW].rearrange("p (s c) -> p s c", c=UW) for g in range(4)]
            A3 = tmpA[:, 0:S * UW].rearrange("p (s c) -> p s c", c=UW)
            B3 = tmpB[:, 0:S * UW].rearrange("p (s c) -> p s c", c=UW)
            def pool_group(g, w):
                cur, curk = U3[g], ("uext", g)
                width = 1
                bufs = [(A3, "tmpA"), (B3, "tmpB")]
                bi = 0
                while width < w:
                    nxt, nxtk = bufs[bi]
                    bi ^= 1
                    lo = 2 * width - 1
                    P.op("dve", lambda e, nxt=nxt, cur=cur, lo=lo, width=width: e.tensor_tensor(
                        nxt[:, :, lo:UW], cur[:, :, lo:UW], cur[:, :, lo - width:UW - width], ALU.add),
                        reads=[curk], writes=[nxtk])
                    cur, curk = nxt, nxtk
                    width *= 2
                dslot = g
                dview = dT[:, dslot, 0:T_].rearrange("p (s c) -> p s c", c=Ls)
                if isp and ps_["half"] == 0:
                    t1, t1k = tmp()
                    P.op("dve", lambda e, t1=t1, cur=cur, g=g: e.tensor_tensor(
                        t1[:, 0:15], cur[:, 0, 15:30], invcnt[:, g * 15:(g + 1) * 15], ALU.mult),
                        reads=[curk, "c32"], writes=[t1k])
                P.op("act", lambda e, cur=cur, w=w: e.activation(
                    cur[:, :, 15:UW], cur[:, :, 15:UW], AF.Copy, scale=1.0 / w), reads=[curk], writes=[curk])
                P.op("dve", lambda e, dview=dview, cur=cur, g=g: e.tensor_tensor(
                    dview, cur[:, :, 15:UW], U3[g][:, :, 15:UW], ALU.subtract),
                    reads=[curk, ("uext", g)], writes=[("dT", dslot)])
                if isp and ps_["half"] == 0:
                    P.op("dve", lambda e, t1=t1, g=g, dslot=dslot: e.tensor_tensor(
                        dT[:, dslot, 0:15], t1[:, 0:15], uext[:, g, 15:30], ALU.subtract),
                        reads=[t1k, ("uext", g), ("dT", dslot)], writes=[("dT", dslot)])

            pool_pending = [(g, w) for g, w in enumerate((2, 4, 8, 16))]
            ckpt('m_u')
            qtiles = []
            for s, sg in enumerate(segs):
                if isp:
                    for m in range(Ls // 128):
                        prev = None
                        if m > 0:
                            prev = (128 + (m - 1) * 128, m, 128, m, "P")
                        elif sg["hist"] == "carry":
                            prev = (0, 0, 128, 0, "P")
                        own = (128 + m * 128, 1 + m, 128, 1 + m, "O")
                        qtiles.append(dict(qoff=m * 128, qsz=128, prev=prev, own=own, mask=True))
                else:
                    kb = sg["kbase"]
                    qtiles.append(dict(qoff=sg["t0"], qsz=DSEQ, prev=(kb, sg["vbase"], 128, 2 * s, "P"),
                                       own=(kb + 128, sg["vbase"] + 1, DSEQ, 2 * s + 1, "O"), mask=False))
            units = [(qi, g) for qi in range(len(qtiles)) for g in range(2)]
            ustate = {}

            def att_A(ui):
                qi, g = units[ui]
                qt = qtiles[qi]
                qoff, qsz = qt["qoff"], qt["qsz"]
                tbq = (qoff // 512) if isp else 0
                kts = [k_ for k_ in (qt["prev"], qt["own"]) if k_ is not None]
                pbi = ui % 3
                pb = pbufs[pbi]
                pbk = ("pbuf", pbi)
                for ki, (kcol, vidx, ksz, kidx, kind) in enumerate(kts):
                    bi, bSk = bank_pair()
                    def mm(e, bi=bi, g=g, kcol=kcol, ksz=ksz, qoff=qoff, qsz=qsz):
                        for hh in range(4):
                            head = g * 4 + hh
                            j = head // 2
                            par = head % 2
                            p0 = 64 * par
                            c0 = (bi + par) * 512 + (hh // 2) * qsz
                            ins = e.matmul(psum_all[0:ksz, c0:c0 + qsz],
                                           kdup[p0:p0 + 64, g, kcol:kcol + ksz],
                                           act[p0:p0 + 64, 8 + j, qoff:qoff + qsz], start=True, stop=True)
                        return ins
                    P.op("pe", mm, reads=[("kdup", g, kidx), ak(8 + 2 * g, tbq), ak(9 + 2 * g, tbq)], writes=bSk)
                    src3 = psum_all[0:ksz, bi * 512:(bi + 2) * 512].rearrange("p (a c) -> p a c", a=2)[:, :, 0:2 * qsz]
                    dst3 = pb[0:ksz, ki, 0:4 * qsz].rearrange("p (a c) -> p a c", a=2)
                    if not qt["mask"]:
                        P.op("act", lambda e, src3=src3, dst3=dst3: e.activation(dst3, src3, AF.Exp, scale=0.125),
                             reads=bSk, writes=[pbk])
                    else:
                        for qh in range(2):
                            if kind == "P":
                                bcol = 316 if qh == 0 else 317
                            else:
                                bcol = 318 if qh == 0 else 316
                            s4 = src3.rearrange("p a (h q) -> p a h q", h=2)[:, :, :, qh * 64:(qh + 1) * 64]
                            d4 = dst3.rearrange("p a (h q) -> p a h q", h=2)[:, :, :, qh * 64:(qh + 1) * 64]
                            P.op("act", lambda e, s4=s4, d4=d4, bcol=bcol: e.activation(
                                d4, s4, AF.Exp, scale=0.125, bias=c32[:, bcol:bcol + 1]),
                                reads=bSk + ["c32"] + ([pbk] if qh else []), writes=[pbk])
                ustate[ui] = (pb, pbk, kts)

            def att_B(ui):
                qi, g = units[ui]
                qt = qtiles[qi]
                qoff, qsz = qt["qoff"], qt["qsz"]
                tbq = (qoff // 512) if isp else 0
                pb, pbk, kts = ustate.pop(ui)
                acc = 4 + 2 * (qi % 2)
                bN, bNk = banks[acc], ("bank", acc)
                bD, bDk = banks[acc + 1], ("bank", acc + 1)
                for jj in range(2):
                    j = g * 2 + jj
                    def mmN(e, bN=bN, pb=pb, g=g, jj=jj, j=j, kts=kts, qsz=qsz):
                        nmm = 2 * len(kts)
                        i = 0
                        for ki, (kcol, vidx, ksz, kidx, kind) in enumerate(kts):
                            for ab in range(2):
                                hh = 2 * jj + ab
                                pc = (hh % 2) * 2 * qsz + (hh // 2) * qsz
                                ins = e.matmul(bN[:, j * qsz:(j + 1) * qsz],
                                               vpad[0:ksz, vidx, g * 256 + ab * 128:g * 256 + (ab + 1) * 128],
                                               pb[0:ksz, ki, pc:pc + qsz], **flags(i, nmm))
                                i += 1
                        return ins
                    P.op("pe", mmN, reads=[pbk] + [("vpad", k_[1]) for k_ in kts], writes=[bNk])
                    def mmD(e, bD=bD, pb=pb, jj=jj, j=j, kts=kts, qsz=qsz):
                        nmm = 2 * len(kts)
                        i = 0
                        for ki, (kcol, vidx, ksz, kidx, kind) in enumerate(kts):
                            for ab in range(2):
                                hh = 2 * jj + ab
                                pc = (hh % 2) * 2 * qsz + (hh // 2) * qsz
                                ins = e.matmul(bD[:, j * qsz:(j + 1) * qsz],
                                               (onesA if ab == 0 else onesB)[0:ksz, :],
                                               pb[0:ksz, ki, pc:pc + qsz], **flags(i, nmm))
                                i += 1
                        return ins
                    P.op("pe", mmD, reads=[pbk, "cb"], writes=[bDk])
                if g == 1:
                    t1, t1k = tmp()
                    nq = 4 * qsz
                    P.op("dve", lambda e, t1=t1, bD=bD, qsz=qsz, nq=nq: e.tensor_tensor(
                        t1[:, 0:nq].rearrange("p (j q) -> p j q", j=4), bD[:, 0:nq].rearrange("p (j q) -> p j q", j=4),
                        esf[:, l, :, 0:qsz], ALU.add), reads=[bDk, "esf"], writes=[t1k])
                    t3, t3k = tmp()
                    P.op("act", lambda e, t1=t1, t3=t3, nq=nq: e.activation(t3[:, 0:nq], t1[:, 0:nq], AF.Ln),
                         reads=[t1k], writes=[t3k])
                    t2, t2k = tmp()
                    P.op("act", lambda e, t3=t3, t2=t2, nq=nq: e.activation(t2[:, 0:nq], t3[:, 0:nq], AF.Exp, scale=-1.0),
                         reads=[t3k], writes=[t2k])
                    P.op("dve", lambda e, t2=t2, bN=bN, qoff=qoff, qsz=qsz, nq=nq: e.tensor_tensor(
                        act[:, 0:4, qoff:qoff + qsz], bN[:, 0:nq].rearrange("p (j q) -> p j q", j=4),
                        t2[:, 0:nq].rearrange("p (j q) -> p j q", j=4), ALU.mult),
                        reads=[bNk, t2k], writes=[ak(c, tbq) for c in range(4)])

            SK = 2
            state["nrot"] = 4
            for i in range(len(units) + SK):
                if i < len(units):
                    att_A(i)
                if i >= SK:
                    att_B(i - SK)
                if i >= 1 and i % 3 == 1 and pool_pending:
                    pool_group(*pool_pending.pop(0))
            while pool_pending:
                pool_group(*pool_pending.pop(0))
            state["nrot"] = NROT_OUT

            ckpt('m_attn')
            if isp and ps_["half"] == 0:
                P.op("pool", lambda e: e.tensor_copy(kcarry[:, l, :, :], kdup[:, :, 128 + 896:128 + 1024]),
                     reads=[("kdup", 0, 8), ("kdup", 1, 8)], writes=[("kcarry", l)])
                P.op("pool", lambda e: e.tensor_copy(vcarry[:, l, :], vpad[:, 8, :]),
                     reads=[("vpad", 8)], writes=[("vcarry", l)])
            else:
                for s, sg in enumerate(segs):
                    nrow = 128 if isp else DSEQ
                    c0 = 0 if isp else sg["t0"]
                    bk, bkk = bank()
                    def mm(e, bk=bk, c0=c0, nrow=nrow):
                        for g in range(2):
                            ins = e.transpose(bk[0:nrow, g * 128:(g + 1) * 128], krot32[:, g, c0:c0 + nrow], ident)
                        return ins
                    P.op("pe", mm, reads=[("krot32", 0), ("krot32", 1), "c32"], writes=[bkk])
                    P.op("dve", lambda e, bk=bk, nrow=nrow: e.tensor_copy(
                        ostage[0:nrow, 0, :].rearrange("r (g d) -> r g d", g=2),
                        bk[0:nrow, 0:256].rearrange("r (g c) -> r g c", g=2)[:, :, 0:64]),
                        reads=[bkk], writes=["ost0"])
                    dst = nkp[l, sg["b"], :, :] if isp else nks[l, sg["b"], 64:128, :]
                    P.op("pool", lambda e, dst=dst, nrow=nrow: e.dma_start(out=dst, in_=ostage[0:nrow, 0, :]),
                         reads=["ost0"], writes=[("out", "nk", l, sg["b"], isp)], dma=True)

            ckpt('m_kout')
            swp, swpk = WS.get(("w_pool", l))
            for g in range(4):
                dslot = g
                for tbi, (off, nn) in enumerate(ps_["tbs"]):
                    bz, bzk = bank()
                    P.op("pe", lambda e, bz=bz, g=g, dslot=dslot, off=off, nn=nn, s=swp: e.matmul(
                        bz[:, 0:nn], wring[:, s, g * 128:(g + 1) * 128], dT[:, dslot, off:off + nn], start=True, stop=True),
                        reads=[swpk, ("dT", dslot)], writes=[bzk])
                    P.op("dve", lambda e, bz=bz, g=g, off=off, nn=nn: e.tensor_scalar(
                        act[:, 4 + g, off:off + nn], bz[:, 0:nn], pscale(l, g), None, ALU.mult),
                        reads=[bzk, "small"], writes=[ak(4 + g, tbi)])
            WS.release()
            if isp and ps_["half"] == 0:
                P.op("pool", lambda e: e.tensor_copy(ucarry[:, l, :, :], uext[:, :, TP:TP + 15]),
                     reads=[("uext", g) for g in range(4)], writes=[("ucarry", l)])
            else:
                for s, sg in enumerate(segs):
                    ub = sg["ubase"]
                    bk, bkk = bank()
                    def mm(e, bk=bk, ub=ub):
                        for g in range(4):
                            ins = e.transpose(bk[0:15, g * 128:(g + 1) * 128], uext[:, g, ub + Ls:ub + Ls + 15], ident)
                        return ins
                    P.op("pe", mm, reads=[("uext", g) for g in range(4)] + ["c32"], writes=[bkk])
                    P.op("dve", lambda e, bk=bk: e.tensor_copy(pstage[0:15, 1, :], bk[0:15, :]), reads=[bkk], writes=["pst1"])
                    dst = npp[l, sg["b"]] if isp else nps[l, sg["b"]]
                    P.op("pool", lambda e, dst=dst: e.dma_start(out=dst, in_=pstage[0:15, 1, :]), reads=["pst1"],
                         writes=[("out", "np", l, sg["b"], isp)], dma=True)

            ckpt('m_pool')
            def wo_block(d, tbi, sw, swk):
                off, nn = ps_["tbs"][tbi]
                bz, bzk = bank()
                def mm(e, s=sw, bk=bz, off=off, nn=nn):
                    for k in range(KC):
                        ins = e.matmul(bk[:, 0:nn], wring[:, s, k * 128:(k + 1) * 128], act[:, k, off:off + nn],
                                       **flags(k, KC))
                    return ins
                P.op("pe", mm, reads=[swk] + [ak(c, tbi) for c in range(KC)], writes=[bzk])
                P.op("dve", lambda e, bz=bz, d=d, off=off, nn=nn: e.tensor_tensor(
                    xT[:, d, off:off + nn], bz[:, 0:nn], xT[:, d, off:off + nn], ALU.add),
                    reads=[bzk, xk(d, tbi)], writes=[xk(d, tbi)])

            if not defer_wout:
                for d in range(KC):
                    sl_ = WS.get(("w_out", l, d))
                    for tbi in range(len(ps_["tbs"])):
                        wo_block(d, tbi, *sl_)
                    WS.release()
                return None
            wslots = [WS.get(("w_out", l, d)) for d in range(KC)]
            tail0 = [(lambda d=d: wo_block(d, 0, *wslots[d])) for d in range(KC)]
            def wt1(d):
                wo_block(d, 1, *wslots[d])
                WS.release()
            tail1 = [(lambda d=d: wt1(d)) for d in range(KC)]
            return tail0, tail1

        def pe_gate(l, ps_):
            for d in range(KC):
                sg_, sgk = WS.get(("pe_gate", l, d))
                su_, suk = WS.get(("pe_up", l, d))
                for tbi, (off, nn) in enumerate(ps_["tbs"]):
                    bg, bgk = bank()
                    bu, buk = bank()
                    def mm(e, s=sg_, bk=bg, off=off, nn=nn):
                        for k in range(KC):
                            ins = e.matmul(bk[:, 0:nn], wring[:, s, k * 128:(k + 1) * 128], hT[:, k, off:off + nn],
                                           **flags(k, KC))
                        return ins
                    P.op("pe", mm, reads=[sgk] + [hk(c, tbi) for c in range(KC)], writes=[bgk])
                    def mm2(e, s=su_, bk=bu, off=off, nn=nn):
                        for k in range(2):
                            ins = e.matmul(bk[:, 0:nn], wring[:, s, k * 128:(k + 1) * 128], peT[:, k, off:off + nn],
                                           **flags(k, 2))
                        return ins
                    tts = list(range(off // 128, (off + nn + 127) // 128))
                    P.op("pe", mm2, reads=[suk] + [("peT", tt) for tt in tts], writes=[buk])
                    t1, t1k = tmp()
                    P.op("act", lambda e, t1=t1, bg=bg, nn=nn: e.activation(t1[:, 0:nn], bg[:, 0:nn], AF.Tanh, scale=0.5),
                         reads=[bgk], writes=[t1k])
                    t2, t2k = tmp()
                    P.op("dve", lambda e, t1=t1, t2=t2, bu=bu, nn=nn: e.scalar_tensor_tensor(
                        t2[:, 0:nn], t1[:, 0:nn], 1.0, bu[:, 0:nn], ALU.add, ALU.mult), reads=[t1k, buk], writes=[t2k])
                    P.op("dve", lambda e, t2=t2, d=d, off=off, nn=nn: e.scalar_tensor_tensor(
                        xT[:, d, off:off + nn], t2[:, 0:nn], 0.5, xT[:, d, off:off + nn], ALU.mult, ALU.add),
                        reads=[t2k, xk(d, tbi)], writes=[xk(d, tbi)])
                WS.release()
                WS.release()

        def load_rope(ps_):
            T_ = ps_["T"]
            if ps_["kind"] == "p":
                c0 = ps_["half"] * TP
                P.op("pool", lambda e, c0=c0: e.dma_start(out=ropeC[:, 0:TP], in_=rope_p_d[0, :, c0:c0 + TP]),
                     writes=["ropeC"], dma=True)
                P.op("pool", lambda e, c0=c0: e.dma_start(out=ropeS[:, 0:TP], in_=rope_p_d[1, :, c0:c0 + TP]),
                     writes=["ropeS"], dma=True)
            else:
                P.op("pool", lambda e, T_=T_: e.dma_start(out=ropeC[:, 0:T_], in_=rope_s_d[0, :, 0:T_]),
                     writes=["ropeC"], dma=True)
                P.op("pool", lambda e, T_=T_: e.dma_start(out=ropeS[:, 0:T_], in_=rope_s_d[1, :, 0:T_]),
                     writes=["ropeS"], dma=True)

        pipelined_entry = False
        for pidx, ps_ in enumerate(passes if DBG_STOP not in ("consts", "prologue") else []):
            nxt = passes[pidx + 1] if pidx + 1 < len(passes) else None
            if nxt is not None and (nxt["kind"] != "p" or DBG_STOP or os.environ.get("KNOPIPE") or os.environ.get("KNOXPIPE")):
                nxt = None
            T_ = ps_["T"]
            if not pipelined_entry:
                load_rope(ps_)
            try:
                if ps_["kind"] == "s" or DBG_STOP or os.environ.get("KNOPIPE"):
                    load_x(ps_)
                    ckpt("load_x")
                    for l in range(L):
                        load_pe(l, ps_)
                        ckpt("load_pe")
                        norm(l, 0, ps_)
                        ckpt("norm0")
                        ffn(l, "a", ps_)
                        ckpt("ffna")
                        norm(l, 1, ps_)
                        mixer(l, ps_)
                        ckpt("mixer")
                        norm(l, 2, ps_)
                        ffn(l, "b", ps_)
                        norm(l, 3, ps_)
                        pe_gate(l, ps_)
                        ckpt("layer")
                    store_y(ps_)
                else:
                    phases = []
                    for l in range(L):
                        phases += [("ffn", l, "a", 0), ("mix", l, None, 1), ("ffn", l, "b", 2), ("gate", l, None, 3)]
                    if not pipelined_entry:
                        load_x(ps_)
                        norm(0, 0, ps_)
                    for pi, (kind, l, which, n) in enumerate(phases):
                        if pi > 0 or pipelined_entry:
                            hook = (lambda l=l, n=n: norm_rest(l, n, ps_, 1))
                        else:
                            hook = (lambda: None)
                        if kind == "ffn":
                            ith = None
                            if which == "b":
                                def ith(it, l=l):
                                    if it < 8:
                                        pe_dma(l, ps_, it)
                                    if 1 <= it <= 8:
                                        pe_tr(l, ps_, it - 1)
                            t0, t1 = ffn_pipe(l, which, ps_, hook, ith)
                        elif kind == "mix":
                            t0, t1 = mixer(l, ps_, hook_mid=hook, defer_wout=True)
                        else:
                            t0, t1 = gate_pipe(l, ps_, hook)
                        if pi + 1 == len(phases) and nxt is not None:
                            load_rope(nxt)
                            load_dma(nxt, 0)
                            load_dma(nxt, 1)
                        for b_ in t0:
                            b_()
                        if pi + 1 < len(phases):
                            _, l2, _, n2 = phases[pi + 1]
                            norm_sq(l2, n2, ps_, 0)
                            for b_ in t1[:2]:
                                b_()
                            norm_rest(l2, n2, ps_, 0)
                            for b_ in t1[2:]:
                                b_()
                            norm_sq(l2, n2, ps_, 1)
                        elif nxt is None:
                            for b_ in t1:
                                b_()
                        else:
                            for tt in range(4):
                                store_tile(ps_, tt)
                                load_tr(nxt, tt)
                                load_dma(nxt, tt + 2)
                            norm_sq(0, 0, nxt, 0)
                            for b_ in t1[:2]:
                                b_()
                            norm_rest(0, 0, nxt, 0)
                            for b_ in t1[2:]:
                                b_()
                            for tt in range(4, 8):
                                store_tile(ps_, tt)
                                load_tr(nxt, tt)
                                if tt + 2 < 8:
                                    load_dma(nxt, tt + 2)
                            norm_sq(0, 0, nxt, 1)
                    if nxt is None:
                        store_y(ps_)
                        pipelined_entry = False
                    else:
                        pipelined_entry = True
            except _Stop:
                break
        if not DBG_STOP:
            assert WS.pos == len(WS.seq), (WS.pos, len(WS.seq))
        P.finish("pool")
        P.build(st)
        stats = P.stats
    return nc, stats


def _small_table(norm_ffa, norm_mix, norm_ffb, norm_pe, q_norm, k_norm, sinks, pool_scale):
    sm = np.zeros((128, 84), np.float32)
    norms = [norm_ffa, norm_mix, norm_ffb, norm_pe]
    p = np.arange(128)
    for l in range(L):
        for n in range(4):
            sm[:, (l * 4 + n) * 8:(l * 4 + n) * 8 + 8] = np.asarray(norms[n][l], np.float32).reshape(8, 128).T
        sm[:, 64 + l * 2 + 0] = np.asarray(q_norm[l], np.float32)[p % 64]
        sm[:, 64 + l * 2 + 1] = np.asarray(k_norm[l], np.float32)[p % 64]
        for j in range(4):
            sm[:, 68 + l * 4 + j] = np.asarray(sinks[l], np.float32)[2 * j + p // 64]
        sm[:, 76 + l * 4:76 + l * 4 + 4] = np.asarray(pool_scale[l], np.float32).reshape(4, 128).T
    return sm


_CACHE = {}


def _get_prog(nb_p, nb_s):
    key = (nb_p, nb_s)
    if key not in _CACHE:
        _CACHE[key] = build_program(nb_p, nb_s)
    return _CACHE[key]


def make_in_maps(inputs, ncores, nb_p, nb_s):
    f32 = lambda a: np.ascontiguousarray(np.asarray(a, dtype=np.float32))
    c32, cb, rope_p, rope_s = _consts()
    sm = _small_table(inputs["norm_ffa"], inputs["norm_mix"], inputs["norm_ffb"], inputs["norm_pe"],
                      inputs["q_norm"], inputs["k_norm"], inputs["sinks"], inputs["pool_scale"])
    shared = dict(
        w_ffa_in=f32(inputs["w_ffa_in"]), w_ffa_out=f32(inputs["w_ffa_out"]),
        w_ffb_in=f32(inputs["w_ffb_in"]), w_ffb_out=f32(inputs["w_ffb_out"]),
        w_in=f32(inputs["w_in"]), w_out=f32(inputs["w_out"]), w_pe_gate=f32(inputs["w_pe_gate"]),
        w_pe_up=f32(inputs["w_pe_up"]), w_pool=f32(inputs["w_pool"]),
        small=sm, c32=c32, cb=cb, rope_p=rope_p, rope_s=rope_s)
    xp = f32(inputs["x_prompt"]); xs = f32(inputs["x_sample"])
    pp = f32(inputs["p_prompt"]); ps = f32(inputs["p_sample"])
    ck = f32(inputs["cache_k"]); cv = f32(inputs["cache_v"]); sp = f32(inputs["state_pool"])
    maps = []
    for i in range(ncores):
        bp = slice(i * nb_p, (i + 1) * nb_p) if nb_p else slice(0, 1)
        bs = slice(i * nb_s, (i + 1) * nb_s) if nb_s else slice(0, 1)
        nbs = max(nb_s, 1)
        m = dict(shared)
        m["xp"] = np.ascontiguousarray(xp[bp])
        m["xs"] = np.ascontiguousarray(xs[bs]).reshape(nbs * DSEQ, D)
        m["pp"] = np.ascontiguousarray(pp[:, bp])
        m["ps"] = np.ascontiguousarray(ps[:, bs]).reshape(L, nbs * DSEQ, PE_DIM)
        m["ck"] = np.ascontiguousarray(ck[:, bs]).reshape(L, nbs, 128, 128)
        m["cv"] = np.ascontiguousarray(cv[:, bs]).reshape(L, nbs, 128, 128)
        m["spool"] = np.ascontiguousarray(sp[:, bs])
        maps.append(m)
    return maps


def gather(results, nb_p, nb_s):
    nbp = max(nb_p, 1)
    nbs = max(nb_s, 1)
    yp = np.concatenate([r["yp"] for r in results], axis=0)
    ys = np.concatenate([r["ys"].reshape(nbs, DSEQ, D) for r in results], axis=0)
    nkp = np.concatenate([r["nkp"].reshape(L, nbp, 128, 2, 64) for r in results], axis=1)
    nvp = np.concatenate([r["nvp"].reshape(L, nbp, 128, 2, 64) for r in results], axis=1)
    npp = np.concatenate([r["npp"] for r in results], axis=1)
    nks = np.concatenate([r["nks"].reshape(L, nbs, 128, 2, 64) for r in results], axis=1)
    nvs = np.concatenate([r["nvs"].reshape(L, nbs, 128, 2, 64) for r in results], axis=1)
    nps = np.concatenate([r["nps"] for r in results], axis=1)
    return tuple(np.ascontiguousarray(a, dtype=np.float32) for a in (yp, ys, nkp, nvp, npp, nks, nvs, nps))


def kernel(**inputs):
    nb_p = inputs["x_prompt"].shape[0] // NCORES
    nb_s = inputs["x_sample"].shape[0] // NCORES
    nc, _ = _get_prog(nb_p, nb_s)
    maps = make_in_maps(inputs, NCORES, nb_p, nb_s)
    res = run_bass_kernel_spmd(nc, maps, core_ids=list(range(NCORES)))
    return gather(res.results, nb_p, nb_s)
```

```python
import math
from contextlib import ExitStack

import numpy as np
import concourse.bass as bass
import concourse.mybir as mybir
from concourse.bass_utils import run_bass_kernel_spmd

F32 = mybir.dt.float32
BF16 = mybir.dt.bfloat16
AF = mybir.ActivationFunctionType
ALU = mybir.AluOpType

L = 2
D = 1024
KC = 8
FF = 2816
FC = 22
FH = 11
SEQ = 2048
TP = 1024
DSEQ = 64
PE_DIM = 256
EPS = 1e-6
PAST = 4096
NCORES = 8

import os
DBG_STOP = os.environ.get("KSTOP", "")


class _Stop(Exception):
    pass


def ckpt(name):
    if DBG_STOP and DBG_STOP == name:
        raise _Stop()


ENGS = ("pe", "act", "dve", "pool", "sp")
N_DMA_SLOTS = 24


class Prog:
    def __init__(self, nc):
        self.nc = nc
        self.ops = []
        self.last_w = {}
        self.readers = {}

    def op(self, eng, fn, reads=(), writes=(), dma=False):
        i = len(self.ops)
        deps = set()
        for r in reads:
            w = self.last_w.get(r)
            if w is not None:
                deps.add(w)
            if isinstance(r, tuple) and r[0] == "bank":
                for x in self.readers.get(r, ()):
                    if self.ops[x]["eng"] != eng:
                        deps.add(x)
        for r in writes:
            w = self.last_w.get(r)
            if w is not None:
                deps.add(w)
            for x in self.readers.get(r, ()):
                deps.add(x)
        self.ops.append(dict(eng=eng, fn=fn, deps=deps, dma=dma, signal=dma, sig=None, waits=None))
        for r in reads:
            self.readers.setdefault(r, []).append(i)
        for r in writes:
            self.last_w[r] = i
            self.readers[r] = []
        return i

    def finish(self, eng="pool"):
        deps = set(self.last_w.values())
        self.ops.append(dict(eng=eng, fn=None, deps=deps, dma=False, signal=False, sig=None, waits=None))

    def build(self, stack):
        nc = self.nc
        ops = self.ops

        def pe_pe(o, od):
            return od["eng"] == "pe" and o["eng"] == "pe" and not od["dma"] and not o["dma"]

        slot_last = {}
        nslot = {"sp": 0, "pool": 0, "act": 0}
        slot_rng = {"sp": (0, 14), "pool": (14, 8), "act": (22, 2)}
        for i, o in enumerate(ops):
            if o["dma"]:
                base, cntq = slot_rng[o["eng"]]
                s = base + nslot[o["eng"]] % cntq
                nslot[o["eng"]] += 1
                o["slot"] = s
                if s in slot_last:
                    o["deps"].add(slot_last[s])
                slot_last[s] = i
        for o in ops:
            for d in o["deps"]:
                if not pe_pe(o, ops[d]):
                    ops[d]["signal"] = True
        cnt = {e: 0 for e in ENGS}
        dcnt = {}
        for o in ops:
            if o["dma"]:
                s = o["slot"]
                dcnt[s] = dcnt.get(s, 0) + 1
                o["sig"] = (("dma", s), 16 * dcnt[s])
            elif o["signal"]:
                cnt[o["eng"]] += 1
                o["sig"] = (o["eng"], cnt[o["eng"]])
        known = {e: {} for e in ENGS}
        clocks = [None] * len(ops)
        for i, o in enumerate(ops):
            e = o["eng"]
            kn = known[e]
            waits = {}
            for d in sorted(o["deps"], reverse=True):
                od = ops[d]
                if pe_pe(o, od):
                    continue
                key, val = od["sig"]
                if kn.get(key, 0) >= val:
                    continue
                waits[key] = max(waits.get(key, 0), val)
                for k2, v2 in clocks[d].items():
                    if kn.get(k2, 0) < v2:
                        kn[k2] = v2
            o["waits"] = list(waits.items())
            ck = dict(kn)
            if o["sig"] is not None:
                k, v = o["sig"]
                if ck.get(k, 0) < v:
                    ck[k] = v
            clocks[i] = ck
            o["deps"] = None
        del clocks
        sems = {}
        for e in ENGS:
            sems[e] = stack.enter_context(nc.semaphore("sem_" + e))
        for s in range(N_DMA_SLOTS):
            sems[("dma", s)] = stack.enter_context(nc.semaphore("sem_dma%d" % s))
        per_eng = {e: [o for o in ops if o["eng"] == e] for e in ENGS}
        self.stats = {e: len(per_eng[e]) for e in ENGS}

        def run(ename, eh):
            for o in per_eng[ename]:
                for k, v in o["waits"]:
                    eh.wait_ge(sems[k], v)
                if o["fn"] is None:
                    continue
                ins = o["fn"](eh)
                if o["sig"] is not None:
                    k, v = o["sig"]
                    ins.then_inc(sems[k], 16 if o["dma"] else 1)

        block = stack.enter_context(nc.Block())

        @block.sync
        def _(e):
            run("sp", e)

        @block.scalar
        def _(e):
            run("act", e)

        @block.vector
        def _(e):
            run("dve", e)

        @block.gpsimd
        def _(e):
            run("pool", e)

        @block.tensor
        def _(e):
            run("pe", e)


def _rope_tables(pos):
    pos = np.asarray(pos, np.float64)
    half = 8
    inv = np.power(500000.0, -np.arange(half, dtype=np.float64) / half)
    ang = (pos.astype(np.float32)[None, :] * inv.astype(np.float32)[:, None]).astype(np.float32).astype(np.float64)
    C = np.ones((128, len(pos)), np.float64)
    S = np.zeros((128, len(pos)), np.float64)
    for p in range(128):
        d = p % 64
        if d < 16:
            C[p] = np.cos(ang[d % 8])
            S[p] = np.sin(ang[d % 8])
    return C.astype(np.float32), S.astype(np.float32)


def _consts():
    ident = np.eye(128, dtype=np.float32)
    R = np.zeros((128, 128), np.float32)
    for m in range(128):
        d = m % 64
        if d < 8:
            R[m + 8, m] = -1.0
        elif d < 16:
            R[m - 8, m] = 1.0
    invcnt = np.zeros((128, 4, 15), np.float32)
    for g, w in enumerate((2, 4, 8, 16)):
        for t in range(15):
            invcnt[:, g, t] = 1.0 / min(t + 1, w)
    negc = np.zeros((128, 3), np.float32)
    negc[0:64, 1] = -30000.0
    negc[64:128, 2] = -30000.0
    c32 = np.concatenate([ident, R, invcnt.reshape(128, 60), negc], axis=1)
    onesD = np.full((128, 128), 1.0 / 1024.0, np.float32)
    blk = np.zeros((128, 128), np.float32)
    blk[0:64, 0:64] = 1.0 / 64.0
    blk[64:128, 64:128] = 1.0 / 64.0
    onesA = np.zeros((128, 128), np.float32)
    onesA[:, 0:64] = 1.0
    onesB = np.zeros((128, 128), np.float32)
    onesB[:, 64:128] = 1.0
    mP = np.ones((128, 128), np.float32)
    mP[0:64, 64:128] = 0.0
    mO = np.ones((128, 128), np.float32)
    mO[64:128, 0:64] = 0.0
    cb = np.concatenate([onesD, blk, onesA, onesB, np.tile(mP, (1, 4)), np.tile(mO, (1, 4))], axis=1)
    cp, sp_ = _rope_tables(np.arange(SEQ))
    cs, ss = _rope_tables(PAST + np.arange(DSEQ))
    rope_p = np.stack([cp, sp_], axis=0)
    rope_s = np.stack([np.tile(cs, (1, 4)), np.tile(ss, (1, 4))], axis=0)
    return c32, cb, rope_p, rope_s


def build_program(nb_p=4, nb_s=4):
    nc = bass.Bass("TRN2", target_bir_lowering=False)

    def din(name, shape):
        return nc.dram_tensor(name, list(shape), F32, kind="ExternalInput").ap()

    def dout(name, shape):
        return nc.dram_tensor(name, list(shape), F32, kind="ExternalOutput").ap()

    NBP = max(nb_p, 1)
    NBS = max(nb_s, 1)
    xp = din("xp", [NBP, SEQ, D])
    xs = din("xs", [NBS * DSEQ, D])
    pp = din("pp", [L, NBP, SEQ, PE_DIM])
    psm = din("ps", [L, NBS * DSEQ, PE_DIM])
    ck = din("ck", [L, NBS, 128, 128])
    cv = din("cv", [L, NBS, 128, 128])
    spool = din("spool", [L, NBS, 15, 512])
    W = {}
    W["ffa_in"] = din("w_ffa_in", [L, D, 2 * FF])
    W["ffa_out"] = din("w_ffa_out", [L, FF, D])
    W["ffb_in"] = din("w_ffb_in", [L, D, 2 * FF])
    W["ffb_out"] = din("w_ffb_out", [L, FF, D])
    W["w_in"] = din("w_in", [L, D, 1280])
    W["w_out"] = din("w_out", [L, D, D])
    W["pe_gate"] = din("w_pe_gate", [L, D, D])
    W["pe_up"] = din("w_pe_up", [L, PE_DIM, D])
    W["w_pool"] = din("w_pool", [L, 4, 128, 128])
    small_d = din("small", [128, 84])
    c32_d = din("c32", [128, 319])
    cb_d = din("cb", [128, 1536])
    rope_p_d = din("rope_p", [2, 128, SEQ])
    rope_s_d = din("rope_s", [2, 128, 256])

    yp = dout("yp", [NBP, SEQ, D])
    ys = dout("ys", [NBS * DSEQ, D])
    nkp = dout("nkp", [L, NBP, 128, 128])
    nvp = dout("nvp", [L, NBP, 128, 128])
    npp = dout("npp", [L, NBP, 15, 512])
    nks = dout("nks", [L, NBS, 128, 128])
    nvs = dout("nvs", [L, NBS, 128, 128])
    nps = dout("nps", [L, NBS, 15, 512])

    tiles = {}
    tile_list = []

    def add_tile(tid, elems):
        tiles[tid] = (len(tile_list), elems)
        tile_list.append(tid)

    for l in range(L):
        for which in ("a", "b"):
            for n in range(2 * FC):
                add_tile(("ffn_in", l, which, n), KC * 128)
            for h in range(2):
                for d in range(KC):
                    add_tile(("ffn_out", l, which, h, d), FH * 128)
        for n in range(11):
            add_tile(("w_in", l, n), KC * 128)
        for d in range(KC):
            add_tile(("w_out", l, d), KC * 128)
        for d in range(KC):
            add_tile(("pe_gate", l, d), KC * 128)
        for d in range(KC):
            add_tile(("pe_up", l, d), 2 * 128)
        add_tile(("w_pool", l), 4 * 128)
    WMAX = FH * 128
    scr = nc.dram_tensor("scr", [len(tile_list), 128, WMAX], BF16).ap()

    st = ExitStack()
    with st:
        def sb(name, shape, dt):
            return st.enter_context(nc.sbuf_tensor("sb_" + name, list(shape), dt))

        T = TP
        xT = sb("xT", [128, KC, T], F32)
        hT = sb("hT", [128, KC, T], BF16)
        act = sb("act", [128, 12, T], BF16)
        kdup = sb("kdup", [128, 2, 1152], BF16)
        krot32 = sb("krot32", [128, 2, 256], F32)
        uext = sb("uext", [128, 4, 1040], F32)
        dT = sb("dT", [128, 4, T], BF16)
        vpad = sb("vpad", [128, 9, 512], BF16)
        tmpA = sb("tmpA", [128, 1040], F32)
        tmpB = sb("tmpB", [128, 1040], F32)
        peT = sb("peT", [128, 2, T], BF16)
        ropeC = sb("ropeC", [128, T], F32)
        ropeS = sb("ropeS", [128, T], F32)
        NSLOT = int(os.environ.get("KNSLOT", "10"))
        wring = sb("wring", [128, NSLOT, WMAX], BF16)
        NTMP = int(os.environ.get("KNTMP", "10"))
        tmps = [sb("tmp%d" % i, [128, 512], F32) for i in range(NTMP)]
        pbufs = [sb("pbuf%d" % i, [128, 2, 512], BF16) for i in range(3)]
        c32 = sb("c32", [128, 319], F32)
        cb = sb("cb", [128, 1536], BF16)
        small = sb("small", [128, 84], F32)
        esf = sb("esf", [128, L, 4, 128], F32)
        neghalf = sb("neghalf", [128, 8], F32)
        epscol = neghalf
        kcarry = sb("kcarry", [128, L, 2, 128], BF16)
        vcarry = sb("vcarry", [128, L, 512], BF16)
        ucarry = sb("ucarry", [128, L, 4, 15], F32)
        kvstage = sb("kvstage", [128, 2, 256], F32)
        pstage = sb("pstage", [128, 2, 512], F32)
        ostage = sb("ostage", [128, 4, 128], F32)
        pin = sb("pin", [128, 2, 256], F32)

        psum_all = st.enter_context(nc.psum_tensor("psum_all", [128, 4096], F32))
        banks = [psum_all[:, i * 512:(i + 1) * 512] for i in range(8)]

        ident = c32[:, 0:128]
        Rm = c32[:, 128:256]
        invcnt = c32[:, 256:316]
        onesD = cb[:, 0:128]
        blk1 = cb[:, 128:256]
        onesA = cb[:, 256:384]
        onesB = cb[:, 384:512]
        maskP = cb[:, 512:1024]
        maskO = cb[:, 1024:1536]

        def gcol(l, n, c):
            return small[:, (l * 4 + n) * 8 + c:(l * 4 + n) * 8 + c + 1]

        def qkg(l, which):
            return small[:, 64 + l * 2 + which:64 + l * 2 + which + 1]

        def pscale(l, g):
            return small[:, 76 + l * 4 + g:76 + l * 4 + g + 1]

        P = Prog(nc)
        state = dict(bank=0, tmp=0, pbuf=0, dmaq=0)

        NROT_OUT = int(os.environ.get("KNROT", "6"))
        state["nrot"] = NROT_OUT

        def bank():
            i = state["bank"] % state["nrot"]
            state["bank"] = (i + 1) % state["nrot"]
            return banks[i], ("bank", i)

        def bank_pair():
            i = state["bank"] % state["nrot"]
            if i % 2:
                i = (i + 1) % state["nrot"]
            state["bank"] = (i + 2) % state["nrot"]
            return i, [("bank", i), ("bank", i + 1)]

        def tmp():
            i = state["tmp"]
            state["tmp"] = (i + 1) % NTMP
            state["tmpcount"] = state.get("tmpcount", 0) + 1
            return tmps[i], ("tmp", i)

        def flags(i, n):
            return dict(start=(i == 0), stop=(i == n - 1))

        P.op("sp", lambda e: e.dma_start(out=c32[:], in_=c32_d), writes=["c32"], dma=True)
        P.op("sp", lambda e: e.dma_start(out=small[:], in_=small_d), writes=["small"], dma=True)
        P.op("pool", lambda e: e.dma_start(out=cb[:], in_=cb_d), writes=["cb"], dma=True)
        P.op("pool", lambda e: e.memset(neghalf[:], EPS), writes=["neghalf"])
        P.op("dve", lambda e: e.memset(vpad[:], 0.0), writes=[("vpad", i) for i in range(9)])
        zt, zk = tmp()
        P.op("dve", lambda e: e.memset(zt[:], 0.0), writes=[zk])
        et, ek = tmp()
        P.op("act", lambda e: e.activation(et[:, 0:8], small[:, 68:76], AF.Exp), reads=["small"], writes=[ek])
        for l in range(L):
            for j in range(4):
                P.op("dve", lambda e, l=l, j=j: e.tensor_scalar_add(esf[:, l, j, :], zt[:, 0:128],
                                                                    et[:, l * 4 + j:l * 4 + j + 1]),
                     reads=[zk, ek], writes=["esf"])

        def src_ap(tid):
            kind = tid[0]
            if kind == "ffn_in":
                _, l, which, n = tid
                w = W["ffa_in" if which == "a" else "ffb_in"]
                return [(w[l, :, n * 128:(n + 1) * 128].rearrange("(k p) c -> p k c", p=128), 0, KC, 0, 128)]
            if kind == "ffn_out":
                _, l, which, h, d = tid
                w = W["ffa_out" if which == "a" else "ffb_out"]
                return [(w[l, h * FH * 128:(h + 1) * FH * 128, d * 128:(d + 1) * 128]
                         .rearrange("(k p) c -> p k c", p=128), 0, FH, 0, 128)]
            if kind == "w_in":
                _, l, n = tid
                w = W["w_in"]
                if n < 4:
                    cols = [(n * 128, 128, 0)]
                elif n < 6:
                    c0 = 512 + (n - 4) * 64
                    cols = [(c0, 64, 0), (c0, 64, 64)]
                elif n == 6:
                    cols = [(640, 128, 0)]
                else:
                    cols = [(768 + (n - 7) * 128, 128, 0)]
                return [(w[l, :, c0:c0 + cn].rearrange("(k p) c -> p k c", p=128), 0, KC, o0, cn)
                        for (c0, cn, o0) in cols]
            if kind in ("w_out", "pe_gate"):
                _, l, d = tid
                w = W[kind]
                return [(w[l, :, d * 128:(d + 1) * 128].rearrange("(k p) c -> p k c", p=128), 0, KC, 0, 128)]
            if kind == "pe_up":
                _, l, d = tid
                w = W["pe_up"]
                return [(w[l, :, d * 128:(d + 1) * 128].rearrange("(k p) c -> p k c", p=128), 0, 2, 0, 128)]
            if kind == "w_pool":
                _, l = tid
                w = W["w_pool"]
                return [(w[l].rearrange("g p c -> p g c"), 0, 4, 0, 128)]
            raise ValueError(tid)

        def wsrc(name, l, r0, nr, c0, ncol):
            return W[name][l, r0:r0 + nr, c0:c0 + ncol].rearrange("(k p) c -> p k c", p=128)

        groups = []
        for l in range(L):
            for which in ("a", "b"):
                for n0 in range(0, 2 * FC, 4):
                    groups.append((("ffn_in", l, which, n0), 4, KC,
                                   [(wsrc("ff%s_in" % which, l, 0, D, n0 * 128, 512), 0, 512)]))
                for h in range(2):
                    for d0 in range(0, KC, 2):
                        groups.append((("ffn_out", l, which, h, d0), 2, FH,
                                       [(wsrc("ff%s_out" % which, l, h * FH * 128, FH * 128, d0 * 128, 256), 0, 256)]))
            groups.append((("w_in", l, 0), 4, KC, [(wsrc("w_in", l, 0, D, 0, 512), 0, 512)]))
            for g in range(2):
                c0 = 512 + g * 64
                groups.append((("w_in", l, 4 + g), 1, KC, [(wsrc("w_in", l, 0, D, c0, 64), 0, 64),
                                                           (wsrc("w_in", l, 0, D, c0, 64), 64, 64)]))
            groups.append((("w_in", l, 6), 1, KC, [(wsrc("w_in", l, 0, D, 640, 128), 0, 128)]))
            groups.append((("w_in", l, 7), 4, KC, [(wsrc("w_in", l, 0, D, 768, 512), 0, 512)]))
            for name in ("w_out", "pe_gate"):
                for d0 in range(0, KC, 4):
                    groups.append(((name, l, d0), 4, KC, [(wsrc(name, l, 0, D, d0 * 128, 512), 0, 512)]))
            groups.append((("pe_up", l, 0), 8, 2, [(wsrc("pe_up", l, 0, PE_DIM, 0, 1024), 0, 1024)]))
            groups.append((("w_pool", l), 1, 4, [(W["w_pool"][l].rearrange("g p c -> p g c"), 0, 128)]))
        assert sum(g_[1] for g_ in groups) == len(tile_list)
        s32 = [(xT[:, 0:4, :].rearrange("p a t -> p (a t)"), [("xT", c, tb) for c in range(0, 4) for tb in range(2)]),
               (xT[:, 4:8, :].rearrange("p a t -> p (a t)"), [("xT", c, tb) for c in range(4, 8) for tb in range(2)]),
               (uext[:, :, :].rearrange("p a t -> p (a t)"), [("uext", g) for g in range(4)])]
        s16 = [(hT[:, 0:4, :].rearrange("p a t -> p (a t)"), [("hT", c, tb) for c in range(0, 4) for tb in range(2)]),
               (hT[:, 4:8, :].rearrange("p a t -> p (a t)"), [("hT", c, tb) for c in range(4, 8) for tb in range(2)]),
               (act[:, 0:4, :].rearrange("p a t -> p (a t)"), [("act", c, tb) for c in range(0, 4) for tb in range(2)])]
        cast_engs = ("act", "act", "act")
        for gi, (tid0, ng, kct, srcs) in enumerate(groups if DBG_STOP != 'consts' else []):
            idx0 = tiles[tid0][0]
            st32f, k32 = s32[gi % 3]
            st16f, k16 = s16[gi % 3]
            gw = ng * 128
            ne = kct * gw
            v32 = st32f[:, 0:ne].rearrange("p (k c) -> p k c", c=gw)
            for (sap, o0, cn) in srcs:
                P.op("sp", lambda e, dst=v32[:, :, o0:o0 + cn], sap=sap: e.dma_start(out=dst, in_=sap),
                     writes=k32, dma=True)
            if ng == 1:
                cin = st32f[:, 0:ne]
                cout = st16f[:, 0:ne]
            else:
                cin = st32f[:, 0:ne].rearrange("p (k n c) -> p k n c", k=kct, n=ng)
                cout = st16f[:, 0:ne].rearrange("p (n k c) -> p k n c", n=ng, k=kct)
            ce = cast_engs[gi % 3]
            if ce == "act":
                P.op("act", lambda e, a=cout, b=cin: e.copy(a, b), reads=k32, writes=k16)
            else:
                P.op(ce, lambda e, a=cout, b=cin: e.tensor_copy(a, b), reads=k32, writes=k16)
            te = kct * 128
            dst = scr[idx0:idx0 + ng, :, 0:te].rearrange("n p e -> p n e")
            P.op("sp", lambda e, dst=dst, a=st16f[:, 0:ne].rearrange("p (n e) -> p n e", n=ng): e.dma_start(out=dst, in_=a),
                 reads=k16, writes=[("scr", tile_list[idx0 + j]) for j in range(ng)], dma=True)

        class WStream:
            def __init__(self):
                self.seq = []
                self.pos = 0
                self.loaded = 0

            def load_next(self):
                if self.loaded >= len(self.seq):
                    return
                tid = self.seq[self.loaded]
                s = self.loaded % NSLOT
                self.loaded += 1
                idx, elems = tiles[tid]
                P.op("sp", lambda e, s=s, idx=idx, elems=elems: e.dma_start(out=wring[:, s, 0:elems],
                                                                             in_=scr[idx, :, 0:elems]),
                     reads=[("scr", tid)], writes=[("w", s)], dma=True)

            def start(self):
                for _ in range(NSLOT):
                    self.load_next()

            def get(self, tid):
                assert self.seq[self.pos] == tid, (self.seq[self.pos], tid)
                s = self.pos % NSLOT
                self.pos += 1
                return s, ("w", s)

            def release(self):
                self.load_next()

        WS = WStream()

        def layer_tiles(l):
            out = []
            for which in ("a", "b"):
                ff = []
                for h in range(2):
                    for fi in range(FH):
                        f = h * FH + fi
                        ff.append(("ffn_in", l, which, f))
                        ff.append(("ffn_in", l, which, FC + f))
                    for d in range(KC):
                        ff.append(("ffn_out", l, which, h, d))
                if which == "a":
                    out += ff
                    out += [("w_in", l, n) for n in range(11)]
                    out.append(("w_pool", l))
                    out += [("w_out", l, d) for d in range(KC)]
                else:
                    out += ff
                    for d in range(KC):
                        out.append(("pe_gate", l, d))
                        out.append(("pe_up", l, d))
            return out

        passes = []
        for b in range(nb_p):
            for half in range(2):
                passes.append(dict(kind="p", b=b, half=half, T=TP, tbs=[(0, 512), (512, 512)]))
        if nb_s > 0:
            passes.append(dict(kind="s", T=nb_s * DSEQ, tbs=[(0, nb_s * DSEQ)]))
        for _ in passes:
            for l in range(L):
                WS.seq += layer_tiles(l)
        if DBG_STOP not in ('consts', 'prologue'):
            WS.start()

        def xk(c, tb):
            return ("xT", c, tb)

        def hk(c, tb):
            return ("hT", c, tb)

        def ak(c, tb):
            return ("act", c, tb)

        def norm_sq(l, n, ps_, tbi):
            off, nn = ps_["tbs"][tbi]
            sq = act[:, 0:8, off:off + nn]
            P.op("act", lambda e, sq=sq, off=off, nn=nn: e.activation(sq, xT[:, :, off:off + nn], AF.Square),
                 reads=[xk(c, tbi) for c in range(KC)], writes=[ak(c, tbi) for c in range(8)])

        def norm_rest(l, n, ps_, tbi):
            off, nn = ps_["tbs"][tbi]
            bk, bkk = bank()
            def mm(e, bk=bk, off=off, nn=nn):
                for c in range(KC):
                    ins = e.matmul(bk[:, 0:nn], onesD, act[:, c, off:off + nn], **flags(c, KC))
                return ins
            P.op("pe", mm, reads=[ak(c, tbi) for c in range(8)] + ["cb"], writes=[bkk])
            t1, t1k = tmp()
            P.op("act", lambda e, t1=t1, bk=bk, nn=nn: e.activation(t1[:, 0:nn], bk[:, 0:nn], AF.Ln,
                                                                    bias=epscol[:, 0:1]),
                 reads=[bkk, "neghalf"], writes=[t1k])
            t2, t2k = tmp()
            P.op("act", lambda e, t1=t1, t2=t2, nn=nn: e.activation(t2[:, 0:nn], t1[:, 0:nn], AF.Exp, scale=-0.5),
                 reads=[t1k], writes=[t2k])
            for c in range(KC):
                P.op("dve", lambda e, c=c, t2=t2, off=off, nn=nn: e.scalar_tensor_tensor(
                    hT[:, c, off:off + nn], xT[:, c, off:off + nn], gcol(l, n, c), t2[:, 0:nn], ALU.mult, ALU.mult),
                    reads=[xk(c, tbi), t2k, "small"], writes=[hk(c, tbi)])

        def norm(l, n, ps_):
            for tbi in range(len(ps_["tbs"])):
                norm_sq(l, n, ps_, tbi)
                norm_rest(l, n, ps_, tbi)

        def ffn(l, which, ps_):
            for h in range(2):
                for fi in range(FH):
                    f = h * FH + fi
                    sg_, sgk = WS.get(("ffn_in", l, which, f))
                    su_, suk = WS.get(("ffn_in", l, which, FC + f))
                    for tbi, (off, nn) in enumerate(ps_["tbs"]):
                        bg, bgk = bank()
                        bu, buk = bank()
                        def mm(e, s=sg_, bk=bg, off=off, nn=nn):
                            for k in range(KC):
                                ins = e.matmul(bk[:, 0:nn], wring[:, s, k * 128:(k + 1) * 128], hT[:, k, off:off + nn],
                                               **flags(k, KC))
                            return ins
                        P.op("pe", mm, reads=[sgk] + [hk(c, tbi) for c in range(KC)], writes=[bgk])
                        def mm2(e, s=su_, bk=bu, off=off, nn=nn):
                            for k in range(KC):
                                ins = e.matmul(bk[:, 0:nn], wring[:, s, k * 128:(k + 1) * 128], hT[:, k, off:off + nn],
                                               **flags(k, KC))
                            return ins
                        P.op("pe", mm2, reads=[suk] + [hk(c, tbi) for c in range(KC)], writes=[buk])
                        t1, t1k = tmp()
                        P.op("act", lambda e, t1=t1, bg=bg, nn=nn: e.activation(t1[:, 0:nn], bg[:, 0:nn], AF.Silu),
                             reads=[bgk], writes=[t1k])
                        P.op("dve", lambda e, t1=t1, bu=bu, fi=fi, off=off, nn=nn: e.tensor_tensor(
                            act[:, fi, off:off + nn], t1[:, 0:nn], bu[:, 0:nn], ALU.mult),
                            reads=[t1k, buk], writes=[ak(fi, tbi)])
                    WS.release()
                    WS.release()
                for d in range(KC):
                    so_, sok = WS.get(("ffn_out", l, which, h, d))
                    for tbi, (off, nn) in enumerate(ps_["tbs"]):
                        by, byk = bank()
                        def mm(e, s=so_, bk=by, off=off, nn=nn):
                            for fi in range(FH):
                                ins = e.matmul(bk[:, 0:nn], wring[:, s, fi * 128:(fi + 1) * 128],
                                               act[:, fi, off:off + nn], **flags(fi, FH))
                            return ins
                        P.op("pe", mm, reads=[sok] + [ak(fi, tbi) for fi in range(FH)], writes=[byk])
                        P.op("dve", lambda e, by=by, d=d, off=off, nn=nn: e.scalar_tensor_tensor(
                            xT[:, d, off:off + nn], by[:, 0:nn], 0.5, xT[:, d, off:off + nn], ALU.mult, ALU.add),
                            reads=[byk, xk(d, tbi)], writes=[xk(d, tbi)])
                    WS.release()

        def ffn_pipe(l, which, ps_, hook_mid, it_hook=None):
            tbs = ps_["tbs"]

            def p1_block(fi, tbi, sg_, sgk, su_, suk):
                off, nn = tbs[tbi]
                bg, bgk = bank()
                bu, buk = bank()
                def mm(e, s=sg_, bk=bg, off=off, nn=nn):
                    for k in range(KC):
                        ins = e.matmul(bk[:, 0:nn], wring[:, s, k * 128:(k + 1) * 128], hT[:, k, off:off + nn],
                                       **flags(k, KC))
                    return ins
                P.op("pe", mm, reads=[sgk] + [hk(c, tbi) for c in range(KC)], writes=[bgk])
                def mm2(e, s=su_, bk=bu, off=off, nn=nn):
                    for k in range(KC):
                        ins = e.matmul(bk[:, 0:nn], wring[:, s, k * 128:(k + 1) * 128], hT[:, k, off:off + nn],
                                       **flags(k, KC))
                    return ins
                P.op("pe", mm2, reads=[suk] + [hk(c, tbi) for c in range(KC)], writes=[buk])
                t1, t1k = tmp()
                P.op("act", lambda e, t1=t1, bg=bg, nn=nn: e.activation(t1[:, 0:nn], bg[:, 0:nn], AF.Silu),
                     reads=[bgk], writes=[t1k])
                P.op("dve", lambda e, t1=t1, bu=bu, fi=fi, off=off, nn=nn: e.tensor_tensor(
                    act[:, fi, off:off + nn], t1[:, 0:nn], bu[:, 0:nn], ALU.mult),
                    reads=[t1k, buk], writes=[ak(fi, tbi)])

            def p2_block(d, tbi, so_, sok):
                off, nn = tbs[tbi]
                by, byk = bank()
                def mm(e, s=so_, bk=by, off=off, nn=nn):
                    for fi in range(FH):
                        ins = e.matmul(bk[:, 0:nn], wring[:, s, fi * 128:(fi + 1) * 128],
                                       act[:, fi, off:off + nn], **flags(fi, FH))
                    return ins
                P.op("pe", mm, reads=[sok] + [ak(fi, tbi) for fi in range(FH)], writes=[byk])
                P.op("dve", lambda e, by=by, d=d, off=off, nn=nn: e.scalar_tensor_tensor(
                    xT[:, d, off:off + nn], by[:, 0:nn], 0.5, xT[:, d, off:off + nn], ALU.mult, ALU.add),
                    reads=[byk, xk(d, tbi)], writes=[xk(d, tbi)])

            NH = 4
            itc = [0]
            held = []
            for fi in range(NH):
                g_ = WS.get(("ffn_in", l, which, fi))
                u_ = WS.get(("ffn_in", l, which, FC + fi))
                held.append(g_ + u_)
            for fi in range(NH):
                p1_block(fi, 0, *held[fi])
                if fi == 0:
                    hook_mid()
            for fi in range(NH):
                p1_block(fi, 1, *held[fi])
                WS.release()
                WS.release()
            for h in range(2):
                for fi in range(NH if h == 0 else 0, FH):
                    f = h * FH + fi
                    g_ = WS.get(("ffn_in", l, which, f))
                    u_ = WS.get(("ffn_in", l, which, FC + f))
                    for tbi in range(2):
                        p1_block(fi, tbi, *(g_ + u_))
                    WS.release()
                    WS.release()
                    if it_hook is not None:
                        it_hook(itc[0])
                        itc[0] += 1
                if h == 0:
                    for d in range(KC):
                        o_ = WS.get(("ffn_out", l, which, 0, d))
                        for tbi in range(2):
                            p2_block(d, tbi, *o_)
                        WS.release()
            slots = [WS.get(("ffn_out", l, which, 1, d)) for d in range(KC)]
            tail0 = [(lambda d=d: p2_block(d, 0, *slots[d])) for d in range(KC)]
            def t1f(d):
                p2_block(d, 1, *slots[d])
                WS.release()
            tail1 = [(lambda d=d: t1f(d)) for d in range(KC)]
            return tail0, tail1

        def gate_pipe(l, ps_, hook_mid):
            tbs = ps_["tbs"]

            def block(d, tbi, sg_, sgk, su_, suk):
                off, nn = tbs[tbi]
                bg, bgk = bank()
                bu, buk = bank()
                def mm(e, s=sg_, bk=bg, off=off, nn=nn):
                    for k in range(KC):
                        ins = e.matmul(bk[:, 0:nn], wring[:, s, k * 128:(k + 1) * 128], hT[:, k, off:off + nn],
                                       **flags(k, KC))
                    return ins
                P.op("pe", mm, reads=[sgk] + [hk(c, tbi) for c in range(KC)], writes=[bgk])
                def mm2(e, s=su_, bk=bu, off=off, nn=nn):
                    for k in range(2):
                        ins = e.matmul(bk[:, 0:nn], wring[:, s, k * 128:(k + 1) * 128], peT[:, k, off:off + nn],
                                       **flags(k, 2))
                    return ins
                tts = list(range(off // 128, (off + nn + 127) // 128))
                P.op("pe", mm2, reads=[suk] + [("peT", tt) for tt in tts], writes=[buk])
                t1, t1k = tmp()
                P.op("act", lambda e, t1=t1, bg=bg, nn=nn: e.activation(t1[:, 0:nn], bg[:, 0:nn], AF.Tanh, scale=0.5),
                     reads=[bgk], writes=[t1k])
                t2, t2k = tmp()
                P.op("dve", lambda e, t1=t1, t2=t2, bu=bu, nn=nn: e.scalar_tensor_tensor(
                    t2[:, 0:nn], t1[:, 0:nn], 1.0, bu[:, 0:nn], ALU.add, ALU.mult), reads=[t1k, buk], writes=[t2k])
                P.op("dve", lambda e, t2=t2, d=d, off=off, nn=nn: e.scalar_tensor_tensor(
                    xT[:, d, off:off + nn], t2[:, 0:nn], 0.5, xT[:, d, off:off + nn], ALU.mult, ALU.add),
                    reads=[t2k, xk(d, tbi)], writes=[xk(d, tbi)])

            held = []
            for d in range(4):
                g_ = WS.get(("pe_gate", l, d))
                u_ = WS.get(("pe_up", l, d))
                held.append(g_ + u_)
            for d in range(4):
                block(d, 0, *held[d])
                if d == 0:
                    hook_mid()
            for d in range(4):
                block(d, 1, *held[d])
                WS.release()
                WS.release()
            held2 = []
            for d in range(4, 8):
                g_ = WS.get(("pe_gate", l, d))
                u_ = WS.get(("pe_up", l, d))
                held2.append(g_ + u_)
            tail0 = [(lambda d=d: block(d, 0, *held2[d - 4])) for d in range(4, 8)]
            def t1f(d):
                block(d, 1, *held2[d - 4])
                WS.release()
                WS.release()
            tail1 = [(lambda d=d: t1f(d)) for d in range(4, 8)]
            return tail0, tail1

        def load_x(ps_):
            for tt in range(ps_["T"] // 128):
                load_tile(ps_, tt)

        def load_tile(ps_, tt):
            load_dma(ps_, tt)
            load_tr(ps_, tt)

        def load_dma(ps_, tt):
            sl = tt % 2
            xin = uext[:, sl, 0:1024]
            if ps_["kind"] == "p":
                src = xp[ps_["b"], ps_["half"] * TP + tt * 128: ps_["half"] * TP + (tt + 1) * 128, :]
            else:
                src = xs[tt * 128:(tt + 1) * 128, :]
            P.op("pool", lambda e, xin=xin, src=src: e.dma_start(out=xin, in_=src), writes=[("uext", sl)], dma=True)

        def load_tr(ps_, tt):
            if True:
                sl = tt % 2
                xin = uext[:, sl, 0:1024]
                tbi = (tt * 128) // 512 if ps_["kind"] == "p" else 0
                for hb in range(2):
                    bk, bkk = bank()
                    def mm(e, bk=bk, xin=xin, hb=hb):
                        for c4 in range(4):
                            c = hb * 4 + c4
                            ins = e.transpose(bk[:, c4 * 128:(c4 + 1) * 128], xin[:, c * 128:(c + 1) * 128], ident)
                        return ins
                    P.op("pe", mm, reads=[("uext", sl), "c32"], writes=[bkk])
                    P.op("act", lambda e, bk=bk, hb=hb, tt=tt: e.copy(
                        xT[:, hb * 4:hb * 4 + 4, tt * 128:(tt + 1) * 128], bk[:].rearrange("p (c t) -> p c t", t=128)),
                        reads=[bkk], writes=[xk(hb * 4 + c4, tbi) for c4 in range(4)])

        def store_y(ps_):
            for tt in range(ps_["T"] // 128):
                store_tile(ps_, tt)

        def store_tile(ps_, tt):
            if True:
                sl = 2 + tt % 2
                yst = uext[:, sl, 0:1024]
                tbi = (tt * 128) // 512 if ps_["kind"] == "p" else 0
                for hb in range(2):
                    bk, bkk = bank()
                    def mm(e, bk=bk, hb=hb, tt=tt):
                        for c4 in range(4):
                            c = hb * 4 + c4
                            ins = e.transpose(bk[:, c4 * 128:(c4 + 1) * 128], xT[:, c, tt * 128:(tt + 1) * 128], ident)
                        return ins
                    P.op("pe", mm, reads=[xk(hb * 4 + c4, tbi) for c4 in range(4)] + ["c32"], writes=[bkk])
                    eng = "act" if hb == 0 else "dve"
                    if eng == "act":
                        P.op("act", lambda e, bk=bk, hb=hb, yst=yst: e.copy(yst[:, hb * 512:(hb + 1) * 512], bk[:]),
                             reads=[bkk], writes=[("uext", sl)])
                    else:
                        P.op("dve", lambda e, bk=bk, hb=hb, yst=yst: e.tensor_copy(yst[:, hb * 512:(hb + 1) * 512], bk[:]),
                             reads=[bkk, ("uext", sl)], writes=[("uext", sl)])
                if ps_["kind"] == "p":
                    dst = yp[ps_["b"], ps_["half"] * TP + tt * 128: ps_["half"] * TP + (tt + 1) * 128, :]
                else:
                    dst = ys[tt * 128:(tt + 1) * 128, :]
                P.op("pool", lambda e, yst=yst, dst=dst: e.dma_start(out=dst, in_=yst), reads=[("uext", sl)],
                     writes=[("out", "y", id(ps_), tt)], dma=True)

        def pe_dma(l, ps_, tt):
            sl = tt % 2
            if ps_["kind"] == "p":
                src = pp[l, ps_["b"], ps_["half"] * TP + tt * 128: ps_["half"] * TP + (tt + 1) * 128, :]
            else:
                src = psm[l, tt * 128:(tt + 1) * 128, :]
            P.op("pool", lambda e, sl=sl, src=src: e.dma_start(out=pin[:, sl, :], in_=src), writes=[("pin", sl)],
                 dma=True)

        def pe_tr(l, ps_, tt):
            sl = tt % 2
            bk, bkk = bank()
            def mm(e, bk=bk, sl=sl):
                for j in range(2):
                    ins = e.transpose(bk[:, j * 128:(j + 1) * 128], pin[:, sl, j * 128:(j + 1) * 128], ident)
                return ins
            P.op("pe", mm, reads=[("pin", sl), "c32"], writes=[bkk])
            P.op("act", lambda e, bk=bk, tt=tt: e.copy(peT[:, :, tt * 128:(tt + 1) * 128],
                                                       bk[:, 0:256].rearrange("p (c t) -> p c t", t=128)),
                 reads=[bkk], writes=[("peT", tt)])

        def load_pe(l, ps_):
            for tt in range(ps_["T"] // 128):
                pe_dma(l, ps_, tt)
                pe_tr(l, ps_, tt)

        def segs_of(ps_):
            if ps_["kind"] == "p":
                return [dict(b=ps_["b"], t0=0, Ls=TP, kbase=0, vbase=0, ubase=0,
                             hist=("none" if ps_["half"] == 0 else "carry"))]
            return [dict(b=s, t0=s * DSEQ, Ls=DSEQ, kbase=s * 192, vbase=2 * s, ubase=s * 79, hist="cache")
                    for s in range(nb_s)]

        def kd_keys_own(ps_, g, tbi):
            if ps_["kind"] == "p":
                return [("kdup", g, 1 + tbi * 4 + i) for i in range(4)]
            return [("kdup", g, 2 * s + 1) for s in range(nb_s)]

        def mixer(l, ps_, hook_mid=None, defer_wout=False):
            T_ = ps_["T"]
            isp = ps_["kind"] == "p"
            segs = segs_of(ps_)
            S = len(segs)
            Ls = segs[0]["Ls"]
            UW = 15 + Ls
            for s, sg in enumerate(segs):
                ub = sg["ubase"]
                if sg["hist"] == "none":
                    P.op("pool", lambda e, ub=ub: e.memset(uext[:, :, ub:ub + 15], 0.0),
                         writes=[("uext", g) for g in range(4)])
                elif sg["hist"] == "carry":
                    P.op("pool", lambda e: e.tensor_copy(kdup[:, :, 0:128], kcarry[:, l, :, :]),
                         reads=[("kcarry", l)], writes=[("kdup", 0, 0), ("kdup", 1, 0)])
                    P.op("pool", lambda e: e.tensor_copy(vpad[:, 0, :], vcarry[:, l, :]),
                         reads=[("vcarry", l)], writes=[("vpad", 0)])
                    P.op("pool", lambda e, ub=ub: e.tensor_copy(uext[:, :, ub:ub + 15], ucarry[:, l, :, :]),
                         reads=[("ucarry", l)], writes=[("uext", g) for g in range(4)])
                else:
                    b = sg["b"]
                    kb = sg["kbase"]
                    vb = sg["vbase"]
                    srck = ck[l, b].rearrange("r (g d) -> r g d", g=2)
                    P.op("pool", lambda e, srck=srck: e.dma_start(
                        out=kvstage[:, 0, :].rearrange("r (g c) -> r g c", g=2)[:, :, 0:64], in_=srck),
                        writes=["kvs0"], dma=True)
                    P.op("pool", lambda e, srck=srck: e.dma_start(
                        out=kvstage[:, 0, :].rearrange("r (g c) -> r g c", g=2)[:, :, 64:128], in_=srck),
                        writes=["kvs0b"], dma=True)
                    bk, bkk = bank()
                    def mm(e, bk=bk):
                        for g in range(2):
                            ins = e.transpose(bk[:, g * 128:(g + 1) * 128], kvstage[:, 0, g * 128:(g + 1) * 128], ident)
                        return ins
                    P.op("pe", mm, reads=["kvs0", "kvs0b", "c32"], writes=[bkk])
                    P.op("act", lambda e, bk=bk, kb=kb: e.copy(kdup[:, :, kb:kb + 128],
                                                               bk[:, 0:256].rearrange("p (g t) -> p g t", g=2)),
                         reads=[bkk], writes=[("kdup", 0, 2 * s), ("kdup", 1, 2 * s)])
                    P.op("pool", lambda e, b=b: e.dma_start(out=kvstage[:, 1, 0:128], in_=cv[l, b]),
                         writes=["kvs1"], dma=True)
                    vin = kvstage[:, 1, 0:128].rearrange("r (g d) -> r g d", g=2)
                    P.op("act", lambda e, vb=vb, vin=vin: e.copy(
                        vpad[:, vb, :].rearrange("r (g c) -> r g c", g=2)[:, :, 0:64], vin),
                        reads=["kvs1"], writes=[("vpad", vb)])
                    P.op("act", lambda e, vb=vb, vin=vin: e.copy(
                        vpad[:, vb, :].rearrange("r (g c) -> r g c", g=2)[:, :, 192:256], vin),
                        reads=["kvs1", ("vpad", vb)], writes=[("vpad", vb)])
                    P.op("pool", lambda e, b=b: e.dma_start(out=pstage[0:15, 0, :], in_=spool[l, b]),
                         writes=["pst0"], dma=True)
                    bk2, bk2k = bank()
                    def mm2(e, bk2=bk2):
                        for g in range(4):
                            ins = e.transpose(bk2[:, g * 16:g * 16 + 15], pstage[0:15, 0, g * 128:(g + 1) * 128],
                                              ident[0:15, 0:15])
                        return ins
                    P.op("pe", mm2, reads=["pst0", "c32"], writes=[bk2k])
                    P.op("act", lambda e, bk2=bk2, ub=ub: e.copy(
                        uext[:, :, ub:ub + 15], bk2[:, 0:64].rearrange("p (g t) -> p g t", g=4)[:, :, 0:15]),
                        reads=[bk2k], writes=[("uext", g) for g in range(4)])
                    P.op("pool", lambda e, b=b: e.dma_start(out=nks[l, b, 0:64, :], in_=ck[l, b, 64:128, :]),
                         writes=[("out", "nks0", l, b)], dma=True)
                    P.op("pool", lambda e, b=b: e.dma_start(out=nvs[l, b, 0:64, :], in_=cv[l, b, 64:128, :]),
                         writes=[("out", "nvs0", l, b)], dma=True)

            ckpt('m_hist')
            if hook_mid is None:
                items = [(n, tbi, off, nn) for n in range(6) for tbi, (off, nn) in enumerate(ps_["tbs"])]
            else:
                items = [(n, tbi, off, nn) for tbi, (off, nn) in enumerate(ps_["tbs"]) for n in range(6)]
            qk_state = {}
            qk_slots = {}

            def qk_S1(i):
                n, tbi, off, nn = items[i]
                if tbi == 0:
                    qk_slots[n] = WS.get(("w_in", l, n))
                sw, swk = qk_slots[n]
                bz, bzk = bank()
                def mm(e, s=sw, bk=bz, off=off, nn=nn):
                    for k in range(KC):
                        ins = e.matmul(bk[:, 0:nn], wring[:, s, k * 128:(k + 1) * 128], hT[:, k, off:off + nn],
                                       **flags(k, KC))
                    return ins
                P.op("pe", mm, reads=[swk] + [hk(c, tbi) for c in range(KC)], writes=[bzk])
                zg, zgk = tmp()
                wh = 0 if n < 4 else 1
                P.op("dve", lambda e, zg=zg, bz=bz, nn=nn, wh=wh: e.tensor_scalar(zg[:, 0:nn], bz[:, 0:nn], qkg(l, wh),
                                                                                  None, ALU.mult),
                     reads=[bzk, "small"], writes=[zgk])
                sl = i % 2
                sqh = dT[:, sl, 0:nn]
                P.op("act", lambda e, sqh=sqh, bz=bz, nn=nn: e.activation(sqh, bz[:, 0:nn], AF.Square),
                     reads=[bzk], writes=[("dT", sl)])
                if tbi == len(ps_["tbs"]) - 1:
                    WS.release()
                qk_state[i] = (zg, zgk, sl, state["tmpcount"])

            def qk_S2(i):
                n, tbi, off, nn = items[i]
                zg, zgk, sl, cnt0 = qk_state.pop(i)
                assert state["tmpcount"] - cnt0 < NTMP, (state["tmpcount"], cnt0)
                sqh = dT[:, sl, 0:nn]
                a1, a1k = tmp()
                P.op("pool", lambda e, a1=a1, zg=zg, off=off, nn=nn: e.tensor_tensor(
                    a1[:, 0:nn], zg[:, 0:nn], ropeC[:, off:off + nn], ALU.mult), reads=[zgk, "ropeC"], writes=[a1k])
                br, brk = bank()
                P.op("pe", lambda e, br=br, zg=zg, nn=nn: e.matmul(br[:, 0:nn], Rm, zg[:, 0:nn], start=True, stop=True),
                     reads=[zgk, "c32"], writes=[brk])
                bs, bsk = bank()
                P.op("pe", lambda e, bs=bs, sqh=sqh, nn=nn: e.matmul(bs[:, 0:nn], blk1, sqh, start=True, stop=True),
                     reads=[("dT", sl), "cb"], writes=[bsk])
                t1, t1k = tmp()
                P.op("act", lambda e, t1=t1, bs=bs, nn=nn: e.activation(t1[:, 0:nn], bs[:, 0:nn], AF.Ln,
                                                                        bias=epscol[:, 0:1]),
                     reads=[bsk, "neghalf"], writes=[t1k])
                a2, a2k = tmp()
                P.op("dve", lambda e, a2=a2, br=br, off=off, nn=nn: e.tensor_tensor(
                    a2[:, 0:nn], br[:, 0:nn], ropeS[:, off:off + nn], ALU.mult), reads=[brk, "ropeS"], writes=[a2k])
                P.op("dve", lambda e, a1=a1, a2=a2, nn=nn: e.tensor_tensor(
                    a2[:, 0:nn], a1[:, 0:nn], a2[:, 0:nn], ALU.add), reads=[a1k, a2k], writes=[a2k])
                t2, t2k = tmp()
                P.op("act", lambda e, t1=t1, t2=t2, nn=nn: e.activation(t2[:, 0:nn], t1[:, 0:nn], AF.Exp, scale=-0.5),
                     reads=[t1k], writes=[t2k])
                if n < 4:
                    P.op("dve", lambda e, a2=a2, t2=t2, n=n, off=off, nn=nn: e.tensor_tensor(
                        act[:, 8 + n, off:off + nn], a2[:, 0:nn], t2[:, 0:nn], ALU.mult),
                        reads=[a2k, t2k], writes=[ak(8 + n, tbi)])
                else:
                    g = n - 4
                    a3, a3k = tmp()
                    P.op("dve", lambda e, a2=a2, t2=t2, a3=a3, nn=nn: e.tensor_tensor(
                        a3[:, 0:nn], a2[:, 0:nn], t2[:, 0:nn], ALU.mult), reads=[a2k, t2k], writes=[a3k])
                    if isp:
                        P.op("act", lambda e, a3=a3, g=g, off=off, nn=nn: e.copy(
                            kdup[:, g, 128 + off:128 + off + nn], a3[:, 0:nn]),
                            reads=[a3k], writes=kd_keys_own(ps_, g, tbi))
                        if ps_["half"] == 1 and tbi == 1:
                            P.op("act", lambda e, a3=a3, g=g: e.copy(krot32[:, g, 0:128], a3[:, 384:512]),
                                 reads=[a3k], writes=[("krot32", g)])
                    else:
                        P.op("act", lambda e, a3=a3, g=g, nn=nn: e.copy(
                            kdup[:, g, 0:S * 192].rearrange("p (s c) -> p s c", c=192)[:, :, 128:192],
                            a3[:, 0:nn].rearrange("p (s c) -> p s c", c=64)),
                            reads=[a3k], writes=kd_keys_own(ps_, g, tbi))
                        P.op("act", lambda e, a3=a3, g=g, nn=nn: e.copy(krot32[:, g, 0:nn], a3[:, 0:nn]),
                             reads=[a3k], writes=[("krot32", g)])

            for i in range(len(items) + 1):
                if hook_mid is not None and i == 2:
                    hook_mid()
                if i < len(items):
                    qk_S1(i)
                if i >= 1:
                    qk_S2(i - 1)

            ckpt('m_qk_done')
            ckpt('m_qk')
            sw, swk = WS.get(("w_in", l, 6))
            ktiles = []
            for s, sg in enumerate(segs):
                if isp:
                    for i in range(Ls // 128):
                        ktiles.append((s, i * 128, 128, 1 + i))
                else:
                    ktiles.append((s, sg["t0"], DSEQ, sg["vbase"] + 1))
            for q0 in range(0, len(ktiles), 4):
                grp = ktiles[q0:q0 + 4]
                bv, bvk = bank()
                def mm(e, s=sw, bv=bv, grp=grp):
                    for gi, (_, toff, ksz, _) in enumerate(grp):
                        for k in range(KC):
                            ins = e.matmul(bv[0:ksz, gi * 128:(gi + 1) * 128], hT[:, k, toff:toff + ksz],
                                           wring[:, s, k * 128:(k + 1) * 128], **flags(k, KC))
                    return ins
                tbset = sorted(set(((toff // 512) if isp else 0) for (_, toff, _, _) in grp))
                P.op("pe", mm, reads=[swk] + [hk(c, tbi) for c in range(KC) for tbi in tbset], writes=[bvk])
                for gi, (s, toff, ksz, vidx) in enumerate(grp if not os.environ.get('KNO_VCOPY') else []):
                    vin = bv[0:ksz, gi * 128:(gi + 1) * 128].rearrange("r (g d) -> r g d", g=2)
                    P.op("act", lambda e, vin=vin, ksz=ksz, vidx=vidx: e.copy(
                        vpad[0:ksz, vidx, :].rearrange("r (g c) -> r g c", g=2)[:, :, 0:64], vin),
                        reads=[bvk], writes=[("vpad", vidx)])
                    P.op("act", lambda e, vin=vin, ksz=ksz, vidx=vidx: e.copy(
                        vpad[0:ksz, vidx, :].rearrange("r (g c) -> r g c", g=2)[:, :, 192:256], vin),
                        reads=[bvk, ("vpad", vidx)], writes=[("vpad", vidx)])
                    sg = segs[s]
                    last = (toff + ksz == sg["t0"] + Ls) and (not isp or ps_["half"] == 1) and not os.environ.get('KNO_VOUT')
                    if last:
                        P.op("dve", lambda e, bv=bv, gi=gi, ksz=ksz: e.tensor_copy(
                            ostage[0:ksz, 1, :], bv[0:ksz, gi * 128:(gi + 1) * 128]), reads=[bvk], writes=["ost1"])
                        if isp:
                            dst = nvp[l, sg["b"], :, :]
                        else:
                            dst = nvs[l, sg["b"], 64:128, :]
                        P.op("pool", lambda e, dst=dst, ksz=ksz: e.dma_start(out=dst, in_=ostage[0:ksz, 1, :]),
                             reads=["ost1"], writes=[("out", "nv", l, sg["b"], isp)], dma=True)
            WS.release()

            ckpt('m_v')
            for g in range(4):
                sw, swk = WS.get(("w_in", l, 7 + g))
                for tbi, (off, nn) in enumerate(ps_["tbs"]):
                    bz, bzk = bank()
                    def mm(e, s=sw, bk=bz, off=off, nn=nn):
                        for k in range(KC):
                            ins = e.matmul(bk[:, 0:nn], wring[:, s, k * 128:(k + 1) * 128], hT[:, k, off:off + nn],
                                           **flags(k, KC))
                        return ins
                    P.op("pe", mm, reads=[swk] + [hk(c, tbi) for c in range(KC)], writes=[bzk])
                    if isp:
                        P.op("dve", lambda e, bz=bz, g=g, off=off, nn=nn: e.tensor_copy(
                            uext[:, g, 15 + off:15 + off + nn], bz[:, 0:nn]),
                            reads=[bzk], writes=[("uext", g)])
                    else:
                        P.op("act", lambda e, bz=bz, g=g, nn=nn: e.copy(
                            uext[:, g, 0:S * 79].rearrange("p (s c) -> p s c", c=79)[:, :, 15:79],
                            bz[:, 0:nn].rearrange("p (s c) -> p s c", c=64)),
                            reads=[bzk], writes=[("uext", g)])
                WS.release()

            U3 = [uext[:, g, 0:S * UW].rearrange("p (s c) -> p s c", c=UW) for g in range(4)]
            A3 = tmpA[:, 0:S * UW].rearrange("p (s c) -> p s c", c=UW)
            B3 = tmpB[:, 0:S * UW].rearrange("p (s c) -> p s c", c=UW)
            def pool_group(g, w):
                cur, curk = U3[g], ("uext", g)
                width = 1
                bufs = [(A3, "tmpA"), (B3, "tmpB")]
                bi = 0
                while width < w:
                    nxt, nxtk = bufs[bi]
                    bi ^= 1
                    lo = 2 * width - 1
                    P.op("dve", lambda e, nxt=nxt, cur=cur, lo=lo, width=width: e.tensor_tensor(
                        nxt[:, :, lo:UW], cur[:, :, lo:UW], cur[:, :, lo - width:UW - width], ALU.add),
                        reads=[curk], writes=[nxtk])
                    cur, curk = nxt, nxtk
                    width *= 2
                dslot = g
                dview = dT[:, dslot, 0:T_].rearrange("p (s c) -> p s c", c=Ls)
                if isp and ps_["half"] == 0:
                    t1, t1k = tmp()
                    P.op("dve", lambda e, t1=t1, cur=cur, g=g: e.tensor_tensor(
                        t1[:, 0:15], cur[:, 0, 15:30], invcnt[:, g * 15:(g + 1) * 15], ALU.mult),
                        reads=[curk, "c32"], writes=[t1k])
                P.op("act", lambda e, cur=cur, w=w: e.activation(
                    cur[:, :, 15:UW], cur[:, :, 15:UW], AF.Copy, scale=1.0 / w), reads=[curk], writes=[curk])
                P.op("dve", lambda e, dview=dview, cur=cur, g=g: e.tensor_tensor(
                    dview, cur[:, :, 15:UW], U3[g][:, :, 15:UW], ALU.subtract),
                    reads=[curk, ("uext", g)], writes=[("dT", dslot)])
                if isp and ps_["half"] == 0:
                    P.op("dve", lambda e, t1=t1, g=g, dslot=dslot: e.tensor_tensor(
                        dT[:, dslot, 0:15], t1[:, 0:15], uext[:, g, 15:30], ALU.subtract),
                        reads=[t1k, ("uext", g), ("dT", dslot)], writes=[("dT", dslot)])

            pool_pending = [(g, w) for g, w in enumerate((2, 4, 8, 16))]
            ckpt('m_u')
            qtiles = []
            for s, sg in enumerate(segs):
                if isp:
                    for m in range(Ls // 128):
                        prev = None
                        if m > 0:
                            prev = (128 + (m - 1) * 128, m, 128, m, "P")
                        elif sg["hist"] == "carry":
                            prev = (0, 0, 128, 0, "P")
                        own = (128 + m * 128, 1 + m, 128, 1 + m, "O")
                        qtiles.append(dict(qoff=m * 128, qsz=128, prev=prev, own=own, mask=True))
                else:
                    kb = sg["kbase"]
                    qtiles.append(dict(qoff=sg["t0"], qsz=DSEQ, prev=(kb, sg["vbase"], 128, 2 * s, "P"),
                                       own=(kb + 128, sg["vbase"] + 1, DSEQ, 2 * s + 1, "O"), mask=False))
            units = [(qi, g) for qi in range(len(qtiles)) for g in range(2)]
            ustate = {}

            def att_A(ui):
                qi, g = units[ui]
                qt = qtiles[qi]
                qoff, qsz = qt["qoff"], qt["qsz"]
                tbq = (qoff // 512) if isp else 0
                kts = [k_ for k_ in (qt["prev"], qt["own"]) if k_ is not None]
                pbi = ui % 3
                pb = pbufs[pbi]
                pbk = ("pbuf", pbi)
                for ki, (kcol, vidx, ksz, kidx, kind) in enumerate(kts):
                    bi, bSk = bank_pair()
                    def mm(e, bi=bi, g=g, kcol=kcol, ksz=ksz, qoff=qoff, qsz=qsz):
                        for hh in range(4):
                            head = g * 4 + hh
                            j = head // 2
                            par = head % 2
                            p0 = 64 * par
                            c0 = (bi + par) * 512 + (hh // 2) * qsz
                            ins = e.matmul(psum_all[0:ksz, c0:c0 + qsz],
                                           kdup[p0:p0 + 64, g, kcol:kcol + ksz],
                                           act[p0:p0 + 64, 8 + j, qoff:qoff + qsz], start=True, stop=True)
                        return ins
                    P.op("pe", mm, reads=[("kdup", g, kidx), ak(8 + 2 * g, tbq), ak(9 + 2 * g, tbq)], writes=bSk)
                    src3 = psum_all[0:ksz, bi * 512:(bi + 2) * 512].rearrange("p (a c) -> p a c", a=2)[:, :, 0:2 * qsz]
                    dst3 = pb[0:ksz, ki, 0:4 * qsz].rearrange("p (a c) -> p a c", a=2)
                    if not qt["mask"]:
                        P.op("act", lambda e, src3=src3, dst3=dst3: e.activation(dst3, src3, AF.Exp, scale=0.125),
                             reads=bSk, writes=[pbk])
                    else:
                        for qh in range(2):
                            if kind == "P":
                                bcol = 316 if qh == 0 else 317
                            else:
                                bcol = 318 if qh == 0 else 316
                            s4 = src3.rearrange("p a (h q) -> p a h q", h=2)[:, :, :, qh * 64:(qh + 1) * 64]
                            d4 = dst3.rearrange("p a (h q) -> p a h q", h=2)[:, :, :, qh * 64:(qh + 1) * 64]
                            P.op("act", lambda e, s4=s4, d4=d4, bcol=bcol: e.activation(
                                d4, s4, AF.Exp, scale=0.125, bias=c32[:, bcol:bcol + 1]),
                                reads=bSk + ["c32"] + ([pbk] if qh else []), writes=[pbk])
                ustate[ui] = (pb, pbk, kts)

            def att_B(ui):
                qi, g = units[ui]
                qt = qtiles[qi]
                qoff, qsz = qt["qoff"], qt["qsz"]
                tbq = (qoff // 512) if isp else 0
                pb, pbk, kts = ustate.pop(ui)
                acc = 4 + 2 * (qi % 2)
                bN, bNk = banks[acc], ("bank", acc)
                bD, bDk = banks[acc + 1], ("bank", acc + 1)
                for jj in range(2):
                    j = g * 2 + jj
                    def mmN(e, bN=bN, pb=pb, g=g, jj=jj, j=j, kts=kts, qsz=qsz):
                        nmm = 2 * len(kts)
                        i = 0
                        for ki, (kcol, vidx, ksz, kidx, kind) in enumerate(kts):
                            for ab in range(2):
                                hh = 2 * jj + ab
                                pc = (hh % 2) * 2 * qsz + (hh // 2) * qsz
                                ins = e.matmul(bN[:, j * qsz:(j + 1) * qsz],
                                               vpad[0:ksz, vidx, g * 256 + ab * 128:g * 256 + (ab + 1) * 128],
                                               pb[0:ksz, ki, pc:pc + qsz], **flags(i, nmm))
                                i += 1
                        return ins
                    P.op("pe", mmN, reads=[pbk] + [("vpad", k_[1]) for k_ in kts], writes=[bNk])
                    def mmD(e, bD=bD, pb=pb, jj=jj, j=j, kts=kts, qsz=qsz):
                        nmm = 2 * len(kts)
                        i = 0
                        for ki, (kcol, vidx, ksz, kidx, kind) in enumerate(kts):
                            for ab in range(2):
                                hh = 2 * jj + ab
                                pc = (hh % 2) * 2 * qsz + (hh // 2) * qsz
                                ins = e.matmul(bD[:, j * qsz:(j + 1) * qsz],
                                               (onesA if ab == 0 else onesB)[0:ksz, :],
                                               pb[0:ksz, ki, pc:pc + qsz], **flags(i, nmm))
                                i += 1
                        return ins
                    P.op("pe", mmD, reads=[pbk, "cb"], writes=[bDk])
                if g == 1:
                    t1, t1k = tmp()
                    nq = 4 * qsz
                    P.op("dve", lambda e, t1=t1, bD=bD, qsz=qsz, nq=nq: e.tensor_tensor(
                        t1[:, 0:nq].rearrange("p (j q) -> p j q", j=4), bD[:, 0:nq].rearrange("p (j q) -> p j q", j=4),
                        esf[:, l, :, 0:qsz], ALU.add), reads=[bDk, "esf"], writes=[t1k])
                    t3, t3k = tmp()
                    P.op("act", lambda e, t1=t1, t3=t3, nq=nq: e.activation(t3[:, 0:nq], t1[:, 0:nq], AF.Ln),
                         reads=[t1k], writes=[t3k])
                    t2, t2k = tmp()
                    P.op("act", lambda e, t3=t3, t2=t2, nq=nq: e.activation(t2[:, 0:nq], t3[:, 0:nq], AF.Exp, scale=-1.0),
                         reads=[t3k], writes=[t2k])
                    P.op("dve", lambda e, t2=t2, bN=bN, qoff=qoff, qsz=qsz, nq=nq: e.tensor_tensor(
                        act[:, 0:4, qoff:qoff + qsz], bN[:, 0:nq].rearrange("p (j q) -> p j q", j=4),
                        t2[:, 0:nq].rearrange("p (j q) -> p j q", j=4), ALU.mult),
                        reads=[bNk, t2k], writes=[ak(c, tbq) for c in range(4)])

            SK = 2
            state["nrot"] = 4
            for i in range(len(units) + SK):
                if i < len(units):
                    att_A(i)
                if i >= SK:
                    att_B(i - SK)
                if i >= 1 and i % 3 == 1 and pool_pending:
                    pool_group(*pool_pending.pop(0))
            while pool_pending:
                pool_group(*pool_pending.pop(0))
            state["nrot"] = NROT_OUT

            ckpt('m_attn')
            if isp and ps_["half"] == 0:
                P.op("pool", lambda e: e.tensor_copy(kcarry[:, l, :, :], kdup[:, :, 128 + 896:128 + 1024]),
                     reads=[("kdup", 0, 8), ("kdup", 1, 8)], writes=[("kcarry", l)])
                P.op("pool", lambda e: e.tensor_copy(vcarry[:, l, :], vpad[:, 8, :]),
                     reads=[("vpad", 8)], writes=[("vcarry", l)])
            else:
                for s, sg in enumerate(segs):
                    nrow = 128 if isp else DSEQ
                    c0 = 0 if isp else sg["t0"]
                    bk, bkk = bank()
                    def mm(e, bk=bk, c0=c0, nrow=nrow):
                        for g in range(2):
                            ins = e.transpose(bk[0:nrow, g * 128:(g + 1) * 128], krot32[:, g, c0:c0 + nrow], ident)
                        return ins
                    P.op("pe", mm, reads=[("krot32", 0), ("krot32", 1), "c32"], writes=[bkk])
                    P.op("dve", lambda e, bk=bk, nrow=nrow: e.tensor_copy(
                        ostage[0:nrow, 0, :].rearrange("r (g d) -> r g d", g=2),
                        bk[0:nrow, 0:256].rearrange("r (g c) -> r g c", g=2)[:, :, 0:64]),
                        reads=[bkk], writes=["ost0"])
                    dst = nkp[l, sg["b"], :, :] if isp else nks[l, sg["b"], 64:128, :]
                    P.op("pool", lambda e, dst=dst, nrow=nrow: e.dma_start(out=dst, in_=ostage[0:nrow, 0, :]),
                         reads=["ost0"], writes=[("out", "nk", l, sg["b"], isp)], dma=True)

            ckpt('m_kout')
            swp, swpk = WS.get(("w_pool", l))
            for g in range(4):
                dslot = g
                for tbi, (off, nn) in enumerate(ps_["tbs"]):
                    bz, bzk = bank()
                    P.op("pe", lambda e, bz=bz, g=g, dslot=dslot, off=off, nn=nn, s=swp: e.matmul(
                        bz[:, 0:nn], wring[:, s, g * 128:(g + 1) * 128], dT[:, dslot, off:off + nn], start=True, stop=True),
                        reads=[swpk, ("dT", dslot)], writes=[bzk])
                    P.op("dve", lambda e, bz=bz, g=g, off=off, nn=nn: e.tensor_scalar(
                        act[:, 4 + g, off:off + nn], bz[:, 0:nn], pscale(l, g), None, ALU.mult),
                        reads=[bzk, "small"], writes=[ak(4 + g, tbi)])
            WS.release()
            if isp and ps_["half"] == 0:
                P.op("pool", lambda e: e.tensor_copy(ucarry[:, l, :, :], uext[:, :, TP:TP + 15]),
                     reads=[("uext", g) for g in range(4)], writes=[("ucarry", l)])
            else:
                for s, sg in enumerate(segs):
                    ub = sg["ubase"]
                    bk, bkk = bank()
                    def mm(e, bk=bk, ub=ub):
                        for g in range(4):
                            ins = e.transpose(bk[0:15, g * 128:(g + 1) * 128], uext[:, g, ub + Ls:ub + Ls + 15], ident)
                        return ins
                    P.op("pe", mm, reads=[("uext", g) for g in range(4)] + ["c32"], writes=[bkk])
                    P.op("dve", lambda e, bk=bk: e.tensor_copy(pstage[0:15, 1, :], bk[0:15, :]), reads=[bkk], writes=["pst1"])
                    dst = npp[l, sg["b"]] if isp else nps[l, sg["b"]]
                    P.op("pool", lambda e, dst=dst: e.dma_start(out=dst, in_=pstage[0:15, 1, :]), reads=["pst1"],
                         writes=[("out", "np", l, sg["b"], isp)], dma=True)

            ckpt('m_pool')
            def wo_block(d, tbi, sw, swk):
                off, nn = ps_["tbs"][tbi]
                bz, bzk = bank()
                def mm(e, s=sw, bk=bz, off=off, nn=nn):
                    for k in range(KC):
                        ins = e.matmul(bk[:, 0:nn], wring[:, s, k * 128:(k + 1) * 128], act[:, k, off:off + nn],
                                       **flags(k, KC))
                    return ins
                P.op("pe", mm, reads=[swk] + [ak(c, tbi) for c in range(KC)], writes=[bzk])
                P.op("dve", lambda e, bz=bz, d=d, off=off, nn=nn: e.tensor_tensor(
                    xT[:, d, off:off + nn], bz[:, 0:nn], xT[:, d, off:off + nn], ALU.add),
                    reads=[bzk, xk(d, tbi)], writes=[xk(d, tbi)])

            if not defer_wout:
                for d in range(KC):
                    sl_ = WS.get(("w_out", l, d))
                    for tbi in range(len(ps_["tbs"])):
                        wo_block(d, tbi, *sl_)
                    WS.release()
                return None
            wslots = [WS.get(("w_out", l, d)) for d in range(KC)]
            tail0 = [(lambda d=d: wo_block(d, 0, *wslots[d])) for d in range(KC)]
            def wt1(d):
                wo_block(d, 1, *wslots[d])
                WS.release()
            tail1 = [(lambda d=d: wt1(d)) for d in range(KC)]
            return tail0, tail1

        def pe_gate(l, ps_):
            for d in range(KC):
                sg_, sgk = WS.get(("pe_gate", l, d))
                su_, suk = WS.get(("pe_up", l, d))
                for tbi, (off, nn) in enumerate(ps_["tbs"]):
                    bg, bgk = bank()
                    bu, buk = bank()
                    def mm(e, s=sg_, bk=bg, off=off, nn=nn):
                        for k in range(KC):
                            ins = e.matmul(bk[:, 0:nn], wring[:, s, k * 128:(k + 1) * 128], hT[:, k, off:off + nn],
                                           **flags(k, KC))
                        return ins
                    P.op("pe", mm, reads=[sgk] + [hk(c, tbi) for c in range(KC)], writes=[bgk])
                    def mm2(e, s=su_, bk=bu, off=off, nn=nn):
                        for k in range(2):
                            ins = e.matmul(bk[:, 0:nn], wring[:, s, k * 128:(k + 1) * 128], peT[:, k, off:off + nn],
                                           **flags(k, 2))
                        return ins
                    tts = list(range(off // 128, (off + nn + 127) // 128))
                    P.op("pe", mm2, reads=[suk] + [("peT", tt) for tt in tts], writes=[buk])
                    t1, t1k = tmp()
                    P.op("act", lambda e, t1=t1, bg=bg, nn=nn: e.activation(t1[:, 0:nn], bg[:, 0:nn], AF.Tanh, scale=0.5),
                         reads=[bgk], writes=[t1k])
                    t2, t2k = tmp()
                    P.op("dve", lambda e, t1=t1, t2=t2, bu=bu, nn=nn: e.scalar_tensor_tensor(
                        t2[:, 0:nn], t1[:, 0:nn], 1.0, bu[:, 0:nn], ALU.add, ALU.mult), reads=[t1k, buk], writes=[t2k])
                    P.op("dve", lambda e, t2=t2, d=d, off=off, nn=nn: e.scalar_tensor_tensor(
                        xT[:, d, off:off + nn], t2[:, 0:nn], 0.5, xT[:, d, off:off + nn], ALU.mult, ALU.add),
                        reads=[t2k, xk(d, tbi)], writes=[xk(d, tbi)])
                WS.release()
                WS.release()

        def load_rope(ps_):
            T_ = ps_["T"]
            if ps_["kind"] == "p":
                c0 = ps_["half"] * TP
                P.op("pool", lambda e, c0=c0: e.dma_start(out=ropeC[:, 0:TP], in_=rope_p_d[0, :, c0:c0 + TP]),
                     writes=["ropeC"], dma=True)
                P.op("pool", lambda e, c0=c0: e.dma_start(out=ropeS[:, 0:TP], in_=rope_p_d[1, :, c0:c0 + TP]),
                     writes=["ropeS"], dma=True)
            else:
                P.op("pool", lambda e, T_=T_: e.dma_start(out=ropeC[:, 0:T_], in_=rope_s_d[0, :, 0:T_]),
                     writes=["ropeC"], dma=True)
                P.op("pool", lambda e, T_=T_: e.dma_start(out=ropeS[:, 0:T_], in_=rope_s_d[1, :, 0:T_]),
                     writes=["ropeS"], dma=True)

        pipelined_entry = False
        for pidx, ps_ in enumerate(passes if DBG_STOP not in ("consts", "prologue") else []):
            nxt = passes[pidx + 1] if pidx + 1 < len(passes) else None
            if nxt is not None and (nxt["kind"] != "p" or DBG_STOP or os.environ.get("KNOPIPE") or os.environ.get("KNOXPIPE")):
                nxt = None
            T_ = ps_["T"]
            if not pipelined_entry:
                load_rope(ps_)
            try:
                if ps_["kind"] == "s" or DBG_STOP or os.environ.get("KNOPIPE"):
                    load_x(ps_)
                    ckpt("load_x")
                    for l in range(L):
                        load_pe(l, ps_)
                        ckpt("load_pe")
                        norm(l, 0, ps_)
                        ckpt("norm0")
                        ffn(l, "a", ps_)
                        ckpt("ffna")
                        norm(l, 1, ps_)
                        mixer(l, ps_)
                        ckpt("mixer")
                        norm(l, 2, ps_)
                        ffn(l, "b", ps_)
                        norm(l, 3, ps_)
                        pe_gate(l, ps_)
                        ckpt("layer")
                    store_y(ps_)
                else:
                    phases = []
                    for l in range(L):
                        phases += [("ffn", l, "a", 0), ("mix", l, None, 1), ("ffn", l, "b", 2), ("gate", l, None, 3)]
                    if not pipelined_entry:
                        load_x(ps_)
                        norm(0, 0, ps_)
                    for pi, (kind, l, which, n) in enumerate(phases):
                        if pi > 0 or pipelined_entry:
                            hook = (lambda l=l, n=n: norm_rest(l, n, ps_, 1))
                        else:
                            hook = (lambda: None)
                        if kind == "ffn":
                            ith = None
                            if which == "b":
                                def ith(it, l=l):
                                    if it < 8:
                                        pe_dma(l, ps_, it)
                                    if 1 <= it <= 8:
                                        pe_tr(l, ps_, it - 1)
                            t0, t1 = ffn_pipe(l, which, ps_, hook, ith)
                        elif kind == "mix":
                            t0, t1 = mixer(l, ps_, hook_mid=hook, defer_wout=True)
                        else:
                            t0, t1 = gate_pipe(l, ps_, hook)
                        if pi + 1 == len(phases) and nxt is not None:
                            load_rope(nxt)
                            load_dma(nxt, 0)
                            load_dma(nxt, 1)
                        for b_ in t0:
                            b_()
                        if pi + 1 < len(phases):
                            _, l2, _, n2 = phases[pi + 1]
                            norm_sq(l2, n2, ps_, 0)
                            for b_ in t1[:2]:
                                b_()
                            norm_rest(l2, n2, ps_, 0)
                            for b_ in t1[2:]:
                                b_()
                            norm_sq(l2, n2, ps_, 1)
                        elif nxt is None:
                            for b_ in t1:
                                b_()
                        else:
                            for tt in range(4):
                                store_tile(ps_, tt)
                                load_tr(nxt, tt)
                                load_dma(nxt, tt + 2)
                            norm_sq(0, 0, nxt, 0)
                            for b_ in t1[:2]:
                                b_()
                            norm_rest(0, 0, nxt, 0)
                            for b_ in t1[2:]:
                                b_()
                            for tt in range(4, 8):
                                store_tile(ps_, tt)
                                load_tr(nxt, tt)
                                if tt + 2 < 8:
                                    load_dma(nxt, tt + 2)
                            norm_sq(0, 0, nxt, 1)
                    if nxt is None:
                        store_y(ps_)
                        pipelined_entry = False
                    else:
                        pipelined_entry = True
            except _Stop:
                break
        if not DBG_STOP:
            assert WS.pos == len(WS.seq), (WS.pos, len(WS.seq))
        P.finish("pool")
        P.build(st)
        stats = P.stats
    return nc, stats


def _small_table(norm_ffa, norm_mix, norm_ffb, norm_pe, q_norm, k_norm, sinks, pool_scale):
    sm = np.zeros((128, 84), np.float32)
    norms = [norm_ffa, norm_mix, norm_ffb, norm_pe]
    p = np.arange(128)
    for l in range(L):
        for n in range(4):
            sm[:, (l * 4 + n) * 8:(l * 4 + n) * 8 + 8] = np.asarray(norms[n][l], np.float32).reshape(8, 128).T
        sm[:, 64 + l * 2 + 0] = np.asarray(q_norm[l], np.float32)[p % 64]
        sm[:, 64 + l * 2 + 1] = np.asarray(k_norm[l], np.float32)[p % 64]
        for j in range(4):
            sm[:, 68 + l * 4 + j] = np.asarray(sinks[l], np.float32)[2 * j + p // 64]
        sm[:, 76 + l * 4:76 + l * 4 + 4] = np.asarray(pool_scale[l], np.float32).reshape(4, 128).T
    return sm


_CACHE = {}


def _get_prog(nb_p, nb_s):
    key = (nb_p, nb_s)
    if key not in _CACHE:
        _CACHE[key] = build_program(nb_p, nb_s)
    return _CACHE[key]


def make_in_maps(inputs, ncores, nb_p, nb_s):
    f32 = lambda a: np.ascontiguousarray(np.asarray(a, dtype=np.float32))
    c32, cb, rope_p, rope_s = _consts()
    sm = _small_table(inputs["norm_ffa"], inputs["norm_mix"], inputs["norm_ffb"], inputs["norm_pe"],
                      inputs["q_norm"], inputs["k_norm"], inputs["sinks"], inputs["pool_scale"])
    shared = dict(
        w_ffa_in=f32(inputs["w_ffa_in"]), w_ffa_out=f32(inputs["w_ffa_out"]),
        w_ffb_in=f32(inputs["w_ffb_in"]), w_ffb_out=f32(inputs["w_ffb_out"]),
        w_in=f32(inputs["w_in"]), w_out=f32(inputs["w_out"]), w_pe_gate=f32(inputs["w_pe_gate"]),
        w_pe_up=f32(inputs["w_pe_up"]), w_pool=f32(inputs["w_pool"]),
        small=sm, c32=c32, cb=cb, rope_p=rope_p, rope_s=rope_s)
    xp = f32(inputs["x_prompt"]); xs = f32(inputs["x_sample"])
    pp = f32(inputs["p_prompt"]); ps = f32(inputs["p_sample"])
    ck = f32(inputs["cache_k"]); cv = f32(inputs["cache_v"]); sp = f32(inputs["state_pool"])
    maps = []
    for i in range(ncores):
        bp = slice(i * nb_p, (i + 1) * nb_p) if nb_p else slice(0, 1)
        bs = slice(i * nb_s, (i + 1) * nb_s) if nb_s else slice(0, 1)
        nbs = max(nb_s, 1)
        m = dict(shared)
        m["xp"] = np.ascontiguousarray(xp[bp])
        m["xs"] = np.ascontiguousarray(xs[bs]).reshape(nbs * DSEQ, D)
        m["pp"] = np.ascontiguousarray(pp[:, bp])
        m["ps"] = np.ascontiguousarray(ps[:, bs]).reshape(L, nbs * DSEQ, PE_DIM)
        m["ck"] = np.ascontiguousarray(ck[:, bs]).reshape(L, nbs, 128, 128)
        m["cv"] = np.ascontiguousarray(cv[:, bs]).reshape(L, nbs, 128, 128)
        m["spool"] = np.ascontiguousarray(sp[:, bs])
        maps.append(m)
    return maps


def gather(results, nb_p, nb_s):
    nbp = max(nb_p, 1)
    nbs = max(nb_s, 1)
    yp = np.concatenate([r["yp"] for r in results], axis=0)
    ys = np.concatenate([r["ys"].reshape(nbs, DSEQ, D) for r in results], axis=0)
    nkp = np.concatenate([r["nkp"].reshape(L, nbp, 128, 2, 64) for r in results], axis=1)
    nvp = np.concatenate([r["nvp"].reshape(L, nbp, 128, 2, 64) for r in results], axis=1)
    npp = np.concatenate([r["npp"] for r in results], axis=1)
    nks = np.concatenate([r["nks"].reshape(L, nbs, 128, 2, 64) for r in results], axis=1)
    nvs = np.concatenate([r["nvs"].reshape(L, nbs, 128, 2, 64) for r in results], axis=1)
    nps = np.concatenate([r["nps"] for r in results], axis=1)
    return tuple(np.ascontiguousarray(a, dtype=np.float32) for a in (yp, ys, nkp, nvp, npp, nks, nvs, nps))


def kernel(**inputs):
    nb_p = inputs["x_prompt"].shape[0] // NCORES
    nb_s = inputs["x_sample"].shape[0] // NCORES
    nc, _ = _get_prog(nb_p, nb_s)
    maps = make_in_maps(inputs, NCORES, nb_p, nb_s)
    res = run_bass_kernel_spmd(nc, maps, core_ids=list(range(NCORES)))
    return gather(res.results, nb_p, nb_s)
```
